# Optimizing a Trainium2 kernel written in Bass

```python
import jax, jax.numpy as jnp
from jax import lax
import numpy as np

D_MODEL = 1024
BATCH = 8
SEQ = 2048
DEPTH = 2

HEAD_DIM = 64
SWA_Q_HEADS = 8
SWA_KV_HEADS = 2
SWA_WINDOW = 128
SB_HEADS = 8
SB_Q_BLOCK = 128
MOBA_HEADS = 8
MOBA_BLOCK = 256
MOBA_TOPK = 3
MOBA_Q_CHUNK = 16
BRANCH_WIDTH = 512
N_BRANCHES = 3
D_FF = 2816
CONV_WIDTH = 3
ROPE_THETA = 10000.0
RMS_EPS = 1e-6

IN_SIZES = (
    SWA_Q_HEADS * HEAD_DIM, SWA_KV_HEADS * HEAD_DIM, SWA_KV_HEADS * HEAD_DIM,
    SB_HEADS * HEAD_DIM, SB_HEADS * HEAD_DIM, SB_HEADS * HEAD_DIM,
    MOBA_HEADS * HEAD_DIM, MOBA_HEADS * HEAD_DIM, MOBA_HEADS * HEAD_DIM,
    N_BRANCHES * D_MODEL,
)
D_IN = sum(IN_SIZES)

kernel_name = "hybrid_gated_swa_stickbreak_moba_block"


def rmsnorm(x, g):
    xf = x.astype(jnp.float32)
    y = xf * lax.rsqrt(jnp.mean(xf * xf, axis=-1, keepdims=True) + RMS_EPS)
    return (y * g.astype(jnp.float32)).astype(x.dtype)


def rope_tables(seq):
    inv = ROPE_THETA ** (-jnp.arange(0, HEAD_DIM, 2, dtype=jnp.float32) / HEAD_DIM)
    ang = jnp.arange(seq, dtype=jnp.float32)[:, None] * inv[None, :]
    return jnp.cos(ang), jnp.sin(ang)


def apply_rope(x, cos, sin):
    x1, x2 = jnp.split(x.astype(jnp.float32), 2, axis=-1)
    c = cos[None, :, None, :]
    s = sin[None, :, None, :]
    return jnp.concatenate([x1 * c - x2 * s, x2 * c + x1 * s], axis=-1).astype(x.dtype)


def swa_sink_attention(q, k, v, sinks):
    B, S = q.shape[0], q.shape[1]
    W = SWA_WINDOW
    nblk = S // W
    G = SWA_Q_HEADS // SWA_KV_HEADS
    scale = HEAD_DIM ** -0.5
    qb = q.reshape(B, nblk, W, SWA_KV_HEADS, G, HEAD_DIM)
    kb = k.reshape(B, nblk, W, SWA_KV_HEADS, HEAD_DIM)
    vb = v.reshape(B, nblk, W, SWA_KV_HEADS, HEAD_DIM)
    pad = ((0, 0), (1, 0), (0, 0), (0, 0), (0, 0))
    kk = jnp.concatenate([jnp.pad(kb, pad)[:, :-1], kb], axis=2)
    vv = jnp.concatenate([jnp.pad(vb, pad)[:, :-1], vb], axis=2)
    scores = jnp.einsum('bnqhgd,bnkhd->bnhgqk', qb, kk).astype(jnp.float32) * scale
    qpos = jnp.arange(W)[:, None] + W
    kpos = jnp.arange(2 * W)[None, :]
    diff = qpos - kpos
    band = (diff >= 0) & (diff < W)
    blk = jnp.arange(nblk)[:, None, None]
    valid = band[None] & ((blk * W + kpos[None] - W) >= 0)
    scores = jnp.where(valid[None, :, None, None], scores, -jnp.inf)
    sink = sinks.astype(jnp.float32).reshape(SWA_KV_HEADS, G)[None, None, :, :, None, None]
    sink = jnp.broadcast_to(sink, scores.shape[:-1] + (1,))
    probs = jax.nn.softmax(jnp.concatenate([scores, sink], axis=-1), axis=-1)[..., :-1]
    out = jnp.einsum('bnhgqk,bnkhd->bnqhgd', probs.astype(v.dtype), vv)
    return out.reshape(B, S, SWA_Q_HEADS * HEAD_DIM)


def stick_breaking_attention(q, k, v):
    B, S, H, dh = q.shape
    nblk = S // SB_Q_BLOCK
    scale = dh ** -0.5
    q_blocks = q.reshape(B, nblk, SB_Q_BLOCK, H, dh).transpose(1, 0, 2, 3, 4)
    kpos = jnp.arange(S)

    def block(args):
        q_blk, i = args
        z = jnp.einsum('bqhd,bkhd->bhqk', q_blk, k).astype(jnp.float32) * scale
        qpos = i * SB_Q_BLOCK + jnp.arange(SB_Q_BLOCK)
        past = kpos[None, :] < qpos[:, None]
        log_beta = jax.nn.log_sigmoid(z)
        log_one_minus = jnp.where(past, jax.nn.log_sigmoid(-z), 0.0)
        remain = lax.cumsum(log_one_minus, axis=3, reverse=True) - log_one_minus
        w = jnp.where(past, jnp.exp(log_beta + remain), 0.0)
        return jnp.einsum('bhqk,bkhd->bqhd', w.astype(v.dtype), v)

    out = lax.map(block, (q_blocks, jnp.arange(nblk)))
    return out.transpose(1, 0, 2, 3, 4).reshape(B, S, H * dh)


def moba_attention(q, k, v):
    B, S, H, dh = q.shape
    nb = -(-S // MOBA_BLOCK)
    pad = nb * MOBA_BLOCK - S
    ksel = min(MOBA_TOPK, nb)
    scale = dh ** -0.5
    kp = jnp.pad(k, ((0, 0), (0, pad), (0, 0), (0, 0)))
    vp = jnp.pad(v, ((0, 0), (0, pad), (0, 0), (0, 0)))
    kb = kp.reshape(B, nb, MOBA_BLOCK, H, dh)
    vb = vp.reshape(B, nb, MOBA_BLOCK, H, dh)
    k_mean = jnp.mean(kb.astype(jnp.float32), axis=2)
    gate = jnp.einsum('bshd,bnhd->bhsn', q.astype(jnp.float32), k_mean)
    q_blk = jnp.arange(S) // MOBA_BLOCK
    past_blk = jnp.arange(nb)[None, :] < q_blk[:, None]
    gate = jnp.where(past_blk[None, None], gate, -jnp.inf)
    _, sel = lax.top_k(gate, ksel)
    sel_valid = sel < q_blk[None, None, :, None]

    C = MOBA_Q_CHUNK
    nchunk = S // C
    q_c = q.reshape(B, nchunk, C, H, dh).transpose(1, 0, 2, 3, 4)
    sel_c = sel.reshape(B, H, nchunk, C, ksel).transpose(2, 0, 1, 3, 4)
    valid_c = sel_valid.reshape(B, H, nchunk, C, ksel).transpose(2, 0, 1, 3, 4)
    kb_h = kb.transpose(0, 3, 1, 2, 4)
    vb_h = vb.transpose(0, 3, 1, 2, 4)
    gather = jax.vmap(jax.vmap(lambda blocks, idx: blocks[idx]))

    def chunk(args):
        q_blk, sel_blk, valid_blk, i = args
        t0 = i * C
        qpos = t0 + jnp.arange(C)
        own_start = (t0 // MOBA_BLOCK) * MOBA_BLOCK
        k_own = lax.dynamic_slice_in_dim(kp, own_start, MOBA_BLOCK, axis=1)
        v_own = lax.dynamic_slice_in_dim(vp, own_start, MOBA_BLOCK, axis=1)
        k_sel = gather(kb_h, sel_blk)
        v_sel = gather(vb_h, sel_blk)
        s_sel = jnp.einsum('bqhd,bhqnkd->bhqnk', q_blk, k_sel).astype(jnp.float32) * scale
        s_sel = jnp.where(valid_blk[..., None], s_sel, -jnp.inf).reshape(B, H, C, ksel * MOBA_BLOCK)
        s_own = jnp.einsum('bqhd,bkhd->bhqk', q_blk, k_own).astype(jnp.float32) * scale
        kpos_own = own_start + jnp.arange(MOBA_BLOCK)
        s_own = jnp.where(kpos_own[None, :] <= qpos[:, None], s_own, -jnp.inf)
        p = jax.nn.softmax(jnp.concatenate([s_sel, s_own], axis=-1), axis=-1)
        p_sel = p[..., :ksel * MOBA_BLOCK].reshape(B, H, C, ksel, MOBA_BLOCK).astype(v.dtype)
        p_own = p[..., ksel * MOBA_BLOCK:].astype(v.dtype)
        return (jnp.einsum('bhqnk,bhqnkd->bqhd', p_sel, v_sel)
                + jnp.einsum('bhqk,bkhd->bqhd', p_own, v_own))

    out = lax.map(chunk, (q_c, sel_c, valid_c, jnp.arange(nchunk)))
    return out.transpose(1, 0, 2, 3, 4).reshape(B, S, H * dh)


def causal_depthwise_conv(u, w, b):
    S = u.shape[1]
    up = jnp.pad(u, ((0, 0), (CONV_WIDTH - 1, 0), (0, 0)))
    out = b
    for tap in range(CONV_WIDTH):
        out = out + w[tap] * up[:, tap:tap + S]
    return out


def hybrid_layer(x, cos, sin, norm_mix, w_in, b_gate, sinks, w_branch, w_out,
                 norm_ffn, w_up, conv_w, conv_b, w_down):
    B, S, _ = x.shape
    h = rmsnorm(x, norm_mix)
    proj = h @ w_in
    points = np.cumsum(IN_SIZES)[:-1].tolist()
    qa, ka, va, qb, kb, vb, qc, kc, vc, gates = jnp.split(proj, points, axis=-1)
    heads = lambda t, n: t.reshape(B, S, n, HEAD_DIM)
    o_a = swa_sink_attention(apply_rope(heads(qa, SWA_Q_HEADS), cos, sin),
                             apply_rope(heads(ka, SWA_KV_HEADS), cos, sin),
                             heads(va, SWA_KV_HEADS), sinks)
    o_b = stick_breaking_attention(heads(qb, SB_HEADS), heads(kb, SB_HEADS), heads(vb, SB_HEADS))
    o_c = moba_attention(apply_rope(heads(qc, MOBA_HEADS), cos, sin),
                         apply_rope(heads(kc, MOBA_HEADS), cos, sin),
                         heads(vc, MOBA_HEADS))
    g = jax.nn.sigmoid(gates.reshape(B, S, N_BRANCHES, D_MODEL) + b_gate.reshape(N_BRANCHES, D_MODEL))
    merged = (g[:, :, 0] * (o_a @ w_branch[0])
              + g[:, :, 1] * (o_b @ w_branch[1])
              + g[:, :, 2] * (o_c @ w_branch[2]))
    x = x + merged @ w_out
    h = rmsnorm(x, norm_ffn)
    u = causal_depthwise_conv(h @ w_up, conv_w, conv_b)
    u_a, u_v = jnp.split(u, 2, axis=-1)
    return x + (jax.nn.silu(u_a) * u_v) @ w_down


def setup_inputs(seed: int = 0) -> dict:
    key = jax.random.key(seed)
    ks = jax.random.split(key, 13)
    f32 = jnp.float32
    nrm = lambda k, shape, s: jax.random.normal(k, shape, f32) * s
    return {
        "x": nrm(ks[0], (BATCH, SEQ, D_MODEL), 1.0),
        "norm_mix": 1.0 + nrm(ks[1], (DEPTH, D_MODEL), 0.02),
        "w_in": nrm(ks[2], (DEPTH, D_MODEL, D_IN), D_MODEL ** -0.5),
        "b_gate": nrm(ks[3], (DEPTH, N_BRANCHES * D_MODEL), 0.1),
        "sinks": nrm(ks[4], (DEPTH, SWA_Q_HEADS), 0.5),
        "w_branch": nrm(ks[5], (DEPTH, N_BRANCHES, BRANCH_WIDTH, D_MODEL), BRANCH_WIDTH ** -0.5),
        "w_out": nrm(ks[6], (DEPTH, D_MODEL, D_MODEL), D_MODEL ** -0.5),
        "norm_ffn": 1.0 + nrm(ks[7], (DEPTH, D_MODEL), 0.02),
        "w_up": nrm(ks[8], (DEPTH, D_MODEL, 2 * D_FF), D_MODEL ** -0.5),
        "conv_w": nrm(ks[9], (DEPTH, CONV_WIDTH, 2 * D_FF), CONV_WIDTH ** -0.5),
        "conv_b": nrm(ks[10], (DEPTH, 2 * D_FF), 0.01),
        "w_down": nrm(ks[11], (DEPTH, D_FF, D_MODEL), D_FF ** -0.5),
        "norm_final": 1.0 + nrm(ks[12], (D_MODEL,), 0.02),
    }


def reference(x, norm_mix, w_in, b_gate, sinks, w_branch, w_out, norm_ffn,
              w_up, conv_w, conv_b, w_down, norm_final):
    cos, sin = rope_tables(x.shape[1])
    for layer in range(DEPTH):
        x = hybrid_layer(x, cos, sin, norm_mix[layer], w_in[layer], b_gate[layer],
                         sinks[layer], w_branch[layer], w_out[layer], norm_ffn[layer],
                         w_up[layer], conv_w[layer], conv_b[layer], w_down[layer])
    return rmsnorm(x, norm_final)
```

```python
from contextlib import ExitStack
import numpy as np
import concourse.bass as bass
import concourse.mybir as mybir
from concourse.bass_utils import run_bass_kernel_spmd

F32 = mybir.dt.float32
BF16 = mybir.dt.bfloat16
AF = mybir.ActivationFunctionType
ALU = mybir.AluOpType
AX = mybir.AxisListType

S = 2048
D = 1024
NL = 2
LV = 224
NV = NL * LV + 8
NEG = -30000.0


class Res:
    __slots__ = ("w", "r")

    def __init__(self):
        self.w = None
        self.r = []


class FW:
    ENG = ("pe", "act", "dve", "pool", "sp")

    def __init__(self, nc, es):
        self.nc = nc
        self.streams = {n: [] for n in self.ENG}
        self.sems = {}
        self.cnt = {}
        self.waited = {n: {} for n in self.ENG}
        self.es = es
        for n in self.ENG:
            self.newsem(n)

    def newsem(self, key):
        self.sems[key] = self.es.enter_context(self.nc.semaphore("s_" + key))
        self.cnt[key] = 0

    def wait(self, eng, tok):
        key, val = tok
        if self.waited[eng].get(key, 0) >= val:
            return
        self.waited[eng][key] = val
        sem = self.sems[key]
        self.streams[eng].append(lambda e, sem=sem, val=val: e.wait_ge(sem, val))

    def _deps(self, eng, reads, writes, extra):
        deps = set()
        for r in reads:
            if r.w is not None:
                deps.add(r.w)
        for w in writes:
            if w.w is not None and w.w[0] != eng:
                deps.add(w.w)
            for t in w.r:
                if t[0] != eng:
                    deps.add(t)
        deps.update(extra)
        for t in sorted(deps):
            if eng == "pe" and t[0] == "pe":
                continue
            self.wait(eng, t)

    def _commit(self, tok, reads, writes):
        for r in reads:
            r.r.append(tok)
        for w in writes:
            w.w = tok
            w.r = []

    def op(self, eng, fns, reads=(), writes=(), extra=()):
        if not isinstance(fns, (list, tuple)):
            fns = [fns]
        self._deps(eng, reads, writes, extra)
        self.cnt[eng] += 1
        tok = (eng, self.cnt[eng])
        sem = self.sems[eng]
        st = self.streams[eng]
        for f in fns[:-1]:
            st.append(f)
        last = fns[-1]
        st.append(lambda e, last=last, sem=sem: last(e).then_inc(sem, 1))
        self._commit(tok, reads, writes)
        return tok

    def dma(self, eng, key, out, in_, reads=(), writes=(), extra=()):
        if key not in self.sems:
            self.newsem(key)
        self._deps(eng, reads, writes, extra)
        self.cnt[key] += 16
        tok = (key, self.cnt[key])
        sem = self.sems[key]
        self.streams[eng].append(lambda e, out=out, in_=in_, sem=sem: e.dma_start(out=out, in_=in_).then_inc(sem, 16))
        self._commit(tok, reads, writes)
        return tok

    def barrier(self):
        for e in self.ENG:
            for k, v in self.cnt.items():
                if k != e and v > 0:
                    self.wait(e, (k, v))

    def replay(self):
        with self.nc.Block() as block:
            @block.tensor
            def _(e):
                for f in self.streams["pe"]:
                    f(e)

            @block.scalar
            def _(e):
                for f in self.streams["act"]:
                    f(e)

            @block.vector
            def _(e):
                for f in self.streams["dve"]:
                    f(e)

            @block.gpsimd
            def _(e):
                for f in self.streams["pool"]:
                    f(e)

            @block.sync
            def _(e):
                for f in self.streams["sp"]:
                    f(e)


def build(dbg=None, nlayers=NL, branches=(0, 1, 2), do_ffn=True):
    nc = bass.Bass("TRN2", target_bir_lowering=False)
    es = ExitStack()
    fw = FW(nc, es)

    def dram(name, shape, kind="ExternalInput"):
        return nc.dram_tensor(name, shape, F32, kind=kind).ap()

    xT_d = dram("xT", [D, S])
    w_in_d = dram("w_in", [NL, D, 6912])
    w_br_d = dram("w_br", [NL, 3, 512, D])
    w_out_d = dram("w_out", [NL, D, D])
    w_up_d = dram("w_up", [NL, D, 5632])
    w_dn_d = dram("w_dn", [NL, 2816, D])
    vecs_d = dram("vecs", [128, NV])
    cos_d = dram("cosT", [128, S])
    sin_d = dram("sinT", [128, S])
    cst_d = dram("cst", [128, 2048])
    out_d = dram("outT", [D, S], kind="ExternalOutput")

    def sb(name, shape, dt):
        return es.enter_context(nc.sbuf_tensor(name, shape, dt))

    xT = sb("xT_s", [128, 8, S], F32)
    hT = sb("hT_s", [128, 8, S], BF16)
    oT = sb("oT_s", [128, 4, S], BF16)
    vecs = sb("vecs_s", [128, NV], F32)
    esink = sb("esink", [128, 8 * NL], F32)
    cst = sb("cst_s", [128, 2048], BF16)
    arena = sb("arena", [128, 14 * 1024], F32)
    wring = [sb(f"wring{i}", [128, 3072], BF16) for i in range(3)]
    ropeb = [sb(f"rope{i}", [128, 2, 512], F32) for i in range(2)]
    banks = [es.enter_context(nc.psum_tensor(f"bank{i}", [128, 512], F32)) for i in range(8)]
    bres = [Res() for _ in range(8)]

    ident = cst[:, 0:128]
    tincl = cst[:, 128:256]
    tcomp = cst[:, 256:384]
    m_le = cst[:, 384:512]
    m_lt = cst[:, 512:640]
    m_gt = cst[:, 640:768]
    m_legt = cst[:, 768:1024]
    onesm = cst[:, 1024:1152]

    r_x = [[Res() for _ in range(4)] for _ in range(8)]
    r_h = [Res() for _ in range(4)]
    r_oT = [[Res() for _ in range(4)] for _ in range(4)]
    r_vecs = Res()
    r_cst = Res()
    r_esink = Res()
    r_wring = [Res() for _ in range(3)]
    r_rope = [Res() for _ in range(2)]
    wr_i = [0]
    last_w = [None]
    rp_i = [0]

    fw.dma("sp", "d_vecs", vecs[:, :], vecs_d[:, :], writes=[r_vecs])
    fw.dma("pool", "d_cst", cst[:, :], cst_d[:, :], writes=[r_cst])
    for c in range(8):
        for tb in range(4):
            fw.dma("sp", "d_x", xT[:, c, tb * 512:(tb + 1) * 512], xT_d[c * 128:(c + 1) * 128, tb * 512:(tb + 1) * 512],
                   writes=[r_x[c][tb]])
    for c in range(8):
        for tb in range(4):
            r_x[c][tb].w = ("d_x", fw.cnt["d_x"])
    for l in range(NL):
        fw.op("act", lambda e, l=l: e.activation(out=esink[:, l * 8:(l + 1) * 8], in_=vecs[:, l * LV + 216:l * LV + 224], func=AF.Exp),
              reads=[r_vecs], writes=[r_esink])

    def vcol(l, off, n=1):
        return vecs[:, l * LV + off:l * LV + off + n]

    def load_w(pieces):
        i = wr_i[0] % 3
        wr_i[0] += 1
        slot, res = wring[i], r_wring[i]
        views = []
        off = 0
        for src in pieces:
            rows, n = src.shape
            kc = rows // 128
            v = slot[:, off:off + kc * n].rearrange("p (c n) -> p c n", c=kc)
            last_w[0] = fw.dma("pool", f"d_w{i}", v, src.rearrange("(c p) n -> p c n", p=128), writes=[res],
                               extra=([last_w[0]] if (last_w[0] is not None and not views) else []))
            views.append(v)
            off += kc * n
        assert off <= 3072
        return views, res

    bank_rr = [0]

    def next_bank(pool):
        b = pool[bank_rr[0] % len(pool)]
        bank_rr[0] += 1
        return b

    def dense(bk, cols, wv, wres, nkc, rhs_fn, rhs_res, wcol0=0, m=128):
        fns = []
        for kc in range(nkc):
            fns.append(lambda e, kc=kc: e.matmul(banks[bk][0:m, cols], lhsT=wv[:, kc, wcol0:wcol0 + m], rhs=rhs_fn(kc),
                                                 start=(kc == 0), stop=(kc == nkc - 1)))
        return fw.op("pe", fns, reads=[wres] + list(rhs_res), writes=[bres[bk]])

    def rmsnorm_to_hT(l, goff):
        sq = arena[:, 0:2048].bitcast(BF16).rearrange("p (c n) -> p c n", c=8)
        rsd = arena[:, 2048:2048 + 1024].rearrange("p (i n) -> p i n", i=2)
        r_sq = Res()
        r_rs = [Res(), Res()]
        for tb in range(4):
            cs = slice(tb * 512, (tb + 1) * 512)
            fw.op("act", lambda e, cs=cs: e.activation(out=sq, in_=xT[:, :, cs], func=AF.Square),
                  reads=[r_x[c][tb] for c in range(8)], writes=[r_sq])
            bk = next_bank(list(range(8)))
            fns = [lambda e, c=c, bk=bk: e.matmul(banks[bk][:, :], lhsT=onesm, rhs=sq[:, c, :], start=(c == 0), stop=(c == 7))
                   for c in range(8)]
            fw.op("pe", fns, reads=[r_sq, r_cst], writes=[bres[bk]])
            rs = rsd[:, tb % 2, :]
            fw.op("act", lambda e, bk=bk, rs=rs: e.activation(out=rs, in_=banks[bk][:, :], func=AF.Ln, bias=1e-6),
                  reads=[bres[bk]], writes=[r_rs[tb % 2]])
            fw.op("act", lambda e, rs=rs: e.activation(out=rs, in_=rs, func=AF.Exp, scale=-0.5),
                  reads=[r_rs[tb % 2]], writes=[r_rs[tb % 2]])
            fns = [lambda e, c=c, cs=cs, rs=rs: e.scalar_tensor_tensor(out=hT[:, c, cs], in0=xT[:, c, cs], scalar=vcol(l, goff + c),
                                                                       in1=rs, op0=ALU.mult, op1=ALU.mult) for c in range(8)]
            fw.op("dve", fns, reads=[r_x[c][tb] for c in range(8)] + [r_rs[tb % 2], r_vecs], writes=[r_h[tb]])

    def load_rope(tb):
        i = rp_i[0] % 2
        rp_i[0] += 1
        cs = slice(tb * 512, (tb + 1) * 512)
        fw.dma("sp", f"d_rp{i}", ropeb[i][:, 0, :], cos_d[:, cs], writes=[r_rope[i]])
        fw.dma("sp", f"d_rp{i}", ropeb[i][:, 1, :], sin_d[:, cs], writes=[r_rope[i]])
        return ropeb[i], r_rope[i]

    def rope_evac(bk, dst, dst_res, tb, t1, t2, r_t):
        rb, rres = load_rope(tb)
        fw.op("dve", lambda e: e.tensor_tensor(out=t1, in0=banks[bk][:, :], in1=rb[:, 0, :], op=ALU.mult),
              reads=[bres[bk], rres], writes=[r_t[0]])
        fns = []
        for g in range(4):
            pg = g ^ 1
            fns.append(lambda e, g=g, pg=pg: e.tensor_tensor(out=t2[g * 32:(g + 1) * 32, :], in0=banks[bk][pg * 32:(pg + 1) * 32, :],
                                                             in1=rb[g * 32:(g + 1) * 32, 1, :], op=ALU.mult))
        fw.op("dve", fns, reads=[bres[bk], rres], writes=[r_t[1]])
        fw.op("dve", lambda e: e.tensor_tensor(out=dst, in0=t1, in1=t2, op=ALU.add), reads=[r_t[0], r_t[1]], writes=[dst_res])

    def mixer_branch(l, br):
        A = arena
        kT = A[:, 0:1024].bitcast(BF16)
        qT = A[:, 1024:2048].bitcast(BF16)
        vaug = A[:, 2048:2048 + 1040].bitcast(BF16).rearrange("p (t h d) -> p t h d", t=16, h=2)
        off = 2048 + 1040
        t1 = A[:, off:off + 512]
        t2 = A[:, off + 512:off + 1024]
        off += 1024
        ebuf = [[A[:, off + (2 * s + h) * 512: off + (2 * s + h + 1) * 512] for h in range(2)] for s in range(2)]
        off += 2048
        exbuf = [[A[:, off + (2 * s + h) * 512: off + (2 * s + h + 1) * 512] for h in range(2)] for s in range(2)]
        off += 2048
        spb = [[A[:, off + (2 * s + h) * 256: off + (2 * s + h + 1) * 256].bitcast(BF16) for h in range(2)] for s in range(2)]
        off += 1024
        pb = [[A[:, off + (2 * s + h) * 256: off + (2 * s + h + 1) * 256].bitcast(BF16) for h in range(2)] for s in range(2)]
        off += 1024
        otok = A[:, off:off + 256].bitcast(BF16)
        off += 256
        mbt = A[:, off:off + 256].bitcast(BF16)
        off += 256
        mb = A[:, off:off + 256].bitcast(BF16)
        off += 256
        small = A[:, off:off + 256]
        off += 256
        assert off <= 14 * 1024
        gm = small[:, 0:64].rearrange("p (h i n) -> p h i n", h=2, i=4)
        mx = small[:, 64:72]
        selt = small[:, 72:80]
        den = small[:, 80:88].rearrange("p (h i) -> p h i", h=2)
        ksum = small[:, 96:104]
        kmf = small[:, 104:112]
        kmh = small[:, 112:116].bitcast(BF16)
        kml = small[:, 116:120].bitcast(BF16)

        r_kT = [Res() for _ in range(4)]
        r_qT = [Res() for _ in range(4)]
        r_v = [Res() for _ in range(4)]
        r_t = [Res(), Res()]
        r_e = [[Res(), Res()], [Res(), Res()]]
        r_ex = [[Res(), Res()], [Res(), Res()]]
        r_sp = [[Res(), Res()], [Res(), Res()]]
        r_p = [[Res(), Res()], [Res(), Res()]]
        r_otok, r_mbt, r_mb, r_small, r_km = Res(), Res(), Res(), Res(), Res()

        if br == 0:
            units = [0]
        else:
            units = [0, 1, 2, 3]
        base = {0: 0, 1: 768, 2: 2304}[br]
        has_den = br != 1
        vw = 65 if has_den else 64

        if has_den:
            fw.op("dve", lambda e: e.memset(vaug[:, :, :, 64:65], 1.0), writes=r_v)
        if br == 2:
            fw.op("dve", lambda e: e.memset(mbt, 0.0), writes=[r_mbt])

        def proj_kv(wv, wres, kcol, vcol0):
            for tb in range(4):
                bk = next_bank([0, 1, 2, 3, 7])
                cs = slice(tb * 512, (tb + 1) * 512)
                dense(bk, slice(0, 512), wv, wres, 8, lambda kc, cs=cs: hT[:, kc, cs], [r_h[tb]], wcol0=kcol)
                if br == 1:
                    fw.op("act", lambda e, bk=bk, cs=cs: e.activation(out=kT[:, cs], in_=banks[bk][:, :], func=AF.Copy),
                          reads=[bres[bk]], writes=[r_kT[tb]])
                else:
                    rope_evac(bk, kT[:, cs], r_kT[tb], tb, t1, t2, r_t)
            for g4 in range(4):
                bk = next_bank([0, 1, 2, 3, 7])
                fns = []
                for j in range(4):
                    tt = g4 * 4 + j
                    for kc in range(8):
                        fns.append(lambda e, j=j, tt=tt, kc=kc, bk=bk: e.matmul(
                            banks[bk][:, j * 128:(j + 1) * 128], lhsT=hT[:, kc, tt * 128:(tt + 1) * 128],
                            rhs=wv[:, kc, vcol0:vcol0 + 128], start=(kc == 0), stop=(kc == 7)))
                fw.op("pe", fns, reads=[wres, r_h[g4]], writes=[bres[bk]])
                fw.op("act", lambda e, bk=bk, g4=g4: e.activation(
                    out=vaug[:, g4 * 4:(g4 + 1) * 4, :, 0:64],
                    in_=banks[bk][:, :].rearrange("p (t h d) -> p t h d", t=4, h=2), func=AF.Copy),
                    reads=[bres[bk]], writes=[r_v[g4]])

        def proj_q(wv, wres, qcol, Q):
            bk = next_bank([7])
            cs = slice(Q * 512, (Q + 1) * 512)
            dense(bk, slice(0, 512), wv, wres, 8, lambda kc: hT[:, kc, cs], [r_h[Q]], wcol0=qcol)
            if br == 1:
                fw.op("act", lambda e: e.activation(out=qT[:, cs], in_=banks[bk][:, :], func=AF.Copy),
                      reads=[bres[bk]], writes=[r_qT[Q]])
            else:
                rope_evac(bk, qT[:, cs], r_qT[Q], Q, t1, t2, r_t)

        def finish_o(obanks, Q, chunk, sink_cols):
            ot4 = otok.rearrange("p (i f) -> p i f", i=4)
            if has_den:
                for hh in range(2):
                    o3 = banks[obanks[hh]][:, 0:260].rearrange("p (i d) -> p i d", i=4)
                    if sink_cols is not None:
                        fw.op("dve", lambda e, hh=hh, o3=o3: e.tensor_scalar(out=den[:, hh, :], in0=o3[:, :, 64], scalar1=esink[:, sink_cols[hh]:sink_cols[hh] + 1],
                                                                            scalar2=None, op0=ALU.add),
                              reads=[bres[obanks[hh]], r_esink], writes=[r_small])
                    else:
                        fw.op("dve", lambda e, hh=hh, o3=o3: e.tensor_copy(out=den[:, hh, :], in_=o3[:, :, 64]),
                              reads=[bres[obanks[hh]]], writes=[r_small])
                    fw.op("dve", lambda e, hh=hh: e.reciprocal(out=den[:, hh, :], in_=den[:, hh, :]), reads=[r_small], writes=[r_small])
                    fw.op("dve", lambda e, hh=hh, o3=o3: e.tensor_tensor(out=ot4[:, :, hh * 64:(hh + 1) * 64], in0=o3[:, :, 0:64],
                                                                        in1=den[:, hh, :].unsqueeze(2).to_broadcast([128, 4, 64]), op=ALU.mult),
                          reads=[bres[obanks[hh]], r_small], writes=[r_otok])
            else:
                fw.op("act", lambda e: e.activation(out=otok, in_=banks[obanks[0]][:, :], func=AF.Copy),
                      reads=[bres[obanks[0]]], writes=[r_otok])
            tb7 = banks[7][:, 0:256].bitcast(BF16)
            fns = [lambda e, i=i: e.transpose(tb7[:, i * 128:(i + 1) * 128], ot4[:, i, :], ident) for i in range(4)]
            fw.op("pe", fns, reads=[r_otok, r_cst], writes=[bres[7]])
            fw.op("dve", lambda e: e.tensor_copy(out=oT[:, chunk, Q * 512:(Q + 1) * 512], in_=tb7), reads=[bres[7]], writes=[r_oT[chunk][Q]])

        if br == 0:
            (wkv,), wres = load_w([w_in_d[l, :, 512:768]])
            proj_kv(wkv, wres, 0, 128)
            for c in range(4):
                (wq,), wqres = load_w([w_in_d[l, :, c * 128:(c + 1) * 128]])
                for Q in range(4):
                    proj_q(wq, wqres, 0, Q)
                    ob = [2 + (Q % 2), 4 + (Q % 2)]
                    first = [True, True]
                    steps = list(range(max(0, 4 * Q - 1), 4 * Q + 4))
                    for si, a in enumerate(steps):
                        s = si % 2
                        i0 = a - 4 * Q
                        if i0 < 0:
                            qt, msk = [0], m_gt
                        elif i0 == 3:
                            qt, msk = [3], m_le
                        else:
                            qt, msk = [i0, i0 + 1], m_legt
                        n = 128 * len(qt)
                        qc = slice(Q * 512 + qt[0] * 128, Q * 512 + qt[0] * 128 + n)
                        for hh in range(2):
                            rows = slice(hh * 64, (hh + 1) * 64)
                            sres = bres[hh]
                            sc = slice(s * 256, s * 256 + n)
                            fw.op("pe", lambda e, hh=hh, rows=rows, sc=sc, qc=qc, a=a: e.matmul(
                                banks[hh][:, sc], lhsT=kT[rows, a * 128:(a + 1) * 128], rhs=qT[rows, qc], start=True, stop=True),
                                reads=[r_kT[a // 4], r_qT[Q]], writes=[sres])
                            P = pb[s][hh][:, 0:n]
                            fw.op("act", lambda e, hh=hh, sc=sc, P=P: e.activation(out=P, in_=banks[hh][:, sc], func=AF.Exp, scale=0.125),
                                  reads=[sres], writes=[r_p[s][hh]])
                            fw.op("dve", lambda e, P=P, msk=msk, n=n: e.tensor_tensor(out=P, in0=P, in1=msk[:, 0:n], op=ALU.mult),
                                  reads=[r_p[s][hh], r_cst], writes=[r_p[s][hh]])
                            fns = []
                            for j, i in enumerate(qt):
                                fns.append(lambda e, hh=hh, j=j, i=i, P=P, a=a, st=first[hh] and j == 0, ob=ob: e.matmul(
                                    banks[ob[hh]][:, i * 65:(i + 1) * 65], lhsT=P[:, j * 128:(j + 1) * 128], rhs=vaug[:, a, hh, :],
                                    start=st, stop=False, skip_group_check=True))
                            first[hh] = False
                            fw.op("pe", fns, reads=[r_p[s][hh], r_v[a // 4]], writes=[bres[ob[hh]]])
                    finish_o(ob, Q, c, [l * 8 + c, l * 8 + 4 + c])
            return

        for u in units:
            qc0 = base + u * 128
            kc0 = base + 512 + u * 128
            vc0 = base + 1024 + u * 128
            (wq, wk, wvv), wres = load_w([w_in_d[l, :, qc0:qc0 + 128], w_in_d[l, :, kc0:kc0 + 128], w_in_d[l, :, vc0:vc0 + 128]])
            for tb in range(4):
                bk = next_bank([0, 1, 2, 3, 7])
                cs = slice(tb * 512, (tb + 1) * 512)
                dense(bk, slice(0, 512), wk, wres, 8, lambda kc, cs=cs: hT[:, kc, cs], [r_h[tb]])
                if br == 1:
                    fw.op("act", lambda e, bk=bk, cs=cs: e.activation(out=kT[:, cs], in_=banks[bk][:, :], func=AF.Copy),
                          reads=[bres[bk]], writes=[r_kT[tb]])
                else:
                    rope_evac(bk, kT[:, cs], r_kT[tb], tb, t1, t2, r_t)
            for g4 in range(4):
                bk = next_bank([0, 1, 2, 3, 7])
                fns = []
                for j in range(4):
                    tt = g4 * 4 + j
                    for kc in range(8):
                        fns.append(lambda e, j=j, tt=tt, kc=kc, bk=bk, wvv=wvv: e.matmul(
                            banks[bk][:, j * 128:(j + 1) * 128], lhsT=hT[:, kc, tt * 128:(tt + 1) * 128],
                            rhs=wvv[:, kc, :], start=(kc == 0), stop=(kc == 7)))
                fw.op("pe", fns, reads=[wres, r_h[g4]], writes=[bres[bk]])
                fw.op("act", lambda e, bk=bk, g4=g4: e.activation(
                    out=vaug[:, g4 * 4:(g4 + 1) * 4, :, 0:64],
                    in_=banks[bk][:, :].rearrange("p (t h d) -> p t h d", t=4, h=2), func=AF.Copy),
                    reads=[bres[bk]], writes=[r_v[g4]])
            if br == 2:
                fw.op("dve", lambda e: e.tensor_reduce(out=ksum, in_=kT.rearrange("p (n k) -> p n k", n=8), axis=AX.X, op=ALU.add),
                      reads=r_kT, writes=[r_small])
                fw.op("dve", lambda e: e.tensor_scalar(out=kmf, in0=ksum, scalar1=1.0 / 256, scalar2=None, op0=ALU.mult),
                      reads=[r_small], writes=[r_small])
                fw.op("dve", lambda e: e.tensor_copy(out=kmh, in_=kmf), reads=[r_small], writes=[r_km])
                fw.op("dve", lambda e: e.tensor_tensor(out=kml, in0=kmf, in1=kmh, op=ALU.subtract), reads=[r_small, r_km], writes=[r_km])

            for Q in range(4):
                proj_q(wq, wres, 0, Q)
                qcs = slice(Q * 512, (Q + 1) * 512)
                need_sel = (br == 2 and Q >= 2)
                if need_sel:
                    gb = [5, 6]
                    gb = [6, 7]
                    for hh in range(2):
                        rows = slice(hh * 64, (hh + 1) * 64)
                        fns = []
                        for i in range(4):
                            qi = slice(Q * 512 + i * 128, Q * 512 + (i + 1) * 128)
                            fns.append(lambda e, hh=hh, i=i, qi=qi, rows=rows: e.matmul(banks[gb[hh]][:, i * 8:(i + 1) * 8], lhsT=qT[rows, qi], rhs=kmh[rows, :], start=True, stop=False))
                            fns.append(lambda e, hh=hh, i=i, qi=qi, rows=rows: e.matmul(banks[gb[hh]][:, i * 8:(i + 1) * 8], lhsT=qT[rows, qi], rhs=kml[rows, :], start=False, stop=True))
                        fw.op("pe", fns, reads=[r_qT[Q], r_km], writes=[bres[gb[hh]]])
                    fw.op("dve", lambda e: e.memset(gm, -1e30), writes=[r_small])
                    for hh in range(2):
                        g3 = banks[gb[hh]][:, 0:32].rearrange("p (i n) -> p i n", i=4)
                        fw.op("dve", [lambda e, hh=hh, g3=g3, Q=Q: e.tensor_copy(out=gm[:, hh, 0:2, 0:2 * Q], in_=g3[:, 0:2, 0:2 * Q]),
                                      lambda e, hh=hh, g3=g3, Q=Q: e.tensor_copy(out=gm[:, hh, 2:4, 0:2 * Q + 1], in_=g3[:, 2:4, 0:2 * Q + 1])],
                              reads=[bres[gb[hh]]], writes=[r_small])
                    mbt4 = mbt.rearrange("p (i f) -> p i f", i=4)
                    for hh in range(2):
                        for i in range(4):
                            nb = 2 * Q + i // 2
                            fw.op("dve", lambda e, hh=hh, i=i: e.max(out=mx, in_=gm[:, hh, i, :]), reads=[r_small], writes=[r_small])
                            fw.op("dve", lambda e, hh=hh, i=i: e.tensor_scalar(out=selt, in0=gm[:, hh, i, :], scalar1=mx[:, 2:3], scalar2=None, op0=ALU.is_ge),
                                  reads=[r_small], writes=[r_small])
                            fw.op("dve", [lambda e, hh=hh, i=i: e.tensor_scalar(out=mbt4[:, i, hh * 64:hh * 64 + 8], in0=selt, scalar1=-1.0, scalar2=-NEG,
                                                                                op0=ALU.add, op1=ALU.mult),
                                          lambda e, hh=hh, i=i, nb=nb: e.memset(mbt4[:, i, hh * 64 + nb:hh * 64 + nb + 1], 0.0)],
                                  reads=[r_small], writes=[r_mbt])
                    tb7 = banks[7][:, 0:256].bitcast(BF16)
                    fns = [lambda e, i=i: e.transpose(tb7[:, i * 128:(i + 1) * 128], mbt4[:, i, :], ident) for i in range(4)]
                    fw.op("pe", fns, reads=[r_mbt, r_cst], writes=[bres[7]])
                    fw.op("dve", lambda e: e.tensor_copy(out=mb, in_=tb7), reads=[bres[7]], writes=[r_mb])

                steps = list(range(4 * Q + 3, -1, -1))
                ns = len(steps)
                if br == 1:
                    ob = [6, 6]
                    xb = [4, 5]
                else:
                    ob = [4, 5]

                def cols_of(a):
                    i0 = max(0, a - 4 * Q)
                    return i0, slice(i0 * 128, 512)

                def stage_qk(t):
                    a = steps[t]
                    s = t % 2
                    i0, cl = cols_of(a)
                    qcl = slice(Q * 512 + i0 * 128, (Q + 1) * 512)
                    for hh in range(2):
                        rows = slice(hh * 64, (hh + 1) * 64)
                        bk = 2 * hh + s
                        msk = need_sel and a < 4 * Q + 2
                        fns = [lambda e, a=a, rows=rows, bk=bk, cl=cl, qcl=qcl, msk=msk: e.matmul(
                            banks[bk][:, cl], lhsT=kT[rows, a * 128:(a + 1) * 128], rhs=qT[rows, qcl], start=True, stop=not msk)]
                        rd = [r_kT[a // 4], r_qT[Q]]
                        if msk:
                            n = a // 2
                            fns.append(lambda e, rows=rows, bk=bk, cl=cl, n=n: e.matmul(
                                banks[bk][:, cl], lhsT=cst[rows, 1152 + n * 128:1152 + (n + 1) * 128], rhs=mb[rows, cl], start=False, stop=True))
                            rd += [r_mb, r_cst]
                        fw.op("pe", fns, reads=rd, writes=[bres[bk]])
                        if br == 1:
                            E = ebuf[s][hh]
                            SP = spb[s][hh]
                            fw.op("act", lambda e, bk=bk, cl=cl, E=E: e.activation(out=E[:, cl], in_=banks[bk][:, cl], func=AF.Exp, scale=0.125),
                                  reads=[bres[bk]], writes=[r_e[s][hh]])
                            fw.op("act", lambda e, cl=cl, E=E, SP=SP: e.activation(out=SP[:, cl], in_=E[:, cl], func=AF.Ln, bias=1.0),
                                  reads=[r_e[s][hh]], writes=[r_sp[s][hh]])
                            if a >= 4 * Q:
                                dc = slice(i0 * 128, (i0 + 1) * 128)
                                fw.op("dve", lambda e, SP=SP, dc=dc: e.tensor_tensor(out=SP[:, dc], in0=SP[:, dc], in1=m_lt, op=ALU.mult),
                                      reads=[r_sp[s][hh], r_cst], writes=[r_sp[s][hh]])
                        else:
                            P = pb[s][hh]
                            fw.op("act", lambda e, bk=bk, cl=cl, P=P: e.activation(out=P[:, cl], in_=banks[bk][:, cl], func=AF.Exp, scale=0.125),
                                  reads=[bres[bk]], writes=[r_p[s][hh]])
                            if a >= 4 * Q:
                                dc = slice(i0 * 128, (i0 + 1) * 128)
                                fw.op("dve", lambda e, P=P, dc=dc: e.tensor_tensor(out=P[:, dc], in0=P[:, dc], in1=m_le, op=ALU.mult),
                                      reads=[r_p[s][hh], r_cst], writes=[r_p[s][hh]])

                def stage_x(t):
                    a = steps[t]
                    s = t % 2
                    i0, cl = cols_of(a)
                    last = (t == ns - 1)
                    for hh in range(2):
                        SP = spb[s][hh]
                        fw.op("pe", lambda e, hh=hh, cl=cl, SP=SP, t=t, last=last, xb=xb: e.matmul(
                            banks[xb[hh]][:, cl], lhsT=tincl, rhs=SP[:, cl], start=(t == 0), stop=last, skip_group_check=True),
                            reads=[r_sp[s][hh], r_cst], writes=[bres[xb[hh]]])
                    for hh in range(2):
                        EX = exbuf[s][hh]
                        fw.op("act", lambda e, hh=hh, cl=cl, EX=EX, xb=xb: e.activation(out=EX[:, cl], in_=banks[xb[hh]][:, cl], func=AF.Exp),
                              reads=[bres[xb[hh]]], writes=[r_ex[s][hh]])
                    if not last:
                        for hh in range(2):
                            SP = spb[s][hh]
                            fw.op("pe", lambda e, hh=hh, cl=cl, SP=SP, xb=xb: e.matmul(
                                banks[xb[hh]][:, cl], lhsT=tcomp, rhs=SP[:, cl], start=False, stop=False, skip_group_check=True),
                                reads=[r_sp[s][hh], r_cst], writes=[bres[xb[hh]]])
                    for hh in range(2):
                        E, EX, W = ebuf[s][hh], exbuf[s][hh], pb[s][hh]
                        fw.op("dve", lambda e, cl=cl, E=E, EX=EX, W=W: e.tensor_tensor(out=W[:, cl], in0=E[:, cl], in1=EX[:, cl], op=ALU.mult),
                              reads=[r_e[s][hh], r_ex[s][hh]], writes=[r_p[s][hh]])
                        if a >= 4 * Q:
                            dc = slice(i0 * 128, (i0 + 1) * 128)
                            fw.op("dve", lambda e, W=W, dc=dc: e.tensor_tensor(out=W[:, dc], in0=W[:, dc], in1=m_lt, op=ALU.mult),
                                  reads=[r_p[s][hh], r_cst], writes=[r_p[s][hh]])

                def stage_pv(t):
                    a = steps[t]
                    s = t % 2
                    i0, cl = cols_of(a)
                    for hh in range(2):
                        P = pb[s][hh]
                        fns = []
                        for i in range(i0, 4):
                            if br == 1:
                                oc = slice(i * 128 + hh * 64, i * 128 + hh * 64 + 64)
                                st = (t == 0 and hh == 0 and i == i0)
                                rhs = vaug[:, a, hh, 0:64]
                            else:
                                oc = slice(i * 65, (i + 1) * 65)
                                st = (t == 0 and i == i0)
                                rhs = vaug[:, a, hh, :]
                            fns.append(lambda e, hh=hh, i=i, oc=oc, st=st, rhs=rhs, P=P, ob=ob: e.matmul(
                                banks[ob[hh]][:, oc], lhsT=P[:, i * 128:(i + 1) * 128], rhs=rhs, start=st, stop=False, skip_group_check=True))
                        fw.op("pe", fns, reads=[r_p[s][hh], r_v[a // 4]], writes=[bres[ob[hh]]])

                stage_qk(0)
                for t in range(ns):
                    if t + 1 < ns:
                        stage_qk(t + 1)
                    if br == 1:
                        stage_x(t)
                        if t >= 1:
                            stage_pv(t - 1)
                    else:
                        stage_pv(t)
                if br == 1:
                    stage_pv(ns - 1)
                finish_o(ob, Q, u, None)

    def post_branch(l, br):
        mT = arena[:, 0:8192].bitcast(BF16).rearrange("p (c n) -> p c n", c=8)
        sg = [arena[:, 8192 + i * 512: 8192 + (i + 1) * 512] for i in range(2)]
        r_sg = [Res(), Res()]
        r_m = [[Res() for _ in range(4)] for _ in range(8)]
        k = 0
        for j in range(8):
            gcol = 3840 + br * 1024 + j * 128
            if br == 0:
                srcs = [w_in_d[l, :, gcol:gcol + 128]]
                (wg,), wres = load_w(srcs)
                i2 = wr_i[0] % 3
                wr_i[0] += 1
                slot, wres2 = wring[i2], r_wring[i2]
                wb = slot[:, 0:512].rearrange("p (c n) -> p c n", c=4)
                for c in range(4):
                    fw.dma("pool", f"d_w{i2}", wb[0:64, c, :], w_br_d[l, 0, c * 64:(c + 1) * 64, j * 128:(j + 1) * 128], writes=[wres2],
                           extra=[last_w[0]])
                    last_w[0] = fw.dma("pool", f"d_w{i2}", wb[64:128, c, :], w_br_d[l, 0, (4 + c) * 64:(5 + c) * 64, j * 128:(j + 1) * 128], writes=[wres2])
            else:
                (wg, wb), wres = load_w([w_in_d[l, :, gcol:gcol + 128], w_br_d[l, br, :, j * 128:(j + 1) * 128]])
                wres2 = wres
            for tb in range(4):
                cs = slice(tb * 512, (tb + 1) * 512)
                bg = next_bank(list(range(8)))
                dense(bg, slice(0, 512), wg, wres, 8, lambda kc, cs=cs: hT[:, kc, cs], [r_h[tb]])
                by = next_bank(list(range(8)))
                dense(by, slice(0, 512), wb, wres2, 4, lambda kc, cs=cs: oT[:, kc, cs], [r_oT[c][tb] for c in range(4)])
                s = k % 2
                k += 1
                fw.op("act", lambda e, bg=bg, s=s, j=j: e.activation(out=sg[s], in_=banks[bg][:, :], func=AF.Sigmoid, bias=vcol(l, 16 + br * 8 + j)),
                      reads=[bres[bg], r_vecs], writes=[r_sg[s]])
                fw.op("dve", lambda e, by=by, s=s, j=j, cs=cs: e.tensor_tensor(out=mT[:, j, cs], in0=banks[by][:, :], in1=sg[s], op=ALU.mult),
                      reads=[bres[by], r_sg[s]], writes=[r_m[j][tb]])
        for j in range(8):
            (wo,), wres = load_w([w_out_d[l, :, j * 128:(j + 1) * 128]])
            for tb in range(4):
                cs = slice(tb * 512, (tb + 1) * 512)
                bk = next_bank(list(range(8)))
                dense(bk, slice(0, 512), wo, wres, 8, lambda kc, cs=cs: mT[:, kc, cs], [r_m[c][tb] for c in range(8)])
                fw.op("dve", lambda e, bk=bk, j=j, cs=cs: e.tensor_tensor(out=xT[:, j, cs], in0=banks[bk][:, :], in1=xT[:, j, cs], op=ALU.add),
                      reads=[bres[bk]], writes=[r_x[j][tb]])

    def ffn(l):
        actT = arena[:, 0:11 * 1024].bitcast(BF16).rearrange("p (c n) -> p c n", c=11)
        off = 11 * 1024
        ub = [[arena[:, off + (2 * s + z) * 514: off + (2 * s + z + 1) * 514] for z in range(2)] for s in range(2)]
        off += 4 * 514
        rb_ = [[oT[:, 2 * s + z, 0:1024].bitcast(F32) for z in range(2)] for s in range(2)]
        assert off <= 14 * 1024, off
        r_ub = [[Res(), Res()], [Res(), Res()]]
        r_rb = [[Res(), Res()], [Res(), Res()]]
        for half in range(2):
            r_act = [[Res() for _ in range(4)] for _ in range(11)]
            for f in range(11):
                fc = half * 11 + f
                (wa, wvv), wres = load_w([w_up_d[l, :, fc * 128:(fc + 1) * 128], w_up_d[l, :, 2816 + fc * 128:2816 + (fc + 1) * 128]])
                for tb in range(4):
                    cs = slice(tb * 512, (tb + 1) * 512)
                    s = tb % 2
                    for z, wz in enumerate((wa, wvv)):
                        ch = fc + z * 22
                        bk = next_bank(list(range(8)))
                        dense(bk, slice(0, 512), wz, wres, 8, lambda kc, cs=cs: hT[:, kc, cs], [r_h[tb]])
                        U, R = ub[s][z], rb_[s][z]
                        if tb == 0:
                            fw.op("dve", lambda e, U=U: e.memset(U[:, 0:2], 0.0), writes=[r_ub[s][z]])
                        else:
                            Up = ub[1 - s][z]
                            fw.op("dve", lambda e, U=U, Up=Up: e.tensor_copy(out=U[:, 0:2], in_=Up[:, 512:514]),
                                  reads=[r_ub[1 - s][z]], writes=[r_ub[s][z]])
                        fw.op("act", lambda e, U=U, bk=bk: e.activation(out=U[:, 2:514], in_=banks[bk][:, :], func=AF.Copy),
                              reads=[bres[bk]], writes=[r_ub[s][z]])
                        fw.op("act", lambda e, R=R, bk=bk, ch=ch: e.activation(out=R, in_=banks[bk][:, :], func=AF.Identity,
                                                                             scale=vcol(l, 40 + 2 * 44 + ch), bias=vcol(l, 172 + ch)),
                              reads=[bres[bk], r_vecs], writes=[r_rb[s][z]])
                        fw.op("dve", lambda e, R=R, U=U, ch=ch: e.scalar_tensor_tensor(out=R, in0=U[:, 1:513], scalar=vcol(l, 40 + 44 + ch), in1=R,
                                                                                      op0=ALU.mult, op1=ALU.add),
                              reads=[r_ub[s][z], r_rb[s][z]], writes=[r_rb[s][z]])
                        fw.op("dve", lambda e, R=R, U=U, ch=ch: e.scalar_tensor_tensor(out=R, in0=U[:, 0:512], scalar=vcol(l, 40 + ch), in1=R,
                                                                                      op0=ALU.mult, op1=ALU.add),
                              reads=[r_ub[s][z], r_rb[s][z]], writes=[r_rb[s][z]])
                    Ra, Rv = rb_[s][0], rb_[s][1]
                    fw.op("act", lambda e, Ra=Ra: e.activation(out=Ra, in_=Ra, func=AF.Silu), reads=[r_rb[s][0]], writes=[r_rb[s][0]])
                    fw.op("dve", lambda e, Ra=Ra, Rv=Rv, f=f, cs=cs: e.tensor_tensor(out=actT[:, f, cs], in0=Ra, in1=Rv, op=ALU.mult),
                          reads=[r_rb[s][0], r_rb[s][1]], writes=[r_act[f][tb]])
            for j in range(8):
                (wd,), wres = load_w([w_dn_d[l, half * 1408:(half + 1) * 1408, j * 128:(j + 1) * 128]])
                for tb in range(4):
                    cs = slice(tb * 512, (tb + 1) * 512)
                    bk = next_bank(list(range(8)))
                    dense(bk, slice(0, 512), wd, wres, 11, lambda kc, cs=cs: actT[:, kc, cs], [r_act[c][tb] for c in range(11)])
                    fw.op("dve", lambda e, bk=bk, j=j, cs=cs: e.tensor_tensor(out=xT[:, j, cs], in0=banks[bk][:, :], in1=xT[:, j, cs], op=ALU.add),
                          reads=[bres[bk]], writes=[r_x[j][tb]])
            fw.barrier()

    for l in range(nlayers):
        rmsnorm_to_hT(l, 0)
        fw.barrier()
        for br in branches:
            mixer_branch(l, br)
            fw.barrier()
            if dbg == "oT":
                break
            post_branch(l, br)
            fw.barrier()
        if dbg == "oT":
            break
        if do_ffn:
            rmsnorm_to_hT(l, 8)
            fw.barrier()
            ffn(l)
            fw.barrier()

    r_out = Res()
    if dbg == "oT":
        for c in range(4):
            fw.dma("pool", "d_out", out_d[c * 128:(c + 1) * 128, :], oT[:, c, :], reads=[r_oT[c][tb] for tb in range(4)], writes=[r_out])
        fw.wait("sp", r_out.w)
    elif dbg == "x":
        for c in range(8):
            fw.dma("sp", "d_out", out_d[c * 128:(c + 1) * 128, :], xT[:, c, :], reads=[r_x[c][tb] for tb in range(4)], writes=[r_out])
    else:
        sq = arena[:, 0:2048].bitcast(BF16).rearrange("p (c n) -> p c n", c=8)
        rsd = arena[:, 2048:2048 + 512]
        ob = [arena[:, 4096 + i * 4096: 4096 + (i + 1) * 4096].rearrange("p (c n) -> p c n", c=8) for i in range(2)]
        r_sq, r_rs, r_ob = Res(), Res(), [Res(), Res()]
        goff = NL * LV
        for tb in range(4):
            cs = slice(tb * 512, (tb + 1) * 512)
            fw.op("act", lambda e, cs=cs: e.activation(out=sq, in_=xT[:, :, cs], func=AF.Square),
                  reads=[r_x[c][tb] for c in range(8)], writes=[r_sq])
            bk = next_bank(list(range(8)))
            fns = [lambda e, c=c, bk=bk: e.matmul(banks[bk][:, :], lhsT=onesm, rhs=sq[:, c, :], start=(c == 0), stop=(c == 7)) for c in range(8)]
            fw.op("pe", fns, reads=[r_sq, r_cst], writes=[bres[bk]])
            fw.op("act", lambda e, bk=bk: e.activation(out=rsd, in_=banks[bk][:, :], func=AF.Ln, bias=1e-6), reads=[bres[bk]], writes=[r_rs])
            fw.op("act", lambda e: e.activation(out=rsd, in_=rsd, func=AF.Exp, scale=-0.5), reads=[r_rs], writes=[r_rs])
            O = ob[tb % 2]
            fns = [lambda e, c=c, cs=cs, O=O: e.scalar_tensor_tensor(out=O[:, c, :], in0=xT[:, c, cs], scalar=vecs[:, goff + c:goff + c + 1],
                                                                    in1=rsd, op0=ALU.mult, op1=ALU.mult) for c in range(8)]
            fw.op("dve", fns, reads=[r_x[c][tb] for c in range(8)] + [r_rs, r_vecs], writes=[r_ob[tb % 2]])
            fw.dma("sp", "d_out", out_d.rearrange("(c p) n -> p c n", p=128)[:, :, cs], O, reads=[r_ob[tb % 2]], writes=[r_out])
    fw.wait("sp", r_out.w)
    fw.barrier()
    fw.replay()
    es.close()
    return nc


def host_consts():
    cst = np.zeros((128, 2048), np.float32)
    j = np.arange(128)[:, None]
    s = np.arange(128)[None, :]
    cst[:, 0:128] = np.eye(128)
    cst[:, 128:256] = -1.0 * (j >= s)
    cst[:, 256:384] = -1.0 * (j < s)
    cst[:, 384:512] = (j <= s)
    cst[:, 512:640] = (j < s)
    cst[:, 640:768] = (j > s)
    cst[:, 768:896] = (j <= s)
    cst[:, 896:1024] = (j > s)
    cst[:, 1024:1152] = 1.0 / 1024
    for n in range(7):
        cst[n, 1152 + n * 128:1152 + (n + 1) * 128] = 1.0
        cst[64 + n, 1152 + n * 128:1152 + (n + 1) * 128] = 1.0
    inv = (10000.0 ** (-(np.arange(0, 64, 2, dtype=np.float32)) / np.float32(64))).astype(np.float32)
    ang = (np.arange(S, dtype=np.float32)[None, :] * inv[:, None]).astype(np.float32)
    cos, sin = np.cos(ang).astype(np.float32), np.sin(ang).astype(np.float32)
    cosT = np.tile(cos, (4, 1))
    sinT = np.concatenate([-sin, sin, -sin, sin], 0)
    return cst, np.ascontiguousarray(cosT), np.ascontiguousarray(sinT)


def host_prep(inputs):
    f = lambda a: np.ascontiguousarray(np.asarray(a, dtype=np.float32))
    w_in = f(inputs["w_in"]).copy()
    perm = []
    for c in range(4):
        perm += list(range(c * 64, c * 64 + 64)) + list(range((4 + c) * 64, (4 + c) * 64 + 64))
    w_in[:, :, 0:512] = w_in[:, :, perm]
    vecs = np.zeros((128, NV), np.float32)
    pc = lambda v: np.asarray(v, np.float32).reshape(-1, 128).T
    for l in range(NL):
        b = l * LV
        vecs[:, b:b + 8] = pc(inputs["norm_mix"][l])
        vecs[:, b + 8:b + 16] = pc(inputs["norm_ffn"][l])
        vecs[:, b + 16:b + 40] = pc(inputs["b_gate"][l])
        cw = np.asarray(inputs["conv_w"][l], np.float32)
        for tap in range(3):
            vecs[:, b + 40 + tap * 44:b + 40 + (tap + 1) * 44] = pc(cw[tap])
        vecs[:, b + 172:b + 216] = pc(inputs["conv_b"][l])
        vecs[:, b + 216:b + 224] = np.asarray(inputs["sinks"][l], np.float32)[None, :]
    vecs[:, NL * LV:NL * LV + 8] = pc(inputs["norm_final"])
    cst, cosT, sinT = host_consts()
    shared = {
        "w_in": w_in, "w_br": f(inputs["w_branch"]), "w_out": f(inputs["w_out"]), "w_up": f(inputs["w_up"]),
        "w_dn": f(inputs["w_down"]), "vecs": vecs, "cosT": cosT, "sinT": sinT, "cst": cst,
    }
    x = np.asarray(inputs["x"], np.float32)
    in_maps = []
    for b in range(8):
        m = dict(shared)
        m["xT"] = np.ascontiguousarray(x[b].T)
        in_maps.append(m)
    return in_maps


_NC = {}


def kernel(**inputs):
    in_maps = host_prep(inputs)
    if "nc" not in _NC:
        _NC["nc"] = build()
    res = run_bass_kernel_spmd(_NC["nc"], in_maps, core_ids=list(range(8)))
    out = np.stack([np.ascontiguousarray(res.results[b]["outT"].T) for b in range(8)], 0)
    return out.astype(np.float32)
```

```python
from contextlib import ExitStack
import numpy as np
import concourse.bass as bass
import concourse.mybir as mybir
from concourse.bass_utils import run_bass_kernel_spmd

F32 = mybir.dt.float32
BF16 = mybir.dt.bfloat16
AF = mybir.ActivationFunctionType
ALU = mybir.AluOpType
AX = mybir.AxisListType

S = 2048
D = 1024
NL = 2
LV = 224
NV = NL * LV + 8
NEG = -30000.0


class Res:
    __slots__ = ("w", "r")

    def __init__(self):
        self.w = None
        self.r = []


class FW:
    ENG = ("pe", "act", "dve", "pool", "sp")

    def __init__(self, nc, es):
        self.nc = nc
        self.streams = {n: [] for n in self.ENG}
        self.sems = {}
        self.cnt = {}
        self.waited = {n: {} for n in self.ENG}
        self.es = es
        for n in self.ENG:
            self.newsem(n)

    def newsem(self, key):
        self.sems[key] = self.es.enter_context(self.nc.semaphore("s_" + key))
        self.cnt[key] = 0

    def wait(self, eng, tok):
        key, val = tok
        if self.waited[eng].get(key, 0) >= val:
            return
        self.waited[eng][key] = val
        sem = self.sems[key]
        self.streams[eng].append(lambda e, sem=sem, val=val: e.wait_ge(sem, val))

    def _deps(self, eng, reads, writes, extra):
        deps = set()
        for r in reads:
            if r.w is not None:
                deps.add(r.w)
        for w in writes:
            if w.w is not None and w.w[0] != eng:
                deps.add(w.w)
            for t in w.r:
                if t[0] != eng:
                    deps.add(t)
        deps.update(extra)
        for t in sorted(deps):
            if eng == "pe" and t[0] == "pe":
                continue
            self.wait(eng, t)

    def _commit(self, tok, reads, writes):
        for r in reads:
            r.r.append(tok)
        for w in writes:
            w.w = tok
            w.r = []

    def op(self, eng, fns, reads=(), writes=(), extra=()):
        if not isinstance(fns, (list, tuple)):
            fns = [fns]
        self._deps(eng, reads, writes, extra)
        self.cnt[eng] += 1
        tok = (eng, self.cnt[eng])
        sem = self.sems[eng]
        st = self.streams[eng]
        for f in fns[:-1]:
            st.append(f)
        last = fns[-1]
        st.append(lambda e, last=last, sem=sem: last(e).then_inc(sem, 1))
        self._commit(tok, reads, writes)
        return tok

    def dma(self, eng, key, out, in_, reads=(), writes=(), extra=()):
        if key not in self.sems:
            self.newsem(key)
        self._deps(eng, reads, writes, extra)
        self.cnt[key] += 16
        tok = (key, self.cnt[key])
        sem = self.sems[key]
        self.streams[eng].append(lambda e, out=out, in_=in_, sem=sem: e.dma_start(out=out, in_=in_).then_inc(sem, 16))
        self._commit(tok, reads, writes)
        return tok

    def barrier(self):
        for e in self.ENG:
            for k, v in self.cnt.items():
                if k != e and v > 0:
                    self.wait(e, (k, v))

    def replay(self):
        with self.nc.Block() as block:
            @block.tensor
            def _(e):
                for f in self.streams["pe"]:
                    f(e)

            @block.scalar
            def _(e):
                for f in self.streams["act"]:
                    f(e)

            @block.vector
            def _(e):
                for f in self.streams["dve"]:
                    f(e)

            @block.gpsimd
            def _(e):
                for f in self.streams["pool"]:
                    f(e)

            @block.sync
            def _(e):
                for f in self.streams["sp"]:
                    f(e)


def build(dbg=None, nlayers=NL, branches=(0, 1, 2), do_ffn=True):
    nc = bass.Bass("TRN2", target_bir_lowering=False)
    es = ExitStack()
    fw = FW(nc, es)

    def dram(name, shape, kind="ExternalInput"):
        return nc.dram_tensor(name, shape, F32, kind=kind).ap()

    xT_d = dram("xT", [D, S])
    WTOT = NL * 159744
    wpack_d = dram("wpack", [128, WTOT])
    plan = []
    woff = [0]
    vecs_d = dram("vecs", [128, NV])
    cos_d = dram("cosT", [128, S])
    sin_d = dram("sinT", [128, S])
    cst_d = dram("cst", [128, 2048])
    out_d = dram("outT", [D, S], kind="ExternalOutput")

    def sb(name, shape, dt):
        return es.enter_context(nc.sbuf_tensor(name, shape, dt))

    xT = sb("xT_s", [128, 8, S], F32)
    hT = sb("hT_s", [128, 8, S], BF16)
    oT = sb("oT_s", [128, 4, S], BF16)
    vecs = sb("vecs_s", [128, NV], F32)
    esink = sb("esink", [128, 8 * NL], F32)
    cst = sb("cst_s", [128, 2048], BF16)
    arena = sb("arena", [128, 14 * 1024], F32)
    wring = [sb(f"wring{i}", [128, 3072], BF16) for i in range(3)]
    ropeb = [sb(f"rope{i}", [128, 2, 512], F32) for i in range(2)]
    banks = [es.enter_context(nc.psum_tensor(f"bank{i}", [128, 512], F32)) for i in range(8)]
    bres = [Res() for _ in range(8)]

    ident = cst[:, 0:128]
    tincl = cst[:, 128:256]
    tcomp = cst[:, 256:384]
    m_le = cst[:, 384:512]
    m_lt = cst[:, 512:640]
    m_gt = cst[:, 640:768]
    m_legt = cst[:, 768:1024]
    onesm = cst[:, 1024:1152]

    r_x = [[Res() for _ in range(4)] for _ in range(8)]
    r_h = [Res() for _ in range(4)]
    r_oT = [[Res() for _ in range(4)] for _ in range(4)]
    r_vecs = Res()
    r_cst = Res()
    r_esink = Res()
    r_wring = [Res() for _ in range(3)]
    r_rope = [Res() for _ in range(2)]
    wr_i = [0]
    last_w = [None]
    rp_i = [0]

    fw.dma("sp", "d_vecs", vecs[:, :], vecs_d[:, :], writes=[r_vecs])
    fw.dma("pool", "d_cst", cst[:, :], cst_d[:, :], writes=[r_cst])
    for c in range(8):
        for tb in range(4):
            fw.dma("sp", "d_x", xT[:, c, tb * 512:(tb + 1) * 512], xT_d[c * 128:(c + 1) * 128, tb * 512:(tb + 1) * 512],
                   writes=[r_x[c][tb]])
    for c in range(8):
        for tb in range(4):
            r_x[c][tb].w = ("d_x", fw.cnt["d_x"])
    for l in range(NL):
        fw.op("act", lambda e, l=l: e.activation(out=esink[:, l * 8:(l + 1) * 8], in_=vecs[:, l * LV + 216:l * LV + 224], func=AF.Exp),
              reads=[r_vecs], writes=[r_esink])

    def vcol(l, off, n=1):
        return vecs[:, l * LV + off:l * LV + off + n]

    def load_w(pieces):
        i = wr_i[0] % 3
        wr_i[0] += 1
        slot, res = wring[i], r_wring[i]
        views = []
        off = 0
        for (name, l, sub, (r0, r1), (c0, c1)) in pieces:
            kc = (r1 - r0) // 128
            n = c1 - c0
            plan.append((name, l, sub, r0, r1, c0, c1, woff[0] + off))
            views.append(slot[:, off:off + kc * n].rearrange("p (c n) -> p c n", c=kc))
            off += kc * n
        assert off <= 3072
        last_w[0] = fw.dma("pool", f"d_w{i}", slot[:, 0:off], wpack_d[:, woff[0]:woff[0] + off], writes=[res],
                           extra=([last_w[0]] if last_w[0] is not None else []))
        woff[0] += off
        return views, res

    bank_rr = [0]

    def next_bank(pool):
        b = pool[bank_rr[0] % len(pool)]
        bank_rr[0] += 1
        return b

    def dense(bk, cols, wv, wres, nkc, rhs_fn, rhs_res, wcol0=0, m=128):
        fns = []
        for kc in range(nkc):
            fns.append(lambda e, kc=kc: e.matmul(banks[bk][0:m, cols], lhsT=wv[:, kc, wcol0:wcol0 + m], rhs=rhs_fn(kc),
                                                 start=(kc == 0), stop=(kc == nkc - 1)))
        return fw.op("pe", fns, reads=[wres] + list(rhs_res), writes=[bres[bk]])

    def rmsnorm_to_hT(l, goff):
        sq = arena[:, 0:2048].bitcast(BF16).rearrange("p (c n) -> p c n", c=8)
        rsd = arena[:, 2048:2048 + 1024].rearrange("p (i n) -> p i n", i=2)
        r_sq = Res()
        r_rs = [Res(), Res()]
        for tb in range(4):
            cs = slice(tb * 512, (tb + 1) * 512)
            fw.op("act", lambda e, cs=cs: e.activation(out=sq, in_=xT[:, :, cs], func=AF.Square),
                  reads=[r_x[c][tb] for c in range(8)], writes=[r_sq])
            bk = next_bank(list(range(8)))
            fns = [lambda e, c=c, bk=bk: e.matmul(banks[bk][:, :], lhsT=onesm, rhs=sq[:, c, :], start=(c == 0), stop=(c == 7))
                   for c in range(8)]
            fw.op("pe", fns, reads=[r_sq, r_cst], writes=[bres[bk]])
            rs = rsd[:, tb % 2, :]
            fw.op("act", lambda e, bk=bk, rs=rs: e.activation(out=rs, in_=banks[bk][:, :], func=AF.Ln, bias=1e-6),
                  reads=[bres[bk]], writes=[r_rs[tb % 2]])
            fw.op("act", lambda e, rs=rs: e.activation(out=rs, in_=rs, func=AF.Exp, scale=-0.5),
                  reads=[r_rs[tb % 2]], writes=[r_rs[tb % 2]])
            fns = [lambda e, c=c, cs=cs, rs=rs: e.scalar_tensor_tensor(out=hT[:, c, cs], in0=xT[:, c, cs], scalar=vcol(l, goff + c),
                                                                       in1=rs, op0=ALU.mult, op1=ALU.mult) for c in range(8)]
            fw.op("dve", fns, reads=[r_x[c][tb] for c in range(8)] + [r_rs[tb % 2], r_vecs], writes=[r_h[tb]])

    def load_rope(tb):
        i = rp_i[0] % 2
        rp_i[0] += 1
        cs = slice(tb * 512, (tb + 1) * 512)
        fw.dma("sp", f"d_rp{i}", ropeb[i][:, 0, :], cos_d[:, cs], writes=[r_rope[i]])
        fw.dma("sp", f"d_rp{i}", ropeb[i][:, 1, :], sin_d[:, cs], writes=[r_rope[i]])
        return ropeb[i], r_rope[i]

    def rope_evac(bk, dst, dst_res, tb, t1, t2, r_t):
        rb, rres = load_rope(tb)
        fw.op("dve", lambda e: e.tensor_tensor(out=t1, in0=banks[bk][:, :], in1=rb[:, 0, :], op=ALU.mult),
              reads=[bres[bk], rres], writes=[r_t[0]])
        fns = []
        for g in range(4):
            pg = g ^ 1
            fns.append(lambda e, g=g, pg=pg: e.tensor_tensor(out=t2[g * 32:(g + 1) * 32, :], in0=banks[bk][pg * 32:(pg + 1) * 32, :],
                                                             in1=rb[g * 32:(g + 1) * 32, 1, :], op=ALU.mult))
        fw.op("dve", fns, reads=[bres[bk], rres], writes=[r_t[1]])
        fw.op("dve", lambda e: e.tensor_tensor(out=dst, in0=t1, in1=t2, op=ALU.add), reads=[r_t[0], r_t[1]], writes=[dst_res])

    def mixer_branch(l, br):
        A = arena
        kT = A[:, 0:1024].bitcast(BF16)
        qT = A[:, 1024:2048].bitcast(BF16)
        vaug = A[:, 2048:2048 + 1040].bitcast(BF16).rearrange("p (t h d) -> p t h d", t=16, h=2)
        off = 2048 + 1040
        t1 = A[:, off:off + 512]
        t2 = A[:, off + 512:off + 1024]
        off += 1024
        ebuf = [[A[:, off + (2 * s + h) * 512: off + (2 * s + h + 1) * 512] for h in range(2)] for s in range(3)]
        off += 3072
        exbuf = [[A[:, off + (2 * s + h) * 512: off + (2 * s + h + 1) * 512] for h in range(2)] for s in range(2)]
        off += 2048
        spb = [[A[:, off + (2 * s + h) * 256: off + (2 * s + h + 1) * 256].bitcast(BF16) for h in range(2)] for s in range(3)]
        off += 1536
        pb = [[A[:, off + (2 * s + h) * 256: off + (2 * s + h + 1) * 256].bitcast(BF16) for h in range(2)] for s in range(2)]
        off += 1024
        otok = A[:, off:off + 256].bitcast(BF16)
        off += 256
        mbt = A[:, off:off + 256].bitcast(BF16)
        off += 256
        mb = A[:, off:off + 256].bitcast(BF16)
        off += 256
        small = A[:, off:off + 256]
        off += 256
        assert off <= 14 * 1024
        gm = small[:, 0:64].rearrange("p (h i n) -> p h i n", h=2, i=4)
        mx = small[:, 64:72]
        selt = small[:, 72:80]
        den = small[:, 80:88].rearrange("p (h i) -> p h i", h=2)
        ksum = small[:, 96:104]
        kmf = small[:, 104:112]
        kmh = small[:, 112:116].bitcast(BF16)
        kml = small[:, 116:120].bitcast(BF16)

        r_kT = [Res() for _ in range(4)]
        r_qT = [Res() for _ in range(4)]
        r_v = [Res() for _ in range(4)]
        r_t = [Res(), Res()]
        r_e = [[Res(), Res()], [Res(), Res()], [Res(), Res()]]
        r_ex = [[Res(), Res()], [Res(), Res()]]
        r_sp = [[Res(), Res()], [Res(), Res()], [Res(), Res()]]
        r_p = [[Res(), Res()], [Res(), Res()]]
        r_otok, r_mbt, r_mb, r_small, r_km = Res(), Res(), Res(), Res(), Res()

        if br == 0:
            units = [0]
        else:
            units = [0, 1, 2, 3]
        base = {0: 0, 1: 768, 2: 2304}[br]
        has_den = br != 1
        vw = 65 if has_den else 64

        if has_den:
            fw.op("dve", lambda e: e.memset(vaug[:, :, :, 64:65], 1.0), writes=r_v)
        if br == 2:
            fw.op("dve", lambda e: e.memset(mbt, 0.0), writes=[r_mbt])

        def proj_kv(wv, wres, kcol, vcol0):
            for tb in range(4):
                bk = next_bank([0, 1, 2, 3, 7])
                cs = slice(tb * 512, (tb + 1) * 512)
                dense(bk, slice(0, 512), wv, wres, 8, lambda kc, cs=cs: hT[:, kc, cs], [r_h[tb]], wcol0=kcol)
                if br == 1:
                    fw.op("act", lambda e, bk=bk, cs=cs: e.activation(out=kT[:, cs], in_=banks[bk][:, :], func=AF.Copy),
                          reads=[bres[bk]], writes=[r_kT[tb]])
                else:
                    rope_evac(bk, kT[:, cs], r_kT[tb], tb, t1, t2, r_t)
            for g4 in range(4):
                bk = next_bank([0, 1, 2, 3, 7])
                fns = []
                for j in range(4):
                    tt = g4 * 4 + j
                    for kc in range(8):
                        fns.append(lambda e, j=j, tt=tt, kc=kc, bk=bk: e.matmul(
                            banks[bk][:, j * 128:(j + 1) * 128], lhsT=hT[:, kc, tt * 128:(tt + 1) * 128],
                            rhs=wv[:, kc, vcol0:vcol0 + 128], start=(kc == 0), stop=(kc == 7)))
                fw.op("pe", fns, reads=[wres, r_h[g4]], writes=[bres[bk]])
                fw.op("act", lambda e, bk=bk, g4=g4: e.activation(
                    out=vaug[:, g4 * 4:(g4 + 1) * 4, :, 0:64],
                    in_=banks[bk][:, :].rearrange("p (t h d) -> p t h d", t=4, h=2), func=AF.Copy),
                    reads=[bres[bk]], writes=[r_v[g4]])

        def proj_q(wv, wres, qcol, Q):
            bk = next_bank([7])
            cs = slice(Q * 512, (Q + 1) * 512)
            dense(bk, slice(0, 512), wv, wres, 8, lambda kc: hT[:, kc, cs], [r_h[Q]], wcol0=qcol)
            if br == 1:
                fw.op("act", lambda e: e.activation(out=qT[:, cs], in_=banks[bk][:, :], func=AF.Copy),
                      reads=[bres[bk]], writes=[r_qT[Q]])
            else:
                rope_evac(bk, qT[:, cs], r_qT[Q], Q, t1, t2, r_t)

        def finish_o(obanks, Q, chunk, sink_cols):
            ot4 = otok.rearrange("p (i f) -> p i f", i=4)
            if has_den:
                for hh in range(2):
                    o3 = banks[obanks[hh]][:, 0:260].rearrange("p (i d) -> p i d", i=4)
                    if sink_cols is not None:
                        fw.op("dve", lambda e, hh=hh, o3=o3: e.tensor_scalar(out=den[:, hh, :], in0=o3[:, :, 64], scalar1=esink[:, sink_cols[hh]:sink_cols[hh] + 1],
                                                                            scalar2=None, op0=ALU.add),
                              reads=[bres[obanks[hh]], r_esink], writes=[r_small])
                    else:
                        fw.op("dve", lambda e, hh=hh, o3=o3: e.tensor_copy(out=den[:, hh, :], in_=o3[:, :, 64]),
                              reads=[bres[obanks[hh]]], writes=[r_small])
                    fw.op("dve", lambda e, hh=hh: e.reciprocal(out=den[:, hh, :], in_=den[:, hh, :]), reads=[r_small], writes=[r_small])
                    fw.op("dve", lambda e, hh=hh, o3=o3: e.tensor_tensor(out=ot4[:, :, hh * 64:(hh + 1) * 64], in0=o3[:, :, 0:64],
                                                                        in1=den[:, hh, :].unsqueeze(2).to_broadcast([128, 4, 64]), op=ALU.mult),
                          reads=[bres[obanks[hh]], r_small], writes=[r_otok])
            else:
                fw.op("act", lambda e: e.activation(out=otok, in_=banks[obanks[0]][:, :], func=AF.Copy),
                      reads=[bres[obanks[0]]], writes=[r_otok])
            tb7 = banks[7][:, 0:256].bitcast(BF16)
            fns = [lambda e, i=i: e.transpose(tb7[:, i * 128:(i + 1) * 128], ot4[:, i, :], ident) for i in range(4)]
            fw.op("pe", fns, reads=[r_otok, r_cst], writes=[bres[7]])
            fw.op("dve", lambda e: e.tensor_copy(out=oT[:, chunk, Q * 512:(Q + 1) * 512], in_=tb7), reads=[bres[7]], writes=[r_oT[chunk][Q]])

        if br == 0:
            (wkv,), wres = load_w([("w_in", l, None, (0, 1024), (512, 768))])
            proj_kv(wkv, wres, 0, 128)
            for c in range(4):
                (wq,), wqres = load_w([("w_in", l, None, (0, 1024), (c * 128, (c + 1) * 128))])
                for Q in range(4):
                    proj_q(wq, wqres, 0, Q)
                    ob = [2 + (Q % 2), 4 + (Q % 2)]
                    first = [True, True]
                    steps = list(range(max(0, 4 * Q - 1), 4 * Q + 4))
                    for si, a in enumerate(steps):
                        s = si % 2
                        i0 = a - 4 * Q
                        if i0 < 0:
                            qt, msk = [0], m_gt
                        elif i0 == 3:
                            qt, msk = [3], m_le
                        else:
                            qt, msk = [i0, i0 + 1], m_legt
                        n = 128 * len(qt)
                        qc = slice(Q * 512 + qt[0] * 128, Q * 512 + qt[0] * 128 + n)
                        for hh in range(2):
                            rows = slice(hh * 64, (hh + 1) * 64)
                            sres = bres[hh]
                            sc = slice(s * 256, s * 256 + n)
                            fw.op("pe", lambda e, hh=hh, rows=rows, sc=sc, qc=qc, a=a: e.matmul(
                                banks[hh][:, sc], lhsT=kT[rows, a * 128:(a + 1) * 128], rhs=qT[rows, qc], start=True, stop=True),
                                reads=[r_kT[a // 4], r_qT[Q]], writes=[sres])
                            P = pb[s][hh][:, 0:n]
                            fw.op("act", lambda e, hh=hh, sc=sc, P=P: e.activation(out=P, in_=banks[hh][:, sc], func=AF.Exp, scale=0.125),
                                  reads=[sres], writes=[r_p[s][hh]])
                            fw.op("dve", lambda e, P=P, msk=msk, n=n: e.tensor_tensor(out=P, in0=P, in1=msk[:, 0:n], op=ALU.mult),
                                  reads=[r_p[s][hh], r_cst], writes=[r_p[s][hh]])
                            fns = []
                            for j, i in enumerate(qt):
                                fns.append(lambda e, hh=hh, j=j, i=i, P=P, a=a, st=first[hh] and j == 0, ob=ob: e.matmul(
                                    banks[ob[hh]][:, i * 65:(i + 1) * 65], lhsT=P[:, j * 128:(j + 1) * 128], rhs=vaug[:, a, hh, :],
                                    start=st, stop=False, skip_group_check=True))
                            first[hh] = False
                            fw.op("pe", fns, reads=[r_p[s][hh], r_v[a // 4]], writes=[bres[ob[hh]]])
                    finish_o(ob, Q, c, [l * 8 + c, l * 8 + 4 + c])
            return

        for u in units:
            qc0 = base + u * 128
            kc0 = base + 512 + u * 128
            vc0 = base + 1024 + u * 128
            (wq, wk, wvv), wres = load_w([("w_in", l, None, (0, 1024), (qc0, qc0 + 128)), ("w_in", l, None, (0, 1024), (kc0, kc0 + 128)), ("w_in", l, None, (0, 1024), (vc0, vc0 + 128))])
            for tb in range(4):
                bk = next_bank([0, 1, 2, 3, 7])
                cs = slice(tb * 512, (tb + 1) * 512)
                dense(bk, slice(0, 512), wk, wres, 8, lambda kc, cs=cs: hT[:, kc, cs], [r_h[tb]])
                if br == 1:
                    fw.op("act", lambda e, bk=bk, cs=cs: e.activation(out=kT[:, cs], in_=banks[bk][:, :], func=AF.Copy),
                          reads=[bres[bk]], writes=[r_kT[tb]])
                else:
                    rope_evac(bk, kT[:, cs], r_kT[tb], tb, t1, t2, r_t)
            for g4 in range(4):
                bk = next_bank([0, 1, 2, 3, 7])
                fns = []
                for j in range(4):
                    tt = g4 * 4 + j
                    for kc in range(8):
                        fns.append(lambda e, j=j, tt=tt, kc=kc, bk=bk, wvv=wvv: e.matmul(
                            banks[bk][:, j * 128:(j + 1) * 128], lhsT=hT[:, kc, tt * 128:(tt + 1) * 128],
                            rhs=wvv[:, kc, :], start=(kc == 0), stop=(kc == 7)))
                fw.op("pe", fns, reads=[wres, r_h[g4]], writes=[bres[bk]])
                fw.op("act", lambda e, bk=bk, g4=g4: e.activation(
                    out=vaug[:, g4 * 4:(g4 + 1) * 4, :, 0:64],
                    in_=banks[bk][:, :].rearrange("p (t h d) -> p t h d", t=4, h=2), func=AF.Copy),
                    reads=[bres[bk]], writes=[r_v[g4]])
            if br == 2:
                fw.op("dve", lambda e: e.tensor_reduce(out=ksum, in_=kT.rearrange("p (n k) -> p n k", n=8), axis=AX.X, op=ALU.add),
                      reads=r_kT, writes=[r_small])
                fw.op("dve", lambda e: e.tensor_scalar(out=kmf, in0=ksum, scalar1=1.0 / 256, scalar2=None, op0=ALU.mult),
                      reads=[r_small], writes=[r_small])
                fw.op("dve", lambda e: e.tensor_copy(out=kmh, in_=kmf), reads=[r_small], writes=[r_km])
                fw.op("dve", lambda e: e.tensor_tensor(out=kml, in0=kmf, in1=kmh, op=ALU.subtract), reads=[r_small, r_km], writes=[r_km])

            for Q in range(4):
                proj_q(wq, wres, 0, Q)
                qcs = slice(Q * 512, (Q + 1) * 512)
                need_sel = (br == 2 and Q >= 2)
                if need_sel:
                    gb = [5, 6]
                    gb = [6, 7]
                    for hh in range(2):
                        rows = slice(hh * 64, (hh + 1) * 64)
                        fns = []
                        for i in range(4):
                            qi = slice(Q * 512 + i * 128, Q * 512 + (i + 1) * 128)
                            fns.append(lambda e, hh=hh, i=i, qi=qi, rows=rows: e.matmul(banks[gb[hh]][:, i * 8:(i + 1) * 8], lhsT=qT[rows, qi], rhs=kmh[rows, :], start=True, stop=False))
                            fns.append(lambda e, hh=hh, i=i, qi=qi, rows=rows: e.matmul(banks[gb[hh]][:, i * 8:(i + 1) * 8], lhsT=qT[rows, qi], rhs=kml[rows, :], start=False, stop=True))
                        fw.op("pe", fns, reads=[r_qT[Q], r_km], writes=[bres[gb[hh]]])
                    fw.op("dve", lambda e: e.memset(gm, -1e30), writes=[r_small])
                    for hh in range(2):
                        g3 = banks[gb[hh]][:, 0:32].rearrange("p (i n) -> p i n", i=4)
                        fw.op("dve", [lambda e, hh=hh, g3=g3, Q=Q: e.tensor_copy(out=gm[:, hh, 0:2, 0:2 * Q], in_=g3[:, 0:2, 0:2 * Q]),
                                      lambda e, hh=hh, g3=g3, Q=Q: e.tensor_copy(out=gm[:, hh, 2:4, 0:2 * Q + 1], in_=g3[:, 2:4, 0:2 * Q + 1])],
                              reads=[bres[gb[hh]]], writes=[r_small])
                    mbt4 = mbt.rearrange("p (i f) -> p i f", i=4)
                    for hh in range(2):
                        for i in range(4):
                            nb = 2 * Q + i // 2
                            fw.op("dve", lambda e, hh=hh, i=i: e.max(out=mx, in_=gm[:, hh, i, :]), reads=[r_small], writes=[r_small])
                            fw.op("dve", lambda e, hh=hh, i=i: e.tensor_scalar(out=selt, in0=gm[:, hh, i, :], scalar1=mx[:, 2:3], scalar2=None, op0=ALU.is_ge),
                                  reads=[r_small], writes=[r_small])
                            fw.op("dve", [lambda e, hh=hh, i=i: e.tensor_scalar(out=mbt4[:, i, hh * 64:hh * 64 + 8], in0=selt, scalar1=-1.0, scalar2=-NEG,
                                                                                op0=ALU.add, op1=ALU.mult),
                                          lambda e, hh=hh, i=i, nb=nb: e.memset(mbt4[:, i, hh * 64 + nb:hh * 64 + nb + 1], 0.0)],
                                  reads=[r_small], writes=[r_mbt])
                    tb7 = banks[7][:, 0:256].bitcast(BF16)
                    fns = [lambda e, i=i: e.transpose(tb7[:, i * 128:(i + 1) * 128], mbt4[:, i, :], ident) for i in range(4)]
                    fw.op("pe", fns, reads=[r_mbt, r_cst], writes=[bres[7]])
                    fw.op("dve", lambda e: e.tensor_copy(out=mb, in_=tb7), reads=[bres[7]], writes=[r_mb])

                steps = list(range(4 * Q + 3, -1, -1))
                ns = len(steps)
                if br == 1:
                    ob = [6, 6]
                    xb = [4, 5]
                else:
                    ob = [4, 5]

                def cols_of(a):
                    i0 = max(0, a - 4 * Q)
                    return i0, slice(i0 * 128, 512)

                def stage_qk(t):
                    a = steps[t]
                    s = t % 2
                    i0, cl = cols_of(a)
                    qcl = slice(Q * 512 + i0 * 128, (Q + 1) * 512)
                    for hh in range(2):
                        rows = slice(hh * 64, (hh + 1) * 64)
                        bk = 2 * hh + s
                        msk = need_sel and a < 4 * Q + 2
                        fns = [lambda e, a=a, rows=rows, bk=bk, cl=cl, qcl=qcl, msk=msk: e.matmul(
                            banks[bk][:, cl], lhsT=kT[rows, a * 128:(a + 1) * 128], rhs=qT[rows, qcl], start=True, stop=not msk)]
                        rd = [r_kT[a // 4], r_qT[Q]]
                        if msk:
                            n = a // 2
                            fns.append(lambda e, rows=rows, bk=bk, cl=cl, n=n: e.matmul(
                                banks[bk][:, cl], lhsT=cst[rows, 1152 + n * 128:1152 + (n + 1) * 128], rhs=mb[rows, cl], start=False, stop=True))
                            rd += [r_mb, r_cst]
                        fw.op("pe", fns, reads=rd, writes=[bres[bk]])
                        if br == 1:
                            s3 = t % 3
                            E = ebuf[s3][hh]
                            SP = spb[s3][hh]
                            fw.op("act", lambda e, bk=bk, cl=cl, E=E: e.activation(out=E[:, cl], in_=banks[bk][:, cl], func=AF.Exp, scale=0.125),
                                  reads=[bres[bk]], writes=[r_e[s3][hh]])
                            fw.op("act", lambda e, cl=cl, E=E, SP=SP: e.activation(out=SP[:, cl], in_=E[:, cl], func=AF.Ln, bias=1.0),
                                  reads=[r_e[s3][hh]], writes=[r_sp[s3][hh]])
                            if a >= 4 * Q:
                                dc = slice(i0 * 128, (i0 + 1) * 128)
                                fw.op("dve", lambda e, SP=SP, dc=dc: e.tensor_tensor(out=SP[:, dc], in0=SP[:, dc], in1=m_lt, op=ALU.mult),
                                      reads=[r_sp[s3][hh], r_cst], writes=[r_sp[s3][hh]])
                        else:
                            P = pb[s][hh]
                            fw.op("act", lambda e, bk=bk, cl=cl, P=P: e.activation(out=P[:, cl], in_=banks[bk][:, cl], func=AF.Exp, scale=0.125),
                                  reads=[bres[bk]], writes=[r_p[s][hh]])
                            if a >= 4 * Q:
                                dc = slice(i0 * 128, (i0 + 1) * 128)
                                fw.op("dve", lambda e, P=P, dc=dc: e.tensor_tensor(out=P[:, dc], in0=P[:, dc], in1=m_le, op=ALU.mult),
                                      reads=[r_p[s][hh], r_cst], writes=[r_p[s][hh]])

                def stage_xa(t):
                    a = steps[t]
                    s = t % 2
                    s3 = t % 3
                    i0, cl = cols_of(a)
                    last = (t == ns - 1)
                    for hh in range(2):
                        SP = spb[s3][hh]
                        fw.op("pe", lambda e, hh=hh, cl=cl, SP=SP, t=t, last=last, xb=xb: e.matmul(
                            banks[xb[hh]][:, cl], lhsT=tincl, rhs=SP[:, cl], start=(t == 0), stop=last, skip_group_check=True),
                            reads=[r_sp[s3][hh], r_cst], writes=[bres[xb[hh]]])
                    for hh in range(2):
                        EX = exbuf[s][hh]
                        fw.op("act", lambda e, hh=hh, cl=cl, EX=EX, xb=xb: e.activation(out=EX[:, cl], in_=banks[xb[hh]][:, cl], func=AF.Exp),
                              reads=[bres[xb[hh]]], writes=[r_ex[s][hh]])

                def stage_xb(t):
                    a = steps[t]
                    s = t % 2
                    s3 = t % 3
                    i0, cl = cols_of(a)
                    last = (t == ns - 1)
                    if not last:
                        for hh in range(2):
                            SP = spb[s3][hh]
                            fw.op("pe", lambda e, hh=hh, cl=cl, SP=SP, xb=xb: e.matmul(
                                banks[xb[hh]][:, cl], lhsT=tcomp, rhs=SP[:, cl], start=False, stop=False, skip_group_check=True),
                                reads=[r_sp[s3][hh], r_cst], writes=[bres[xb[hh]]])
                    for hh in range(2):
                        E, EX, W = ebuf[s3][hh], exbuf[s][hh], pb[s][hh]
                        fw.op("dve", lambda e, cl=cl, E=E, EX=EX, W=W: e.tensor_tensor(out=W[:, cl], in0=E[:, cl], in1=EX[:, cl], op=ALU.mult),
                              reads=[r_e[s3][hh], r_ex[s][hh]], writes=[r_p[s][hh]])
                        if a >= 4 * Q:
                            dc = slice(i0 * 128, (i0 + 1) * 128)
                            fw.op("dve", lambda e, W=W, dc=dc: e.tensor_tensor(out=W[:, dc], in0=W[:, dc], in1=m_lt, op=ALU.mult),
                                  reads=[r_p[s][hh], r_cst], writes=[r_p[s][hh]])

                def stage_pv(t):
                    a = steps[t]
                    s = t % 2
                    i0, cl = cols_of(a)
                    for hh in range(2):
                        P = pb[s][hh]
                        fns = []
                        for i in range(i0, 4):
                            if br == 1:
                                oc = slice(i * 128 + hh * 64, i * 128 + hh * 64 + 64)
                                st = (t == 0 and hh == 0 and i == i0)
                                rhs = vaug[:, a, hh, 0:64]
                            else:
                                oc = slice(i * 65, (i + 1) * 65)
                                st = (t == 0 and i == i0)
                                rhs = vaug[:, a, hh, :]
                            fns.append(lambda e, hh=hh, i=i, oc=oc, st=st, rhs=rhs, P=P, ob=ob: e.matmul(
                                banks[ob[hh]][:, oc], lhsT=P[:, i * 128:(i + 1) * 128], rhs=rhs, start=st, stop=False, skip_group_check=True))
                        fw.op("pe", fns, reads=[r_p[s][hh], r_v[a // 4]], writes=[bres[ob[hh]]])

                if br == 1:
                    stage_qk(0)
                    if ns > 1:
                        stage_qk(1)
                    for t in range(ns):
                        stage_xa(t)
                        if t >= 1:
                            stage_pv(t - 1)
                        if t + 2 < ns:
                            stage_qk(t + 2)
                        stage_xb(t)
                    stage_pv(ns - 1)
                else:
                    stage_qk(0)
                    for t in range(ns):
                        if t + 1 < ns:
                            stage_qk(t + 1)
                        stage_pv(t)
                finish_o(ob, Q, u, None)

    def post_branch(l, br):
        mT = arena[:, 0:8192].bitcast(BF16).rearrange("p (c n) -> p c n", c=8)
        sg = [arena[:, 8192 + i * 512: 8192 + (i + 1) * 512] for i in range(2)]
        r_sg = [Res(), Res()]
        r_m = [[Res() for _ in range(4)] for _ in range(8)]
        k = 0
        for j in range(8):
            gcol = 3840 + br * 1024 + j * 128
            (wg, wb), wres = load_w([("w_in", l, None, (0, 1024), (gcol, gcol + 128)), ("w_br", l, br, (0, 512), (j * 128, (j + 1) * 128))])
            wres2 = wres
            for tb in range(4):
                cs = slice(tb * 512, (tb + 1) * 512)
                bg = next_bank(list(range(8)))
                dense(bg, slice(0, 512), wg, wres, 8, lambda kc, cs=cs: hT[:, kc, cs], [r_h[tb]])
                by = next_bank(list(range(8)))
                dense(by, slice(0, 512), wb, wres2, 4, lambda kc, cs=cs: oT[:, kc, cs], [r_oT[c][tb] for c in range(4)])
                s = k % 2
                k += 1
                fw.op("act", lambda e, bg=bg, s=s, j=j: e.activation(out=sg[s], in_=banks[bg][:, :], func=AF.Sigmoid, bias=vcol(l, 16 + br * 8 + j)),
                      reads=[bres[bg], r_vecs], writes=[r_sg[s]])
                fw.op("dve", lambda e, by=by, s=s, j=j, cs=cs: e.tensor_tensor(out=mT[:, j, cs], in0=banks[by][:, :], in1=sg[s], op=ALU.mult),
                      reads=[bres[by], r_sg[s]], writes=[r_m[j][tb]])
        for j in range(8):
            (wo,), wres = load_w([("w_out", l, None, (0, 1024), (j * 128, (j + 1) * 128))])
            for tb in range(4):
                cs = slice(tb * 512, (tb + 1) * 512)
                bk = next_bank(list(range(8)))
                dense(bk, slice(0, 512), wo, wres, 8, lambda kc, cs=cs: mT[:, kc, cs], [r_m[c][tb] for c in range(8)])
                fw.op("dve", lambda e, bk=bk, j=j, cs=cs: e.tensor_tensor(out=xT[:, j, cs], in0=banks[bk][:, :], in1=xT[:, j, cs], op=ALU.add),
                      reads=[bres[bk]], writes=[r_x[j][tb]])

    def ffn(l):
        actT = arena[:, 0:11 * 1024].bitcast(BF16).rearrange("p (c n) -> p c n", c=11)
        rb_ = [[oT[:, (2 * s + z) // 2, ((2 * s + z) % 2) * 1024:((2 * s + z) % 2 + 1) * 1024].bitcast(F32) for z in range(2)] for s in range(4)]
        r_rb = [[Res(), Res()] for _ in range(4)]
        tiles = [(0, 512), (512, 510), (1022, 510), (1532, 510), (2042, 6)]
        allh = list(r_h)
        k = [0]
        stages = []
        for half in range(2):
            for f in range(11):
                fc = half * 11 + f
                stages.append(("up", half, f, [("w_up", l, None, (0, 1024), (fc * 128, (fc + 1) * 128)),
                                                ("w_up", l, None, (0, 1024), (2816 + fc * 128, 2816 + (fc + 1) * 128))]))
            for j in range(8):
                stages.append(("dn", half, j, [("w_dn", l, None, (half * 1408, (half + 1) * 1408), (j * 128, (j + 1) * 128))]))
        loaded = [None] * len(stages)
        loaded[0] = load_w(stages[0][3])
        r_act = [Res() for _ in range(11)]
        pend = []

        def flush(keep):
            while len(pend) > keep:
                (s, f, t0, n, ra) = pend.pop(0)
                Ra, Rv = rb_[s][0], rb_[s][1]
                fw.op("act", lambda e, Ra=Ra, n=n: e.activation(out=Ra[:, 0:n], in_=Ra[:, 0:n], func=AF.Silu), reads=[r_rb[s][0]], writes=[r_rb[s][0]])
                fw.op("pool", lambda e, Ra=Ra, Rv=Rv, f=f, t0=t0, n=n: e.tensor_tensor(out=actT[:, f, t0:t0 + n], in0=Ra[:, 0:n], in1=Rv[:, 0:n], op=ALU.mult),
                      reads=[r_rb[s][0], r_rb[s][1]], writes=[ra[f]])

        for si, (kind, half, idx, _) in enumerate(stages):
            if si + 1 < len(stages):
                loaded[si + 1] = load_w(stages[si + 1][3])
            views, wres = loaded[si]
            if kind == "up":
                f = idx
                fc = half * 11 + f
                wa, wvv = views
                for (t0, n) in tiles:
                    s = k[0] % 4
                    k[0] += 1
                    info = []
                    for z, wz in enumerate((wa, wvv)):
                        ch = fc + z * 22
                        bk = next_bank(list(range(8)))
                        if t0 == 0:
                            cs = slice(0, 512)
                            N = 512
                        else:
                            cs = slice(t0 - 2, t0 + n)
                            N = n + 2
                        dense(bk, slice(0, N), wz, wres, 8, lambda kc, cs=cs: hT[:, kc, cs], allh)
                        info.append((z, ch, bk))
                    for (z, ch, bk) in info:
                        R = rb_[s][z]
                        src = banks[bk][:, 0:512] if t0 == 0 else banks[bk][:, 2:n + 2]
                        fw.op("act", lambda e, R=R, src=src, ch=ch, n=n: e.activation(out=R[:, 0:n], in_=src, func=AF.Identity,
                                                                                   scale=vcol(l, 40 + 2 * 44 + ch), bias=vcol(l, 172 + ch)),
                              reads=[bres[bk], r_vecs], writes=[r_rb[s][z]])
                    for tap, off in ((1, 1), (0, 2)):
                        for (z, ch, bk) in info:
                            R = rb_[s][z]
                            if t0 == 0:
                                dst = R[:, off:512]
                                src = banks[bk][:, 0:512 - off]
                            else:
                                dst = R[:, 0:n]
                                src = banks[bk][:, 2 - off:2 - off + n]
                            fw.op("dve", lambda e, dst=dst, src=src, ch=ch, tap=tap: e.scalar_tensor_tensor(
                                out=dst, in0=src, scalar=vcol(l, 40 + tap * 44 + ch), in1=dst, op0=ALU.mult, op1=ALU.add),
                                reads=[bres[bk], r_rb[s][z], r_vecs], writes=[r_rb[s][z]])
                    pend.append((s, f, t0, n, r_act))
                    flush(2)
            else:
                j = idx
                if j == 0:
                    flush(0)
                (wd,) = views
                for tb in range(4):
                    cs = slice(tb * 512, (tb + 1) * 512)
                    bk = next_bank(list(range(8)))
                    dense(bk, slice(0, 512), wd, wres, 11, lambda kc, cs=cs: actT[:, kc, cs], r_act)
                    fw.op("dve", lambda e, bk=bk, j=j, cs=cs: e.tensor_tensor(out=xT[:, j, cs], in0=banks[bk][:, :], in1=xT[:, j, cs], op=ALU.add),
                          reads=[bres[bk]], writes=[r_x[j][tb]])
        fw.barrier()

    for l in range(nlayers):
        rmsnorm_to_hT(l, 0)
        fw.barrier()
        for br in branches:
            mixer_branch(l, br)
            fw.barrier()
            if dbg == "oT":
                break
            post_branch(l, br)
            fw.barrier()
        if dbg == "oT":
            break
        if do_ffn:
            rmsnorm_to_hT(l, 8)
            fw.barrier()
            ffn(l)
            fw.barrier()

    r_out = Res()
    if dbg == "oT":
        for c in range(4):
            fw.dma("pool", "d_out", out_d[c * 128:(c + 1) * 128, :], oT[:, c, :], reads=[r_oT[c][tb] for tb in range(4)], writes=[r_out])
        fw.wait("sp", r_out.w)
    elif dbg == "x":
        for c in range(8):
            fw.dma("sp", "d_out", out_d[c * 128:(c + 1) * 128, :], xT[:, c, :], reads=[r_x[c][tb] for tb in range(4)], writes=[r_out])
    else:
        sq = arena[:, 0:2048].bitcast(BF16).rearrange("p (c n) -> p c n", c=8)
        rsd = arena[:, 2048:2048 + 512]
        ob = [arena[:, 4096 + i * 4096: 4096 + (i + 1) * 4096].rearrange("p (c n) -> p c n", c=8) for i in range(2)]
        r_sq, r_rs, r_ob = Res(), Res(), [Res(), Res()]
        goff = NL * LV
        for tb in range(4):
            cs = slice(tb * 512, (tb + 1) * 512)
            fw.op("act", lambda e, cs=cs: e.activation(out=sq, in_=xT[:, :, cs], func=AF.Square),
                  reads=[r_x[c][tb] for c in range(8)], writes=[r_sq])
            bk = next_bank(list(range(8)))
            fns = [lambda e, c=c, bk=bk: e.matmul(banks[bk][:, :], lhsT=onesm, rhs=sq[:, c, :], start=(c == 0), stop=(c == 7)) for c in range(8)]
            fw.op("pe", fns, reads=[r_sq, r_cst], writes=[bres[bk]])
            fw.op("act", lambda e, bk=bk: e.activation(out=rsd, in_=banks[bk][:, :], func=AF.Ln, bias=1e-6), reads=[bres[bk]], writes=[r_rs])
            fw.op("act", lambda e: e.activation(out=rsd, in_=rsd, func=AF.Exp, scale=-0.5), reads=[r_rs], writes=[r_rs])
            O = ob[tb % 2]
            fns = [lambda e, c=c, cs=cs, O=O: e.scalar_tensor_tensor(out=O[:, c, :], in0=xT[:, c, cs], scalar=vecs[:, goff + c:goff + c + 1],
                                                                    in1=rsd, op0=ALU.mult, op1=ALU.mult) for c in range(8)]
            fw.op("dve", fns, reads=[r_x[c][tb] for c in range(8)] + [r_rs, r_vecs], writes=[r_ob[tb % 2]])
            fw.dma("sp", "d_out", out_d.rearrange("(c p) n -> p c n", p=128)[:, :, cs], O, reads=[r_ob[tb % 2]], writes=[r_out])
    fw.wait("sp", r_out.w)
    fw.barrier()
    fw.replay()
    es.close()
    build.last_plan = plan
    return nc


def host_consts():
    cst = np.zeros((128, 2048), np.float32)
    j = np.arange(128)[:, None]
    s = np.arange(128)[None, :]
    cst[:, 0:128] = np.eye(128)
    cst[:, 128:256] = -1.0 * (j >= s)
    cst[:, 256:384] = -1.0 * (j < s)
    cst[:, 384:512] = (j <= s)
    cst[:, 512:640] = (j < s)
    cst[:, 640:768] = (j > s)
    cst[:, 768:896] = (j <= s)
    cst[:, 896:1024] = (j > s)
    cst[:, 1024:1152] = 1.0 / 1024
    for n in range(7):
        cst[n, 1152 + n * 128:1152 + (n + 1) * 128] = 1.0
        cst[64 + n, 1152 + n * 128:1152 + (n + 1) * 128] = 1.0
    inv = (10000.0 ** (-(np.arange(0, 64, 2, dtype=np.float32)) / np.float32(64))).astype(np.float32)
    ang = (np.arange(S, dtype=np.float32)[None, :] * inv[:, None]).astype(np.float32)
    cos, sin = np.cos(ang).astype(np.float32), np.sin(ang).astype(np.float32)
    cosT = np.tile(cos, (4, 1))
    sinT = np.concatenate([-sin, sin, -sin, sin], 0)
    return cst, np.ascontiguousarray(cosT), np.ascontiguousarray(sinT)


def host_prep(inputs, plan):
    f = lambda a: np.ascontiguousarray(np.asarray(a, dtype=np.float32))
    w_in = f(inputs["w_in"]).copy()
    perm = []
    for c in range(4):
        perm += list(range(c * 64, c * 64 + 64)) + list(range((4 + c) * 64, (4 + c) * 64 + 64))
    w_in[:, :, 0:512] = w_in[:, :, perm]
    w_br = f(inputs["w_branch"]).copy()
    w_br[:, 0] = w_br[:, 0][:, perm, :]
    vecs = np.zeros((128, NV), np.float32)
    pc = lambda v: np.asarray(v, np.float32).reshape(-1, 128).T
    for l in range(NL):
        b = l * LV
        vecs[:, b:b + 8] = pc(inputs["norm_mix"][l])
        vecs[:, b + 8:b + 16] = pc(inputs["norm_ffn"][l])
        vecs[:, b + 16:b + 40] = pc(inputs["b_gate"][l])
        cw = np.asarray(inputs["conv_w"][l], np.float32)
        for tap in range(3):
            vecs[:, b + 40 + tap * 44:b + 40 + (tap + 1) * 44] = pc(cw[tap])
        vecs[:, b + 172:b + 216] = pc(inputs["conv_b"][l])
        vecs[:, b + 216:b + 224] = np.asarray(inputs["sinks"][l], np.float32)[None, :]
    vecs[:, NL * LV:NL * LV + 8] = pc(inputs["norm_final"])
    cst, cosT, sinT = host_consts()
    srcs = {"w_in": w_in, "w_br": w_br, "w_out": f(inputs["w_out"]), "w_up": f(inputs["w_up"]), "w_dn": f(inputs["w_down"])}
    wpack = np.zeros((128, NL * 159744), np.float32)
    for (name, l, sub, r0, r1, c0, c1, off) in plan:
        A = srcs[name][l] if sub is None else srcs[name][l][sub]
        A = A[r0:r1, c0:c1]
        kc, n = (r1 - r0) // 128, c1 - c0
        wpack[:, off:off + kc * n] = A.reshape(kc, 128, n).transpose(1, 0, 2).reshape(128, kc * n)
    shared = {"wpack": wpack, "vecs": vecs, "cosT": cosT, "sinT": sinT, "cst": cst}
    x = np.asarray(inputs["x"], np.float32)
    in_maps = []
    for b in range(8):
        m = dict(shared)
        m["xT"] = np.ascontiguousarray(x[b].T)
        in_maps.append(m)
    return in_maps


_NC = {}


def kernel(**inputs):
    if "nc" not in _NC:
        _NC["nc"] = build()
        _NC["plan"] = build.last_plan
    nc = _NC["nc"]
    in_maps = host_prep(inputs, _NC["plan"])
    res = run_bass_kernel_spmd(nc, in_maps, core_ids=list(range(8)))
    out = np.stack([np.ascontiguousarray(res.results[b]["outT"].T) for b in range(8)], 0)
    return out.astype(np.float32)
```

```python
from contextlib import ExitStack
import numpy as np
import concourse.bass as bass
import concourse.mybir as mybir
from concourse.bass_utils import run_bass_kernel_spmd

F32 = mybir.dt.float32
BF16 = mybir.dt.bfloat16
AF = mybir.ActivationFunctionType
ALU = mybir.AluOpType
AX = mybir.AxisListType

S = 2048
D = 1024
NL = 2
LV = 224
NV = NL * LV + 8
NEG = -30000.0


class Res:
    __slots__ = ("w", "r")

    def __init__(self):
        self.w = None
        self.r = []


class FW:
    ENG = ("pe", "act", "dve", "pool", "sp")

    def __init__(self, nc, es):
        self.nc = nc
        self.streams = {n: [] for n in self.ENG}
        self.sems = {}
        self.cnt = {}
        self.waited = {n: {} for n in self.ENG}
        self.es = es
        for n in self.ENG:
            self.newsem(n)

    def newsem(self, key):
        self.sems[key] = self.es.enter_context(self.nc.semaphore("s_" + key))
        self.cnt[key] = 0

    def wait(self, eng, tok):
        key, val = tok
        if self.waited[eng].get(key, 0) >= val:
            return
        self.waited[eng][key] = val
        sem = self.sems[key]
        self.streams[eng].append(lambda e, sem=sem, val=val: e.wait_ge(sem, val))

    def _deps(self, eng, reads, writes, extra):
        deps = set()
        for r in reads:
            if r.w is not None:
                deps.add(r.w)
        for w in writes:
            if w.w is not None and w.w[0] != eng:
                deps.add(w.w)
            for t in w.r:
                if t[0] != eng:
                    deps.add(t)
        deps.update(extra)
        for t in sorted(deps):
            if eng == "pe" and t[0] == "pe":
                continue
            self.wait(eng, t)

    def _commit(self, tok, reads, writes):
        for r in reads:
            r.r.append(tok)
        for w in writes:
            w.w = tok
            w.r = []

    def op(self, eng, fns, reads=(), writes=(), extra=()):
        if not isinstance(fns, (list, tuple)):
            fns = [fns]
        self._deps(eng, reads, writes, extra)
        self.cnt[eng] += 1
        tok = (eng, self.cnt[eng])
        sem = self.sems[eng]
        st = self.streams[eng]
        for f in fns[:-1]:
            st.append(f)
        last = fns[-1]
        st.append(lambda e, last=last, sem=sem: last(e).then_inc(sem, 1))
        self._commit(tok, reads, writes)
        return tok

    def dma(self, eng, key, out, in_, reads=(), writes=(), extra=()):
        if key not in self.sems:
            self.newsem(key)
        self._deps(eng, reads, writes, extra)
        self.cnt[key] += 16
        tok = (key, self.cnt[key])
        sem = self.sems[key]
        self.streams[eng].append(lambda e, out=out, in_=in_, sem=sem: e.dma_start(out=out, in_=in_).then_inc(sem, 16))
        self._commit(tok, reads, writes)
        return tok

    def barrier(self):
        for e in self.ENG:
            for k, v in self.cnt.items():
                if k != e and v > 0:
                    self.wait(e, (k, v))

    def replay(self):
        with self.nc.Block() as block:
            @block.tensor
            def _(e):
                for f in self.streams["pe"]:
                    f(e)

            @block.scalar
            def _(e):
                for f in self.streams["act"]:
                    f(e)

            @block.vector
            def _(e):
                for f in self.streams["dve"]:
                    f(e)

            @block.gpsimd
            def _(e):
                for f in self.streams["pool"]:
                    f(e)

            @block.sync
            def _(e):
                for f in self.streams["sp"]:
                    f(e)


def build(dbg=None, nlayers=NL, branches=(0, 1, 2), do_ffn=True):
    nc = bass.Bass("TRN2", target_bir_lowering=False)
    es = ExitStack()
    fw = FW(nc, es)

    def dram(name, shape, kind="ExternalInput"):
        return nc.dram_tensor(name, shape, F32, kind=kind).ap()

    xT_d = dram("xT", [D, S])
    WTOT = NL * 159744
    wpack_d = dram("wpack", [128, WTOT])
    plan = []
    woff = [0]
    vecs_d = dram("vecs", [128, NV])
    cos_d = dram("cosT", [128, S])
    sin_d = dram("sinT", [128, S])
    cst_d = dram("cst", [128, 2048])
    out_d = dram("outT", [D, S], kind="ExternalOutput")

    def sb(name, shape, dt):
        return es.enter_context(nc.sbuf_tensor(name, shape, dt))

    xT = sb("xT_s", [128, 8, S], F32)
    hT = sb("hT_s", [128, 8, S], BF16)
    oT = sb("oT_s", [128, 4, S], BF16)
    vecs = sb("vecs_s", [128, NV], F32)
    esink = sb("esink", [128, 8 * NL], F32)
    cst = sb("cst_s", [128, 2048], BF16)
    arena = sb("arena", [128, 14 * 1024], F32)
    wring = [sb(f"wring{i}", [128, 3072], BF16) for i in range(3)]
    ropeb = [sb(f"rope{i}", [128, 2, 512], F32) for i in range(2)]
    banks = [es.enter_context(nc.psum_tensor(f"bank{i}", [128, 512], F32)) for i in range(8)]
    bres = [Res() for _ in range(8)]

    ident = cst[:, 0:128]
    tincl = cst[:, 128:256]
    tcomp = cst[:, 256:384]
    m_le = cst[:, 384:512]
    m_lt = cst[:, 512:640]
    m_gt = cst[:, 640:768]
    m_legt = cst[:, 768:1024]
    onesm = cst[:, 1024:1152]

    r_x = [[Res() for _ in range(4)] for _ in range(8)]
    r_h = [Res() for _ in range(4)]
    r_oT = [[Res() for _ in range(4)] for _ in range(4)]
    r_vecs = Res()
    r_cst = Res()
    r_esink = Res()
    r_wring = [Res() for _ in range(3)]
    r_rope = [Res() for _ in range(2)]
    wr_i = [0]
    last_w = [None]
    rp_i = [0]

    fw.dma("sp", "d_vecs", vecs[:, :], vecs_d[:, :], writes=[r_vecs])
    fw.dma("pool", "d_cst", cst[:, :], cst_d[:, :], writes=[r_cst])
    for c in range(8):
        for tb in range(4):
            fw.dma("sp", "d_x", xT[:, c, tb * 512:(tb + 1) * 512], xT_d[c * 128:(c + 1) * 128, tb * 512:(tb + 1) * 512],
                   writes=[r_x[c][tb]])
    for c in range(8):
        for tb in range(4):
            r_x[c][tb].w = ("d_x", fw.cnt["d_x"])
    for l in range(NL):
        fw.op("act", lambda e, l=l: e.activation(out=esink[:, l * 8:(l + 1) * 8], in_=vecs[:, l * LV + 216:l * LV + 224], func=AF.Exp),
              reads=[r_vecs], writes=[r_esink])

    def vcol(l, off, n=1):
        return vecs[:, l * LV + off:l * LV + off + n]

    def load_w(pieces):
        i = wr_i[0] % 3
        wr_i[0] += 1
        slot, res = wring[i], r_wring[i]
        views = []
        off = 0
        for (name, l, sub, (r0, r1), (c0, c1)) in pieces:
            kc = (r1 - r0) // 128
            n = c1 - c0
            plan.append((name, l, sub, r0, r1, c0, c1, woff[0] + off))
            views.append(slot[:, off:off + kc * n].rearrange("p (c n) -> p c n", c=kc))
            off += kc * n
        assert off <= 3072
        last_w[0] = fw.dma("pool", f"d_w{i}", slot[:, 0:off], wpack_d[:, woff[0]:woff[0] + off], writes=[res],
                           extra=([last_w[0]] if last_w[0] is not None else []))
        woff[0] += off
        return views, res

    bank_rr = [0]

    def next_bank(pool):
        b = pool[bank_rr[0] % len(pool)]
        bank_rr[0] += 1
        return b

    def dense(bk, cols, wv, wres, nkc, rhs_fn, rhs_res, wcol0=0, m=128):
        fns = []
        for kc in range(nkc):
            fns.append(lambda e, kc=kc: e.matmul(banks[bk][0:m, cols], lhsT=wv[:, kc, wcol0:wcol0 + m], rhs=rhs_fn(kc),
                                                 start=(kc == 0), stop=(kc == nkc - 1)))
        return fw.op("pe", fns, reads=[wres] + list(rhs_res), writes=[bres[bk]])

    def rmsnorm_to_hT(l, goff):
        sq = arena[:, 0:2048].bitcast(BF16).rearrange("p (c n) -> p c n", c=8)
        rsd = arena[:, 2048:2048 + 1024].rearrange("p (i n) -> p i n", i=2)
        r_sq = Res()
        r_rs = [Res(), Res()]
        for tb in range(4):
            cs = slice(tb * 512, (tb + 1) * 512)
            fw.op("act", lambda e, cs=cs: e.activation(out=sq, in_=xT[:, :, cs], func=AF.Square),
                  reads=[r_x[c][tb] for c in range(8)], writes=[r_sq])
            bk = next_bank(list(range(8)))
            fns = [lambda e, c=c, bk=bk: e.matmul(banks[bk][:, :], lhsT=onesm, rhs=sq[:, c, :], start=(c == 0), stop=(c == 7))
                   for c in range(8)]
            fw.op("pe", fns, reads=[r_sq, r_cst], writes=[bres[bk]])
            rs = rsd[:, tb % 2, :]
            fw.op("act", lambda e, bk=bk, rs=rs: e.activation(out=rs, in_=banks[bk][:, :], func=AF.Ln, bias=1e-6),
                  reads=[bres[bk]], writes=[r_rs[tb % 2]])
            fw.op("act", lambda e, rs=rs: e.activation(out=rs, in_=rs, func=AF.Exp, scale=-0.5),
                  reads=[r_rs[tb % 2]], writes=[r_rs[tb % 2]])
            fns = [lambda e, c=c, cs=cs, rs=rs: e.scalar_tensor_tensor(out=hT[:, c, cs], in0=xT[:, c, cs], scalar=vcol(l, goff + c),
                                                                       in1=rs, op0=ALU.mult, op1=ALU.mult) for c in range(8)]
            fw.op("dve", fns, reads=[r_x[c][tb] for c in range(8)] + [r_rs[tb % 2], r_vecs], writes=[r_h[tb]])

    def load_rope(tb):
        i = rp_i[0] % 2
        rp_i[0] += 1
        cs = slice(tb * 512, (tb + 1) * 512)
        fw.dma("sp", f"d_rp{i}", ropeb[i][:, 0, :], cos_d[:, cs], writes=[r_rope[i]])
        fw.dma("sp", f"d_rp{i}", ropeb[i][:, 1, :], sin_d[:, cs], writes=[r_rope[i]])
        return ropeb[i], r_rope[i]

    def rope_evac(bk, dst, dst_res, tb, t1, t2, r_t):
        rb, rres = load_rope(tb)
        fw.op("dve", lambda e: e.tensor_tensor(out=t1, in0=banks[bk][:, :], in1=rb[:, 0, :], op=ALU.mult),
              reads=[bres[bk], rres], writes=[r_t[0]])
        fns = []
        for g in range(4):
            pg = g ^ 1
            fns.append(lambda e, g=g, pg=pg: e.tensor_tensor(out=t2[g * 32:(g + 1) * 32, :], in0=banks[bk][pg * 32:(pg + 1) * 32, :],
                                                             in1=rb[g * 32:(g + 1) * 32, 1, :], op=ALU.mult))
        fw.op("dve", fns, reads=[bres[bk], rres], writes=[r_t[1]])
        fw.op("dve", lambda e: e.tensor_tensor(out=dst, in0=t1, in1=t2, op=ALU.add), reads=[r_t[0], r_t[1]], writes=[dst_res])

    def mixer_branch(l, br):
        A = arena
        kT = A[:, 0:1024].bitcast(BF16)
        qT = A[:, 1024:2048].bitcast(BF16)
        vaug = A[:, 2048:2048 + 1040].bitcast(BF16).rearrange("p (t h d) -> p t h d", t=16, h=2)
        off = 2048 + 1040
        t1 = A[:, off:off + 512]
        t2 = A[:, off + 512:off + 1024]
        off += 1024
        ebuf = [[A[:, off + (2 * s + h) * 512: off + (2 * s + h + 1) * 512] for h in range(2)] for s in range(3)]
        off += 3072
        exbuf = [[A[:, off + (2 * s + h) * 512: off + (2 * s + h + 1) * 512] for h in range(2)] for s in range(2)]
        off += 2048
        spb = [[A[:, off + (2 * s + h) * 256: off + (2 * s + h + 1) * 256].bitcast(BF16) for h in range(2)] for s in range(3)]
        off += 1536
        pb = [[A[:, off + (2 * s + h) * 256: off + (2 * s + h + 1) * 256].bitcast(BF16) for h in range(2)] for s in range(2)]
        off += 1024
        otok = A[:, off:off + 256].bitcast(BF16)
        off += 256
        mbt = A[:, off:off + 256].bitcast(BF16)
        off += 256
        mb = A[:, off:off + 256].bitcast(BF16)
        off += 256
        small = A[:, off:off + 256]
        off += 256
        qT2 = A[:, off:off + 1024].bitcast(BF16)
        off += 1024
        mb3 = A[:, off:off + 256].bitcast(BF16)
        off += 256
        assert off <= 14 * 1024, off
        gm = small[:, 0:64].rearrange("p (h i n) -> p h i n", h=2, i=4)
        mx = small[:, 64:72]
        selt = small[:, 72:80]
        den = small[:, 80:88].rearrange("p (h i) -> p h i", h=2)
        ksum = small[:, 96:104]
        kmf = small[:, 104:112]
        kmh = small[:, 112:116].bitcast(BF16)
        kml = small[:, 116:120].bitcast(BF16)

        r_kT = [Res() for _ in range(4)]
        r_qT = [Res() for _ in range(4)]
        r_v = [Res() for _ in range(4)]
        r_t = [Res(), Res()]
        r_e = [[Res(), Res()], [Res(), Res()], [Res(), Res()]]
        r_ex = [[Res(), Res()], [Res(), Res()]]
        r_sp = [[Res(), Res()], [Res(), Res()], [Res(), Res()]]
        r_p = [[Res(), Res()], [Res(), Res()]]
        r_otok, r_mbt, r_mb, r_small, r_km = Res(), Res(), Res(), Res(), Res()

        if br == 0:
            units = [0]
        else:
            units = [0, 1, 2, 3]
        base = {0: 0, 1: 768, 2: 2304}[br]
        has_den = br != 1
        vw = 65 if has_den else 64

        if has_den:
            fw.op("dve", lambda e: e.memset(vaug[:, :, :, 64:65], 1.0), writes=r_v)
        if br == 2:
            fw.op("dve", lambda e: e.memset(mbt, 0.0), writes=[r_mbt])

        def proj_kv(wv, wres, kcol, vcol0):
            for tb in range(4):
                bk = next_bank([0, 1, 2, 3, 7])
                cs = slice(tb * 512, (tb + 1) * 512)
                dense(bk, slice(0, 512), wv, wres, 8, lambda kc, cs=cs: hT[:, kc, cs], [r_h[tb]], wcol0=kcol)
                if br == 1:
                    fw.op("act", lambda e, bk=bk, cs=cs: e.activation(out=kT[:, cs], in_=banks[bk][:, :], func=AF.Copy),
                          reads=[bres[bk]], writes=[r_kT[tb]])
                else:
                    rope_evac(bk, kT[:, cs], r_kT[tb], tb, t1, t2, r_t)
            for g4 in range(4):
                bk = next_bank([0, 1, 2, 3, 7])
                fns = []
                for j in range(4):
                    tt = g4 * 4 + j
                    for kc in range(8):
                        fns.append(lambda e, j=j, tt=tt, kc=kc, bk=bk: e.matmul(
                            banks[bk][:, j * 128:(j + 1) * 128], lhsT=hT[:, kc, tt * 128:(tt + 1) * 128],
                            rhs=wv[:, kc, vcol0:vcol0 + 128], start=(kc == 0), stop=(kc == 7)))
                fw.op("pe", fns, reads=[wres, r_h[g4]], writes=[bres[bk]])
                fw.op("act", lambda e, bk=bk, g4=g4: e.activation(
                    out=vaug[:, g4 * 4:(g4 + 1) * 4, :, 0:64],
                    in_=banks[bk][:, :].rearrange("p (t h d) -> p t h d", t=4, h=2), func=AF.Copy),
                    reads=[bres[bk]], writes=[r_v[g4]])

        def proj_q(wv, wres, qcol, Q):
            bk = next_bank([7])
            cs = slice(Q * 512, (Q + 1) * 512)
            dense(bk, slice(0, 512), wv, wres, 8, lambda kc: hT[:, kc, cs], [r_h[Q]], wcol0=qcol)
            if br == 1:
                fw.op("act", lambda e: e.activation(out=qT[:, cs], in_=banks[bk][:, :], func=AF.Copy),
                      reads=[bres[bk]], writes=[r_qT[Q]])
            else:
                rope_evac(bk, qT[:, cs], r_qT[Q], Q, t1, t2, r_t)

        def finish_o(obanks, Q, chunk, sink_cols):
            ot4 = otok.rearrange("p (i f) -> p i f", i=4)
            if has_den:
                for hh in range(2):
                    o3 = banks[obanks[hh]][:, 0:260].rearrange("p (i d) -> p i d", i=4)
                    if sink_cols is not None:
                        fw.op("dve", lambda e, hh=hh, o3=o3: e.tensor_scalar(out=den[:, hh, :], in0=o3[:, :, 64], scalar1=esink[:, sink_cols[hh]:sink_cols[hh] + 1],
                                                                            scalar2=None, op0=ALU.add),
                              reads=[bres[obanks[hh]], r_esink], writes=[r_small])
                    else:
                        fw.op("dve", lambda e, hh=hh, o3=o3: e.tensor_copy(out=den[:, hh, :], in_=o3[:, :, 64]),
                              reads=[bres[obanks[hh]]], writes=[r_small])
                    fw.op("dve", lambda e, hh=hh: e.reciprocal(out=den[:, hh, :], in_=den[:, hh, :]), reads=[r_small], writes=[r_small])
                    fw.op("dve", lambda e, hh=hh, o3=o3: e.tensor_tensor(out=ot4[:, :, hh * 64:(hh + 1) * 64], in0=o3[:, :, 0:64],
                                                                        in1=den[:, hh, :].unsqueeze(2).to_broadcast([128, 4, 64]), op=ALU.mult),
                          reads=[bres[obanks[hh]], r_small], writes=[r_otok])
            else:
                fw.op("act", lambda e: e.activation(out=otok, in_=banks[obanks[0]][:, :], func=AF.Copy),
                      reads=[bres[obanks[0]]], writes=[r_otok])
            tb7 = banks[7][:, 0:256].bitcast(BF16)
            fns = [lambda e, i=i: e.transpose(tb7[:, i * 128:(i + 1) * 128], ot4[:, i, :], ident) for i in range(4)]
            fw.op("pe", fns, reads=[r_otok, r_cst], writes=[bres[7]])
            fw.op("dve", lambda e: e.tensor_copy(out=oT[:, chunk, Q * 512:(Q + 1) * 512], in_=tb7), reads=[bres[7]], writes=[r_oT[chunk][Q]])

        if br == 0:
            (wkv,), wres = load_w([("w_in", l, None, (0, 1024), (512, 768))])
            proj_kv(wkv, wres, 0, 128)
            qTs = [qT, qT2]
            r_qTs = [[Res() for _ in range(4)] for _ in range(2)]
            r_S = [[Res(), Res()], [Res(), Res()]]

            def proj_chunk(c):
                (wq,), wqres = load_w([("w_in", l, None, (0, 1024), (c * 128, (c + 1) * 128))])
                for Q in range(4):
                    bk = next_bank([6, 7])
                    cs = slice(Q * 512, (Q + 1) * 512)
                    dense(bk, slice(0, 512), wq, wqres, 8, lambda kc, cs=cs: hT[:, kc, cs], [r_h[Q]])
                    rope_evac(bk, qTs[c % 2][:, cs], r_qTs[c % 2][Q], Q, t1, t2, r_t)

            def attn_chunk(c):
                qTc, r_q = qTs[c % 2], r_qTs[c % 2]
                steps = []
                for Q in range(4):
                    lst = list(range(max(0, 4 * Q - 1), 4 * Q + 4))
                    for si, a in enumerate(lst):
                        steps.append((Q, a, si == 0, si == len(lst) - 1))
                n_st = len(steps)

                def geom(Q, a):
                    i0 = a - 4 * Q
                    if i0 < 0:
                        return [0], m_gt
                    if i0 == 3:
                        return [3], m_le
                    return [i0, i0 + 1], m_legt

                def s_qk(idx):
                    Q, a, _, _ = steps[idx]
                    s = idx % 2
                    qt, msk = geom(Q, a)
                    n = 128 * len(qt)
                    qc = slice(Q * 512 + qt[0] * 128, Q * 512 + qt[0] * 128 + n)
                    sc = slice(0, n)
                    for hh in range(2):
                        rows = slice(hh * 64, (hh + 1) * 64)
                        fw.op("pe", lambda e, hh=hh, rows=rows, sc=sc, qc=qc, a=a, s=s: e.matmul(
                            banks[2 * hh + s][:, sc], lhsT=kT[rows, a * 128:(a + 1) * 128], rhs=qTc[rows, qc], start=True, stop=True),
                            reads=[r_kT[a // 4], r_q[Q]], writes=[bres[2 * hh + s]])
                    for hh in range(2):
                        P = pb[s][hh][:, 0:n]
                        fw.op("act", lambda e, hh=hh, sc=sc, P=P, s=s: e.activation(out=P, in_=banks[2 * hh + s][:, sc], func=AF.Exp, scale=0.125),
                              reads=[bres[2 * hh + s]], writes=[r_p[s][hh]])
                        fw.op("dve", lambda e, P=P, msk=msk, n=n: e.tensor_tensor(out=P, in0=P, in1=msk[:, 0:n], op=ALU.mult),
                              reads=[r_p[s][hh], r_cst], writes=[r_p[s][hh]])

                def s_pv(idx):
                    Q, a, isfirst, islast = steps[idx]
                    s = idx % 2
                    qt, msk = geom(Q, a)
                    ob = [4, 5]
                    for hh in range(2):
                        P = pb[s][hh]
                        fns = []
                        for j, i in enumerate(qt):
                            fns.append(lambda e, hh=hh, j=j, i=i, P=P, a=a, st=(isfirst and j == 0), ob=ob: e.matmul(
                                banks[ob[hh]][:, i * 65:(i + 1) * 65], lhsT=P[:, j * 128:(j + 1) * 128], rhs=vaug[:, a, hh, :],
                                start=st, stop=False, skip_group_check=True))
                        fw.op("pe", fns, reads=[r_p[s][hh], r_v[a // 4]], writes=[bres[ob[hh]]])
                    if islast:
                        finish_o(ob, Q, c, [l * 8 + c, l * 8 + 4 + c])

                s_qk(0)
                for idx in range(n_st):
                    if idx + 1 < n_st:
                        s_qk(idx + 1)
                    s_pv(idx)

            proj_chunk(0)
            for c in range(4):
                if c + 1 < 4:
                    proj_chunk(c + 1)
                attn_chunk(c)
            return

        for u in units:
            qc0 = base + u * 128
            kc0 = base + 512 + u * 128
            vc0 = base + 1024 + u * 128
            (wq, wk, wvv), wres = load_w([("w_in", l, None, (0, 1024), (qc0, qc0 + 128)), ("w_in", l, None, (0, 1024), (kc0, kc0 + 128)), ("w_in", l, None, (0, 1024), (vc0, vc0 + 128))])
            for tb in range(4):
                bk = next_bank([0, 1, 2, 3, 7])
                cs = slice(tb * 512, (tb + 1) * 512)
                dense(bk, slice(0, 512), wk, wres, 8, lambda kc, cs=cs: hT[:, kc, cs], [r_h[tb]])
                if br == 1:
                    fw.op("act", lambda e, bk=bk, cs=cs: e.activation(out=kT[:, cs], in_=banks[bk][:, :], func=AF.Copy),
                          reads=[bres[bk]], writes=[r_kT[tb]])
                else:
                    rope_evac(bk, kT[:, cs], r_kT[tb], tb, t1, t2, r_t)
            for g4 in range(4):
                bk = next_bank([0, 1, 2, 3, 7])
                fns = []
                for j in range(4):
                    tt = g4 * 4 + j
                    for kc in range(8):
                        fns.append(lambda e, j=j, tt=tt, kc=kc, bk=bk, wvv=wvv: e.matmul(
                            banks[bk][:, j * 128:(j + 1) * 128], lhsT=hT[:, kc, tt * 128:(tt + 1) * 128],
                            rhs=wvv[:, kc, :], start=(kc == 0), stop=(kc == 7)))
                fw.op("pe", fns, reads=[wres, r_h[g4]], writes=[bres[bk]])
                fw.op("act", lambda e, bk=bk, g4=g4: e.activation(
                    out=vaug[:, g4 * 4:(g4 + 1) * 4, :, 0:64],
                    in_=banks[bk][:, :].rearrange("p (t h d) -> p t h d", t=4, h=2), func=AF.Copy),
                    reads=[bres[bk]], writes=[r_v[g4]])
            if br == 2:
                fw.op("dve", lambda e: e.tensor_reduce(out=ksum, in_=kT.rearrange("p (n k) -> p n k", n=8), axis=AX.X, op=ALU.add),
                      reads=r_kT, writes=[r_small])
                fw.op("dve", lambda e: e.tensor_scalar(out=kmf, in0=ksum, scalar1=1.0 / 256, scalar2=None, op0=ALU.mult),
                      reads=[r_small], writes=[r_small])
                fw.op("dve", lambda e: e.tensor_copy(out=kmh, in_=kmf), reads=[r_small], writes=[r_km])
                fw.op("dve", lambda e: e.tensor_tensor(out=kml, in0=kmf, in1=kmh, op=ALU.subtract), reads=[r_small, r_km], writes=[r_km])

            for Q in range(4):
                proj_q(wq, wres, 0, Q)
            for Q in range(4):
                qcs = slice(Q * 512, (Q + 1) * 512)
                need_sel = (br == 2 and Q >= 2)
                if need_sel:
                    gb = [5, 6]
                    gb = [6, 7]
                    for hh in range(2):
                        rows = slice(hh * 64, (hh + 1) * 64)
                        fns = []
                        for i in range(4):
                            qi = slice(Q * 512 + i * 128, Q * 512 + (i + 1) * 128)
                            fns.append(lambda e, hh=hh, i=i, qi=qi, rows=rows: e.matmul(banks[gb[hh]][:, i * 8:(i + 1) * 8], lhsT=qT[rows, qi], rhs=kmh[rows, :], start=True, stop=False))
                            fns.append(lambda e, hh=hh, i=i, qi=qi, rows=rows: e.matmul(banks[gb[hh]][:, i * 8:(i + 1) * 8], lhsT=qT[rows, qi], rhs=kml[rows, :], start=False, stop=True))
                        fw.op("pe", fns, reads=[r_qT[Q], r_km], writes=[bres[gb[hh]]])
                    fw.op("dve", lambda e: e.memset(gm, -1e30), writes=[r_small])
                    for hh in range(2):
                        g3 = banks[gb[hh]][:, 0:32].rearrange("p (i n) -> p i n", i=4)
                        fw.op("dve", [lambda e, hh=hh, g3=g3, Q=Q: e.tensor_copy(out=gm[:, hh, 0:2, 0:2 * Q], in_=g3[:, 0:2, 0:2 * Q]),
                                      lambda e, hh=hh, g3=g3, Q=Q: e.tensor_copy(out=gm[:, hh, 2:4, 0:2 * Q + 1], in_=g3[:, 2:4, 0:2 * Q + 1])],
                              reads=[bres[gb[hh]]], writes=[r_small])
                    mbt4 = mbt.rearrange("p (i f) -> p i f", i=4)
                    for hh in range(2):
                        for i in range(4):
                            nb = 2 * Q + i // 2
                            fw.op("dve", lambda e, hh=hh, i=i: e.max(out=mx, in_=gm[:, hh, i, :]), reads=[r_small], writes=[r_small])
                            fw.op("dve", lambda e, hh=hh, i=i: e.tensor_scalar(out=selt, in0=gm[:, hh, i, :], scalar1=mx[:, 2:3], scalar2=None, op0=ALU.is_ge),
                                  reads=[r_small], writes=[r_small])
                            fw.op("dve", [lambda e, hh=hh, i=i: e.tensor_scalar(out=mbt4[:, i, hh * 64:hh * 64 + 8], in0=selt, scalar1=-1.0, scalar2=-NEG,
                                                                                op0=ALU.add, op1=ALU.mult),
                                          lambda e, hh=hh, i=i, nb=nb: e.memset(mbt4[:, i, hh * 64 + nb:hh * 64 + nb + 1], 0.0)],
                                  reads=[r_small], writes=[r_mbt])
                    tb7 = banks[7][:, 0:256].bitcast(BF16)
                    fns = [lambda e, i=i: e.transpose(tb7[:, i * 128:(i + 1) * 128], mbt4[:, i, :], ident) for i in range(4)]
                    fw.op("pe", fns, reads=[r_mbt, r_cst], writes=[bres[7]])
                    fw.op("dve", lambda e: e.tensor_copy(out=mb, in_=tb7), reads=[bres[7]], writes=[r_mb])

                steps = list(range(4 * Q + 3, -1, -1))
                ns = len(steps)
                if br == 1:
                    ob = [6, 6]
                    xb = [4, 5]
                else:
                    ob = [4, 5]

                def cols_of(a):
                    i0 = max(0, a - 4 * Q)
                    return i0, slice(i0 * 128, 512)

                def stage_qk(t):
                    a = steps[t]
                    s = t % 2
                    i0, cl = cols_of(a)
                    qcl = slice(Q * 512 + i0 * 128, (Q + 1) * 512)
                    for hh in range(2):
                        rows = slice(hh * 64, (hh + 1) * 64)
                        bk = 2 * hh + s
                        msk = need_sel and a < 4 * Q + 2
                        fns = [lambda e, a=a, rows=rows, bk=bk, cl=cl, qcl=qcl, msk=msk: e.matmul(
                            banks[bk][:, cl], lhsT=kT[rows, a * 128:(a + 1) * 128], rhs=qT[rows, qcl], start=True, stop=not msk)]
                        rd = [r_kT[a // 4], r_qT[Q]]
                        if msk:
                            n = a // 2
                            fns.append(lambda e, rows=rows, bk=bk, cl=cl, n=n: e.matmul(
                                banks[bk][:, cl], lhsT=cst[rows, 1152 + n * 128:1152 + (n + 1) * 128], rhs=mb[rows, cl], start=False, stop=True))
                            rd += [r_mb, r_cst]
                        fw.op("pe", fns, reads=rd, writes=[bres[bk]])
                        if br == 1:
                            s3 = t % 3
                            E = ebuf[s3][hh]
                            SP = spb[s3][hh]
                            fw.op("act", lambda e, bk=bk, cl=cl, E=E: e.activation(out=E[:, cl], in_=banks[bk][:, cl], func=AF.Exp, scale=0.125),
                                  reads=[bres[bk]], writes=[r_e[s3][hh]])
                            fw.op("act", lambda e, cl=cl, E=E, SP=SP: e.activation(out=SP[:, cl], in_=E[:, cl], func=AF.Ln, bias=1.0),
                                  reads=[r_e[s3][hh]], writes=[r_sp[s3][hh]])
                            if a >= 4 * Q:
                                dc = slice(i0 * 128, (i0 + 1) * 128)
                                fw.op("dve", lambda e, SP=SP, dc=dc: e.tensor_tensor(out=SP[:, dc], in0=SP[:, dc], in1=m_lt, op=ALU.mult),
                                      reads=[r_sp[s3][hh], r_cst], writes=[r_sp[s3][hh]])
                        else:
                            P = pb[s][hh]
                            fw.op("act", lambda e, bk=bk, cl=cl, P=P: e.activation(out=P[:, cl], in_=banks[bk][:, cl], func=AF.Exp, scale=0.125),
                                  reads=[bres[bk]], writes=[r_p[s][hh]])
                            if a >= 4 * Q:
                                dc = slice(i0 * 128, (i0 + 1) * 128)
                                fw.op("dve", lambda e, P=P, dc=dc: e.tensor_tensor(out=P[:, dc], in0=P[:, dc], in1=m_le, op=ALU.mult),
                                      reads=[r_p[s][hh], r_cst], writes=[r_p[s][hh]])

                def stage_xa(t):
                    a = steps[t]
                    s = t % 2
                    s3 = t % 3
                    i0, cl = cols_of(a)
                    last = (t == ns - 1)
                    for hh in range(2):
                        SP = spb[s3][hh]
                        fw.op("pe", lambda e, hh=hh, cl=cl, SP=SP, t=t, last=last, xb=xb: e.matmul(
                            banks[xb[hh]][:, cl], lhsT=tincl, rhs=SP[:, cl], start=(t == 0), stop=last, skip_group_check=True),
                            reads=[r_sp[s3][hh], r_cst], writes=[bres[xb[hh]]])
                    for hh in range(2):
                        EX = exbuf[s][hh]
                        fw.op("act", lambda e, hh=hh, cl=cl, EX=EX, xb=xb: e.activation(out=EX[:, cl], in_=banks[xb[hh]][:, cl], func=AF.Exp),
                              reads=[bres[xb[hh]]], writes=[r_ex[s][hh]])

                def stage_xb(t):
                    a = steps[t]
                    s = t % 2
                    s3 = t % 3
                    i0, cl = cols_of(a)
                    last = (t == ns - 1)
                    if not last:
                        for hh in range(2):
                            SP = spb[s3][hh]
                            fw.op("pe", lambda e, hh=hh, cl=cl, SP=SP, xb=xb: e.matmul(
                                banks[xb[hh]][:, cl], lhsT=tcomp, rhs=SP[:, cl], start=False, stop=False, skip_group_check=True),
                                reads=[r_sp[s3][hh], r_cst], writes=[bres[xb[hh]]])
                    for hh in range(2):
                        E, EX, W = ebuf[s3][hh], exbuf[s][hh], pb[s][hh]
                        fw.op("dve", lambda e, cl=cl, E=E, EX=EX, W=W: e.tensor_tensor(out=W[:, cl], in0=E[:, cl], in1=EX[:, cl], op=ALU.mult),
                              reads=[r_e[s3][hh], r_ex[s][hh]], writes=[r_p[s][hh]])
                        if a >= 4 * Q:
                            dc = slice(i0 * 128, (i0 + 1) * 128)
                            fw.op("dve", lambda e, W=W, dc=dc: e.tensor_tensor(out=W[:, dc], in0=W[:, dc], in1=m_lt, op=ALU.mult),
                                  reads=[r_p[s][hh], r_cst], writes=[r_p[s][hh]])

                def stage_pv(t):
                    a = steps[t]
                    s = t % 2
                    i0, cl = cols_of(a)
                    for hh in range(2):
                        P = pb[s][hh]
                        fns = []
                        for i in range(i0, 4):
                            if br == 1:
                                oc = slice(i * 128 + hh * 64, i * 128 + hh * 64 + 64)
                                st = (t == 0 and hh == 0 and i == i0)
                                rhs = vaug[:, a, hh, 0:64]
                            else:
                                oc = slice(i * 65, (i + 1) * 65)
                                st = (t == 0 and i == i0)
                                rhs = vaug[:, a, hh, :]
                            fns.append(lambda e, hh=hh, i=i, oc=oc, st=st, rhs=rhs, P=P, ob=ob: e.matmul(
                                banks[ob[hh]][:, oc], lhsT=P[:, i * 128:(i + 1) * 128], rhs=rhs, start=st, stop=False, skip_group_check=True))
                        fw.op("pe", fns, reads=[r_p[s][hh], r_v[a // 4]], writes=[bres[ob[hh]]])

                if br == 1:
                    stage_qk(0)
                    if ns > 1:
                        stage_qk(1)
                    for t in range(ns):
                        stage_xa(t)
                        if t >= 1:
                            stage_pv(t - 1)
                        if t + 2 < ns:
                            stage_qk(t + 2)
                        stage_xb(t)
                    stage_pv(ns - 1)
                else:
                    stage_qk(0)
                    for t in range(ns):
                        if t + 1 < ns:
                            stage_qk(t + 1)
                        stage_pv(t)
                finish_o(ob, Q, u, None)

    def post_branch(l, br):
        mT = arena[:, 0:8192].bitcast(BF16).rearrange("p (c n) -> p c n", c=8)
        sg = [arena[:, 8192 + i * 512: 8192 + (i + 1) * 512] for i in range(2)]
        r_sg = [Res(), Res()]
        r_m = [[Res() for _ in range(4)] for _ in range(8)]
        k = 0
        for j in range(8):
            gcol = 3840 + br * 1024 + j * 128
            (wg, wb), wres = load_w([("w_in", l, None, (0, 1024), (gcol, gcol + 128)), ("w_br", l, br, (0, 512), (j * 128, (j + 1) * 128))])
            wres2 = wres
            for tb in range(4):
                cs = slice(tb * 512, (tb + 1) * 512)
                bg = next_bank(list(range(8)))
                dense(bg, slice(0, 512), wg, wres, 8, lambda kc, cs=cs: hT[:, kc, cs], [r_h[tb]])
                by = next_bank(list(range(8)))
                dense(by, slice(0, 512), wb, wres2, 4, lambda kc, cs=cs: oT[:, kc, cs], [r_oT[c][tb] for c in range(4)])
                s = k % 2
                k += 1
                fw.op("act", lambda e, bg=bg, s=s, j=j: e.activation(out=sg[s], in_=banks[bg][:, :], func=AF.Sigmoid, bias=vcol(l, 16 + br * 8 + j)),
                      reads=[bres[bg], r_vecs], writes=[r_sg[s]])
                fw.op("dve", lambda e, by=by, s=s, j=j, cs=cs: e.tensor_tensor(out=mT[:, j, cs], in0=banks[by][:, :], in1=sg[s], op=ALU.mult),
                      reads=[bres[by], r_sg[s]], writes=[r_m[j][tb]])
        for j in range(8):
            (wo,), wres = load_w([("w_out", l, None, (0, 1024), (j * 128, (j + 1) * 128))])
            for tb in range(4):
                cs = slice(tb * 512, (tb + 1) * 512)
                bk = next_bank(list(range(8)))
                dense(bk, slice(0, 512), wo, wres, 8, lambda kc, cs=cs: mT[:, kc, cs], [r_m[c][tb] for c in range(8)])
                fw.op("dve", lambda e, bk=bk, j=j, cs=cs: e.tensor_tensor(out=xT[:, j, cs], in0=banks[bk][:, :], in1=xT[:, j, cs], op=ALU.add),
                      reads=[bres[bk]], writes=[r_x[j][tb]])

    def ffn(l):
        actT = arena[:, 0:11 * 1024].bitcast(BF16).rearrange("p (c n) -> p c n", c=11)
        rb_ = [[oT[:, (2 * s + z) // 2, ((2 * s + z) % 2) * 1024:((2 * s + z) % 2 + 1) * 1024].bitcast(F32) for z in range(2)] for s in range(4)]
        r_rb = [[Res(), Res()] for _ in range(4)]
        tiles = [(0, 512), (512, 510), (1022, 510), (1532, 510), (2042, 6)]
        allh = list(r_h)
        k = [0]
        stages = []
        for half in range(2):
            for f in range(11):
                fc = half * 11 + f
                stages.append(("up", half, f, [("w_up", l, None, (0, 1024), (fc * 128, (fc + 1) * 128)),
                                                ("w_up", l, None, (0, 1024), (2816 + fc * 128, 2816 + (fc + 1) * 128))]))
            for j in range(8):
                stages.append(("dn", half, j, [("w_dn", l, None, (half * 1408, (half + 1) * 1408), (j * 128, (j + 1) * 128))]))
        loaded = [None] * len(stages)
        loaded[0] = load_w(stages[0][3])
        r_act = [Res() for _ in range(11)]
        pend = []

        def flush(keep):
            while len(pend) > keep:
                (s, f, t0, n, ra) = pend.pop(0)
                Ra, Rv = rb_[s][0], rb_[s][1]
                fw.op("act", lambda e, Ra=Ra, n=n: e.activation(out=Ra[:, 0:n], in_=Ra[:, 0:n], func=AF.Silu), reads=[r_rb[s][0]], writes=[r_rb[s][0]])
                fw.op("pool", lambda e, Ra=Ra, Rv=Rv, f=f, t0=t0, n=n: e.tensor_tensor(out=actT[:, f, t0:t0 + n], in0=Ra[:, 0:n], in1=Rv[:, 0:n], op=ALU.mult),
                      reads=[r_rb[s][0], r_rb[s][1]], writes=[ra[f]])

        for si, (kind, half, idx, _) in enumerate(stages):
            if si + 1 < len(stages):
                loaded[si + 1] = load_w(stages[si + 1][3])
            views, wres = loaded[si]
            if kind == "up":
                f = idx
                fc = half * 11 + f
                wa, wvv = views
                for (t0, n) in tiles:
                    s = k[0] % 4
                    k[0] += 1
                    info = []
                    for z, wz in enumerate((wa, wvv)):
                        ch = fc + z * 22
                        bk = next_bank(list(range(8)))
                        if t0 == 0:
                            cs = slice(0, 512)
                            N = 512
                        else:
                            cs = slice(t0 - 2, t0 + n)
                            N = n + 2
                        dense(bk, slice(0, N), wz, wres, 8, lambda kc, cs=cs: hT[:, kc, cs], allh)
                        info.append((z, ch, bk))
                    for (z, ch, bk) in info:
                        R = rb_[s][z]
                        src = banks[bk][:, 0:512] if t0 == 0 else banks[bk][:, 2:n + 2]
                        fw.op("act", lambda e, R=R, src=src, ch=ch, n=n: e.activation(out=R[:, 0:n], in_=src, func=AF.Identity,
                                                                                   scale=vcol(l, 40 + 2 * 44 + ch), bias=vcol(l, 172 + ch)),
                              reads=[bres[bk], r_vecs], writes=[r_rb[s][z]])
                    for tap, off in ((1, 1), (0, 2)):
                        for (z, ch, bk) in info:
                            R = rb_[s][z]
                            if t0 == 0:
                                dst = R[:, off:512]
                                src = banks[bk][:, 0:512 - off]
                            else:
                                dst = R[:, 0:n]
                                src = banks[bk][:, 2 - off:2 - off + n]
                            fw.op("dve", lambda e, dst=dst, src=src, ch=ch, tap=tap: e.scalar_tensor_tensor(
                                out=dst, in0=src, scalar=vcol(l, 40 + tap * 44 + ch), in1=dst, op0=ALU.mult, op1=ALU.add),
                                reads=[bres[bk], r_rb[s][z], r_vecs], writes=[r_rb[s][z]])
                    pend.append((s, f, t0, n, r_act))
                    flush(2)
            else:
                j = idx
                if j == 0:
                    flush(0)
                (wd,) = views
                for tb in range(4):
                    cs = slice(tb * 512, (tb + 1) * 512)
                    bk = next_bank(list(range(8)))
                    dense(bk, slice(0, 512), wd, wres, 11, lambda kc, cs=cs: actT[:, kc, cs], r_act)
                    fw.op("dve", lambda e, bk=bk, j=j, cs=cs: e.tensor_tensor(out=xT[:, j, cs], in0=banks[bk][:, :], in1=xT[:, j, cs], op=ALU.add),
                          reads=[bres[bk]], writes=[r_x[j][tb]])
        fw.barrier()

    for l in range(nlayers):
        rmsnorm_to_hT(l, 0)
        fw.barrier()
        for br in branches:
            mixer_branch(l, br)
            fw.barrier()
            if dbg == "oT":
                break
            post_branch(l, br)
            fw.barrier()
        if dbg == "oT":
            break
        if do_ffn:
            rmsnorm_to_hT(l, 8)
            fw.barrier()
            ffn(l)
            fw.barrier()

    r_out = Res()
    if dbg == "oT":
        for c in range(4):
            fw.dma("pool", "d_out", out_d[c * 128:(c + 1) * 128, :], oT[:, c, :], reads=[r_oT[c][tb] for tb in range(4)], writes=[r_out])
        fw.wait("sp", r_out.w)
    elif dbg == "x":
        for c in range(8):
            fw.dma("sp", "d_out", out_d[c * 128:(c + 1) * 128, :], xT[:, c, :], reads=[r_x[c][tb] for tb in range(4)], writes=[r_out])
    else:
        sq = arena[:, 0:2048].bitcast(BF16).rearrange("p (c n) -> p c n", c=8)
        rsd = arena[:, 2048:2048 + 512]
        ob = [arena[:, 4096 + i * 4096: 4096 + (i + 1) * 4096].rearrange("p (c n) -> p c n", c=8) for i in range(2)]
        r_sq, r_rs, r_ob = Res(), Res(), [Res(), Res()]
        goff = NL * LV
        for tb in range(4):
            cs = slice(tb * 512, (tb + 1) * 512)
            fw.op("act", lambda e, cs=cs: e.activation(out=sq, in_=xT[:, :, cs], func=AF.Square),
                  reads=[r_x[c][tb] for c in range(8)], writes=[r_sq])
            bk = next_bank(list(range(8)))
            fns = [lambda e, c=c, bk=bk: e.matmul(banks[bk][:, :], lhsT=onesm, rhs=sq[:, c, :], start=(c == 0), stop=(c == 7)) for c in range(8)]
            fw.op("pe", fns, reads=[r_sq, r_cst], writes=[bres[bk]])
            fw.op("act", lambda e, bk=bk: e.activation(out=rsd, in_=banks[bk][:, :], func=AF.Ln, bias=1e-6), reads=[bres[bk]], writes=[r_rs])
            fw.op("act", lambda e: e.activation(out=rsd, in_=rsd, func=AF.Exp, scale=-0.5), reads=[r_rs], writes=[r_rs])
            O = ob[tb % 2]
            fns = [lambda e, c=c, cs=cs, O=O: e.scalar_tensor_tensor(out=O[:, c, :], in0=xT[:, c, cs], scalar=vecs[:, goff + c:goff + c + 1],
                                                                    in1=rsd, op0=ALU.mult, op1=ALU.mult) for c in range(8)]
            fw.op("dve", fns, reads=[r_x[c][tb] for c in range(8)] + [r_rs, r_vecs], writes=[r_ob[tb % 2]])
            fw.dma("sp", "d_out", out_d.rearrange("(c p) n -> p c n", p=128)[:, :, cs], O, reads=[r_ob[tb % 2]], writes=[r_out])
    fw.wait("sp", r_out.w)
    fw.barrier()
    fw.replay()
    es.close()
    build.last_plan = plan
    return nc


def host_consts():
    cst = np.zeros((128, 2048), np.float32)
    j = np.arange(128)[:, None]
    s = np.arange(128)[None, :]
    cst[:, 0:128] = np.eye(128)
    cst[:, 128:256] = -1.0 * (j >= s)
    cst[:, 256:384] = -1.0 * (j < s)
    cst[:, 384:512] = (j <= s)
    cst[:, 512:640] = (j < s)
    cst[:, 640:768] = (j > s)
    cst[:, 768:896] = (j <= s)
    cst[:, 896:1024] = (j > s)
    cst[:, 1024:1152] = 1.0 / 1024
    for n in range(7):
        cst[n, 1152 + n * 128:1152 + (n + 1) * 128] = 1.0
        cst[64 + n, 1152 + n * 128:1152 + (n + 1) * 128] = 1.0
    inv = (10000.0 ** (-(np.arange(0, 64, 2, dtype=np.float32)) / np.float32(64))).astype(np.float32)
    ang = (np.arange(S, dtype=np.float32)[None, :] * inv[:, None]).astype(np.float32)
    cos, sin = np.cos(ang).astype(np.float32), np.sin(ang).astype(np.float32)
    cosT = np.tile(cos, (4, 1))
    sinT = np.concatenate([-sin, sin, -sin, sin], 0)
    return cst, np.ascontiguousarray(cosT), np.ascontiguousarray(sinT)


def host_prep(inputs, plan):
    f = lambda a: np.ascontiguousarray(np.asarray(a, dtype=np.float32))
    w_in = f(inputs["w_in"]).copy()
    perm = []
    for c in range(4):
        perm += list(range(c * 64, c * 64 + 64)) + list(range((4 + c) * 64, (4 + c) * 64 + 64))
    w_in[:, :, 0:512] = w_in[:, :, perm]
    w_br = f(inputs["w_branch"]).copy()
    w_br[:, 0] = w_br[:, 0][:, perm, :]
    vecs = np.zeros((128, NV), np.float32)
    pc = lambda v: np.asarray(v, np.float32).reshape(-1, 128).T
    for l in range(NL):
        b = l * LV
        vecs[:, b:b + 8] = pc(inputs["norm_mix"][l])
        vecs[:, b + 8:b + 16] = pc(inputs["norm_ffn"][l])
        vecs[:, b + 16:b + 40] = pc(inputs["b_gate"][l])
        cw = np.asarray(inputs["conv_w"][l], np.float32)
        for tap in range(3):
            vecs[:, b + 40 + tap * 44:b + 40 + (tap + 1) * 44] = pc(cw[tap])
        vecs[:, b + 172:b + 216] = pc(inputs["conv_b"][l])
        vecs[:, b + 216:b + 224] = np.asarray(inputs["sinks"][l], np.float32)[None, :]
    vecs[:, NL * LV:NL * LV + 8] = pc(inputs["norm_final"])
    cst, cosT, sinT = host_consts()
    srcs = {"w_in": w_in, "w_br": w_br, "w_out": f(inputs["w_out"]), "w_up": f(inputs["w_up"]), "w_dn": f(inputs["w_down"])}
    wpack = np.zeros((128, NL * 159744), np.float32)
    for (name, l, sub, r0, r1, c0, c1, off) in plan:
        A = srcs[name][l] if sub is None else srcs[name][l][sub]
        A = A[r0:r1, c0:c1]
        kc, n = (r1 - r0) // 128, c1 - c0
        wpack[:, off:off + kc * n] = A.reshape(kc, 128, n).transpose(1, 0, 2).reshape(128, kc * n)
    shared = {"wpack": wpack, "vecs": vecs, "cosT": cosT, "sinT": sinT, "cst": cst}
    x = np.asarray(inputs["x"], np.float32)
    in_maps = []
    for b in range(8):
        m = dict(shared)
        m["xT"] = np.ascontiguousarray(x[b].T)
        in_maps.append(m)
    return in_maps


_NC = {}


def kernel(**inputs):
    if "nc" not in _NC:
        _NC["nc"] = build()
        _NC["plan"] = build.last_plan
    nc = _NC["nc"]
    in_maps = host_prep(inputs, _NC["plan"])
    res = run_bass_kernel_spmd(nc, in_maps, core_ids=list(range(8)))
    out = np.stack([np.ascontiguousarray(res.results[b]["outT"].T) for b in range(8)], 0)
    return out.astype(np.float32)
```

```python
from contextlib import ExitStack
import numpy as np
import concourse.bass as bass
import concourse.mybir as mybir
from concourse.bass_utils import run_bass_kernel_spmd

F32 = mybir.dt.float32
BF16 = mybir.dt.bfloat16
AF = mybir.ActivationFunctionType
ALU = mybir.AluOpType
AX = mybir.AxisListType

S = 2048
D = 1024
NL = 2
LV = 224
NV = NL * LV + 8
NEG = -30000.0


class Res:
    __slots__ = ("w", "r")

    def __init__(self):
        self.w = None
        self.r = []


class FW:
    ENG = ("pe", "act", "dve", "pool", "sp")

    def __init__(self, nc, es):
        self.nc = nc
        self.streams = {n: [] for n in self.ENG}
        self.sems = {}
        self.cnt = {}
        self.waited = {n: {} for n in self.ENG}
        self.es = es
        for n in self.ENG:
            self.newsem(n)

    def newsem(self, key):
        self.sems[key] = self.es.enter_context(self.nc.semaphore("s_" + key))
        self.cnt[key] = 0

    def wait(self, eng, tok):
        key, val = tok
        if self.waited[eng].get(key, 0) >= val:
            return
        self.waited[eng][key] = val
        sem = self.sems[key]
        self.streams[eng].append(lambda e, sem=sem, val=val: e.wait_ge(sem, val))

    def _deps(self, eng, reads, writes, extra):
        deps = set()
        for r in reads:
            if r.w is not None:
                deps.add(r.w)
        for w in writes:
            if w.w is not None and w.w[0] != eng:
                deps.add(w.w)
            for t in w.r:
                if t[0] != eng:
                    deps.add(t)
        deps.update(extra)
        for t in sorted(deps):
            if eng == "pe" and t[0] == "pe":
                continue
            self.wait(eng, t)

    def _commit(self, tok, reads, writes):
        for r in reads:
            r.r.append(tok)
        for w in writes:
            w.w = tok
            w.r = []

    def op(self, eng, fns, reads=(), writes=(), extra=()):
        if not isinstance(fns, (list, tuple)):
            fns = [fns]
        self._deps(eng, reads, writes, extra)
        self.cnt[eng] += 1
        tok = (eng, self.cnt[eng])
        sem = self.sems[eng]
        st = self.streams[eng]
        for f in fns[:-1]:
            st.append(f)
        last = fns[-1]
        st.append(lambda e, last=last, sem=sem: last(e).then_inc(sem, 1))
        self._commit(tok, reads, writes)
        return tok

    def dma(self, eng, key, out, in_, reads=(), writes=(), extra=()):
        if key not in self.sems:
            self.newsem(key)
        self._deps(eng, reads, writes, extra)
        self.cnt[key] += 16
        tok = (key, self.cnt[key])
        sem = self.sems[key]
        self.streams[eng].append(lambda e, out=out, in_=in_, sem=sem: e.dma_start(out=out, in_=in_).then_inc(sem, 16))
        self._commit(tok, reads, writes)
        return tok

    def barrier(self):
        for e in self.ENG:
            for k, v in self.cnt.items():
                if k != e and v > 0:
                    self.wait(e, (k, v))

    def replay(self):
        with self.nc.Block() as block:
            @block.tensor
            def _(e):
                for f in self.streams["pe"]:
                    f(e)

            @block.scalar
            def _(e):
                for f in self.streams["act"]:
                    f(e)

            @block.vector
            def _(e):
                for f in self.streams["dve"]:
                    f(e)

            @block.gpsimd
            def _(e):
                for f in self.streams["pool"]:
                    f(e)

            @block.sync
            def _(e):
                for f in self.streams["sp"]:
                    f(e)


def run_interleaved(main, bg, ratio):
    n = 0
    bg_alive = bg is not None
    for _ in main:
        n += 1
        if bg_alive and n % ratio == 0:
            try:
                next(bg)
            except StopIteration:
                bg_alive = False
    if bg_alive:
        for _ in bg:
            pass


def build(dbg=None, nlayers=NL, branches=(0, 1, 2), do_ffn=True):
    nc = bass.Bass("TRN2", target_bir_lowering=False)
    es = ExitStack()
    fw = FW(nc, es)

    def dram(name, shape, kind="ExternalInput"):
        return nc.dram_tensor(name, shape, F32, kind=kind).ap()

    xT_d = dram("xT", [D, S])
    WTOT = NL * 159744
    wpack_d = dram("wpack", [128, WTOT])
    plan = []
    woff = [0]
    vecs_d = dram("vecs", [128, NV])
    cos_d = dram("cosT", [128, S])
    sin_d = dram("sinT", [128, S])
    cst_d = dram("cst", [128, 2048])
    out_d = dram("outT", [D, S], kind="ExternalOutput")

    def sb(name, shape, dt):
        return es.enter_context(nc.sbuf_tensor(name, shape, dt))

    xT = sb("xT_s", [128, 8, S], F32)
    hT = sb("hT_s", [128, 8, S], BF16)
    oT = sb("oT_s", [128, 4, S], BF16)
    vecs = sb("vecs_s", [128, NV], F32)
    esink = sb("esink", [128, 8 * NL], F32)
    cst = sb("cst_s", [128, 2048], BF16)
    arena = sb("arena", [128, 14 * 1024], F32)
    wring = [sb(f"wring{i}", [128, 3072], BF16) for i in range(3)]
    ropeb = [sb(f"rope{i}", [128, 2, 512], F32) for i in range(2)]
    banks = [es.enter_context(nc.psum_tensor(f"bank{i}", [128, 512], F32)) for i in range(8)]
    bres = [Res() for _ in range(8)]

    ident = cst[:, 0:128]
    tincl = cst[:, 128:256]
    tcomp = cst[:, 256:384]
    m_le = cst[:, 384:512]
    m_lt = cst[:, 512:640]
    m_gt = cst[:, 640:768]
    m_legt = cst[:, 768:1024]
    onesm = cst[:, 1024:1152]

    r_x = [[Res() for _ in range(4)] for _ in range(8)]
    r_h = [Res() for _ in range(4)]
    r_oT = [[Res() for _ in range(4)] for _ in range(4)]
    r_vecs = Res()
    r_cst = Res()
    r_esink = Res()
    r_wring = [Res() for _ in range(3)]
    r_rope = [Res() for _ in range(2)]
    wr_i = [0]
    last_w = [None]
    rp_i = [0]

    fw.dma("sp", "d_vecs", vecs[:, :], vecs_d[:, :], writes=[r_vecs])
    fw.dma("pool", "d_cst", cst[:, :], cst_d[:, :], writes=[r_cst])
    for c in range(8):
        for tb in range(4):
            fw.dma("sp", "d_x", xT[:, c, tb * 512:(tb + 1) * 512], xT_d[c * 128:(c + 1) * 128, tb * 512:(tb + 1) * 512],
                   writes=[r_x[c][tb]])
    for c in range(8):
        for tb in range(4):
            r_x[c][tb].w = ("d_x", fw.cnt["d_x"])
    for l in range(NL):
        fw.op("act", lambda e, l=l: e.activation(out=esink[:, l * 8:(l + 1) * 8], in_=vecs[:, l * LV + 216:l * LV + 224], func=AF.Exp),
              reads=[r_vecs], writes=[r_esink])

    def vcol(l, off, n=1):
        return vecs[:, l * LV + off:l * LV + off + n]

    def load_w(pieces):
        i = wr_i[0] % 3
        wr_i[0] += 1
        slot, res = wring[i], r_wring[i]
        views = []
        off = 0
        for (name, l, sub, (r0, r1), (c0, c1)) in pieces:
            kc = (r1 - r0) // 128
            n = c1 - c0
            plan.append((name, l, sub, r0, r1, c0, c1, woff[0] + off))
            views.append(slot[:, off:off + kc * n].rearrange("p (c n) -> p c n", c=kc))
            off += kc * n
        assert off <= 3072
        last_w[0] = fw.dma("pool", f"d_w{i}", slot[:, 0:off], wpack_d[:, woff[0]:woff[0] + off], writes=[res],
                           extra=([last_w[0]] if last_w[0] is not None else []))
        woff[0] += off
        return views, res

    bank_rr = [0]

    def next_bank(pool):
        b = pool[bank_rr[0] % len(pool)]
        bank_rr[0] += 1
        return b

    def dense(bk, cols, wv, wres, nkc, rhs_fn, rhs_res, wcol0=0, m=128):
        fns = []
        for kc in range(nkc):
            fns.append(lambda e, kc=kc: e.matmul(banks[bk][0:m, cols], lhsT=wv[:, kc, wcol0:wcol0 + m], rhs=rhs_fn(kc),
                                                 start=(kc == 0), stop=(kc == nkc - 1)))
        return fw.op("pe", fns, reads=[wres] + list(rhs_res), writes=[bres[bk]])

    def rmsnorm_to_hT(l, goff):
        sq = arena[:, 0:2048].bitcast(BF16).rearrange("p (c n) -> p c n", c=8)
        rsd = arena[:, 2048:2048 + 1024].rearrange("p (i n) -> p i n", i=2)
        r_sq = Res()
        r_rs = [Res(), Res()]
        for tb in range(4):
            cs = slice(tb * 512, (tb + 1) * 512)
            fw.op("act", lambda e, cs=cs: e.activation(out=sq, in_=xT[:, :, cs], func=AF.Square),
                  reads=[r_x[c][tb] for c in range(8)], writes=[r_sq])
            bk = next_bank(list(range(8)))
            fns = [lambda e, c=c, bk=bk: e.matmul(banks[bk][:, :], lhsT=onesm, rhs=sq[:, c, :], start=(c == 0), stop=(c == 7))
                   for c in range(8)]
            fw.op("pe", fns, reads=[r_sq, r_cst], writes=[bres[bk]])
            rs = rsd[:, tb % 2, :]
            fw.op("act", lambda e, bk=bk, rs=rs: e.activation(out=rs, in_=banks[bk][:, :], func=AF.Ln, bias=1e-6),
                  reads=[bres[bk]], writes=[r_rs[tb % 2]])
            fw.op("act", lambda e, rs=rs: e.activation(out=rs, in_=rs, func=AF.Exp, scale=-0.5),
                  reads=[r_rs[tb % 2]], writes=[r_rs[tb % 2]])
            fns = [lambda e, c=c, cs=cs, rs=rs: e.scalar_tensor_tensor(out=hT[:, c, cs], in0=xT[:, c, cs], scalar=vcol(l, goff + c),
                                                                       in1=rs, op0=ALU.mult, op1=ALU.mult) for c in range(8)]
            fw.op("dve", fns, reads=[r_x[c][tb] for c in range(8)] + [r_rs[tb % 2], r_vecs], writes=[r_h[tb]])

    def load_rope(tb):
        i = rp_i[0] % 2
        rp_i[0] += 1
        cs = slice(tb * 512, (tb + 1) * 512)
        fw.dma("sp", f"d_rp{i}", ropeb[i][:, 0, :], cos_d[:, cs], writes=[r_rope[i]])
        fw.dma("sp", f"d_rp{i}", ropeb[i][:, 1, :], sin_d[:, cs], writes=[r_rope[i]])
        return ropeb[i], r_rope[i]

    def rope_evac(bk, dst, dst_res, tb, t1, t2, r_t):
        rb, rres = load_rope(tb)
        fw.op("dve", lambda e: e.tensor_tensor(out=t1, in0=banks[bk][:, :], in1=rb[:, 0, :], op=ALU.mult),
              reads=[bres[bk], rres], writes=[r_t[0]])
        fns = []
        for g in range(4):
            pg = g ^ 1
            fns.append(lambda e, g=g, pg=pg: e.tensor_tensor(out=t2[g * 32:(g + 1) * 32, :], in0=banks[bk][pg * 32:(pg + 1) * 32, :],
                                                             in1=rb[g * 32:(g + 1) * 32, 1, :], op=ALU.mult))
        fw.op("dve", fns, reads=[bres[bk], rres], writes=[r_t[1]])
        fw.op("dve", lambda e: e.tensor_tensor(out=dst, in0=t1, in1=t2, op=ALU.add), reads=[r_t[0], r_t[1]], writes=[dst_res])

    def mixer_branch(l, br):
        A = arena
        kT = A[:, 0:1024].bitcast(BF16)
        qT = A[:, 1024:2048].bitcast(BF16)
        vaug = A[:, 2048:2048 + 1040].bitcast(BF16).rearrange("p (t h d) -> p t h d", t=16, h=2)
        off = 2048 + 1040
        t1 = A[:, off:off + 512]
        t2 = A[:, off + 512:off + 1024]
        off += 1024
        ebuf = [[A[:, off + (2 * s + h) * 512: off + (2 * s + h + 1) * 512] for h in range(2)] for s in range(3)]
        off += 3072
        exbuf = [[A[:, off + (2 * s + h) * 512: off + (2 * s + h + 1) * 512] for h in range(2)] for s in range(2)]
        off += 2048
        spb = [[A[:, off + (2 * s + h) * 256: off + (2 * s + h + 1) * 256].bitcast(BF16) for h in range(2)] for s in range(3)]
        off += 1536
        pb = [[A[:, off + (2 * s + h) * 256: off + (2 * s + h + 1) * 256].bitcast(BF16) for h in range(2)] for s in range(2)]
        off += 1024
        otok = A[:, off:off + 256].bitcast(BF16)
        off += 256
        mbt = A[:, off:off + 256].bitcast(BF16)
        off += 256
        mb = A[:, off:off + 256].bitcast(BF16)
        off += 256
        small = A[:, off:off + 256]
        off += 256
        qT2 = A[:, off:off + 1024].bitcast(BF16)
        off += 1024
        mb3 = A[:, off:off + 256].bitcast(BF16)
        off += 256
        assert off <= 14 * 1024, off
        gm = small[:, 0:64].rearrange("p (h i n) -> p h i n", h=2, i=4)
        mx = small[:, 64:72]
        selt = small[:, 72:80]
        den = small[:, 80:88].rearrange("p (h i) -> p h i", h=2)
        ksum = small[:, 96:104]
        kmf = small[:, 104:112]
        kmh = small[:, 112:116].bitcast(BF16)
        kml = small[:, 116:120].bitcast(BF16)

        r_kT = [Res() for _ in range(4)]
        r_qT = [Res() for _ in range(4)]
        r_v = [Res() for _ in range(4)]
        r_t = [Res(), Res()]
        r_e = [[Res(), Res()], [Res(), Res()], [Res(), Res()]]
        r_ex = [[Res(), Res()], [Res(), Res()]]
        r_sp = [[Res(), Res()], [Res(), Res()], [Res(), Res()]]
        r_p = [[Res(), Res()], [Res(), Res()]]
        r_otok, r_mbt, r_mb, r_small, r_km = Res(), Res(), Res(), Res(), Res()
        r_osb = [Res(), Res()]
        r_den = [Res(), Res()]

        if br == 0:
            units = [0]
        else:
            units = [0, 1, 2, 3]
        base = {0: 0, 1: 768, 2: 2304}[br]
        has_den = br != 1
        vw = 65 if has_den else 64

        if has_den:
            fw.op("dve", lambda e: e.memset(vaug[:, :, :, 64:65], 1.0), writes=r_v)
        if br == 2:
            fw.op("dve", lambda e: e.memset(mbt, 0.0), writes=[r_mbt])

        def proj_kv(wv, wres, kcol, vcol0):
            for tb in range(4):
                bk = next_bank([0, 1, 2, 3, 7])
                cs = slice(tb * 512, (tb + 1) * 512)
                dense(bk, slice(0, 512), wv, wres, 8, lambda kc, cs=cs: hT[:, kc, cs], [r_h[tb]], wcol0=kcol)
                if br == 1:
                    fw.op("act", lambda e, bk=bk, cs=cs: e.activation(out=kT[:, cs], in_=banks[bk][:, :], func=AF.Copy),
                          reads=[bres[bk]], writes=[r_kT[tb]])
                else:
                    rope_evac(bk, kT[:, cs], r_kT[tb], tb, t1, t2, r_t)
            for g4 in range(4):
                bk = next_bank([0, 1, 2, 3, 7])
                fns = []
                for j in range(4):
                    tt = g4 * 4 + j
                    for kc in range(8):
                        fns.append(lambda e, j=j, tt=tt, kc=kc, bk=bk: e.matmul(
                            banks[bk][:, j * 128:(j + 1) * 128], lhsT=hT[:, kc, tt * 128:(tt + 1) * 128],
                            rhs=wv[:, kc, vcol0:vcol0 + 128], start=(kc == 0), stop=(kc == 7)))
                fw.op("pe", fns, reads=[wres, r_h[g4]], writes=[bres[bk]])
                fw.op("act", lambda e, bk=bk, g4=g4: e.activation(
                    out=vaug[:, g4 * 4:(g4 + 1) * 4, :, 0:64],
                    in_=banks[bk][:, :].rearrange("p (t h d) -> p t h d", t=4, h=2), func=AF.Copy),
                    reads=[bres[bk]], writes=[r_v[g4]])

        def proj_q(wv, wres, qcol, Q):
            bk = next_bank([7])
            cs = slice(Q * 512, (Q + 1) * 512)
            dense(bk, slice(0, 512), wv, wres, 8, lambda kc: hT[:, kc, cs], [r_h[Q]], wcol0=qcol)
            if br == 1:
                fw.op("act", lambda e: e.activation(out=qT[:, cs], in_=banks[bk][:, :], func=AF.Copy),
                      reads=[bres[bk]], writes=[r_qT[Q]])
            else:
                rope_evac(bk, qT[:, cs], r_qT[Q], Q, t1, t2, r_t)

        def finish_o(obanks, Q, chunk, sink_cols):
            ot4 = otok.rearrange("p (i f) -> p i f", i=4)
            if has_den:
                for hh in range(2):
                    osb = A[:, 8192 + hh * 260: 8192 + (hh + 1) * 260]
                    fw.op("act", lambda e, hh=hh, osb=osb: e.activation(out=osb, in_=banks[obanks[hh]][:, 0:260], func=AF.Copy),
                          reads=[bres[obanks[hh]]], writes=[r_osb[hh]])
                for hh in range(2):
                    osb = A[:, 8192 + hh * 260: 8192 + (hh + 1) * 260]
                    o3 = osb.rearrange("p (i d) -> p i d", i=4)
                    if sink_cols is not None:
                        fw.op("dve", lambda e, hh=hh, o3=o3: e.tensor_scalar(out=den[:, hh, :], in0=o3[:, :, 64], scalar1=esink[:, sink_cols[hh]:sink_cols[hh] + 1],
                                                                            scalar2=None, op0=ALU.add),
                              reads=[r_osb[hh], r_esink], writes=[r_den[hh]])
                    else:
                        fw.op("dve", lambda e, hh=hh, o3=o3: e.tensor_copy(out=den[:, hh, :], in_=o3[:, :, 64]),
                              reads=[r_osb[hh]], writes=[r_den[hh]])
                    fw.op("dve", lambda e, hh=hh: e.reciprocal(out=den[:, hh, :], in_=den[:, hh, :]), reads=[r_den[hh]], writes=[r_den[hh]])
                    fw.op("dve", lambda e, hh=hh, o3=o3: e.tensor_tensor(out=ot4[:, :, hh * 64:(hh + 1) * 64], in0=o3[:, :, 0:64],
                                                                        in1=den[:, hh, :].unsqueeze(2).to_broadcast([128, 4, 64]), op=ALU.mult),
                          reads=[r_osb[hh], r_den[hh]], writes=[r_otok])
            else:
                fw.op("act", lambda e: e.activation(out=otok, in_=banks[obanks[0]][:, :], func=AF.Copy),
                      reads=[bres[obanks[0]]], writes=[r_otok])
            tb7 = banks[7][:, 0:256].bitcast(BF16)
            fns = [lambda e, i=i: e.transpose(tb7[:, i * 128:(i + 1) * 128], ot4[:, i, :], ident) for i in range(4)]
            fw.op("pe", fns, reads=[r_otok, r_cst], writes=[bres[7]])
            fw.op("dve", lambda e: e.tensor_copy(out=oT[:, chunk, Q * 512:(Q + 1) * 512], in_=tb7), reads=[bres[7]], writes=[r_oT[chunk][Q]])

        if br == 0:
            (wkv,), wres = load_w([("w_in", l, None, (0, 1024), (512, 768))])
            proj_kv(wkv, wres, 0, 128)
            qTs = [qT, qT2]
            r_qTs = [[Res() for _ in range(4)] for _ in range(2)]
            r_S = [[Res(), Res()], [Res(), Res()]]

            def proj_chunk(c):
                (wq,), wqres = load_w([("w_in", l, None, (0, 1024), (c * 128, (c + 1) * 128))])
                for Q in range(4):
                    bk = next_bank([6, 7])
                    cs = slice(Q * 512, (Q + 1) * 512)
                    dense(bk, slice(0, 512), wq, wqres, 8, lambda kc, cs=cs: hT[:, kc, cs], [r_h[Q]])
                    rope_evac(bk, qTs[c % 2][:, cs], r_qTs[c % 2][Q], Q, t1, t2, r_t)
                    yield

            def attn_chunk(c):
                qTc, r_q = qTs[c % 2], r_qTs[c % 2]
                steps = []
                for Q in range(4):
                    lst = list(range(max(0, 4 * Q - 1), 4 * Q + 4))
                    for si, a in enumerate(lst):
                        steps.append((Q, a, si == 0, si == len(lst) - 1))
                n_st = len(steps)

                def geom(Q, a):
                    i0 = a - 4 * Q
                    if i0 < 0:
                        return [0], m_gt
                    if i0 == 3:
                        return [3], m_le
                    return [i0, i0 + 1], m_legt

                def s_qk(idx):
                    Q, a, _, _ = steps[idx]
                    s = idx % 2
                    qt, msk = geom(Q, a)
                    n = 128 * len(qt)
                    qc = slice(Q * 512 + qt[0] * 128, Q * 512 + qt[0] * 128 + n)
                    sc = slice(0, n)
                    for hh in range(2):
                        rows = slice(hh * 64, (hh + 1) * 64)
                        fw.op("pe", lambda e, hh=hh, rows=rows, sc=sc, qc=qc, a=a, s=s: e.matmul(
                            banks[2 * hh + s][:, sc], lhsT=kT[rows, a * 128:(a + 1) * 128], rhs=qTc[rows, qc], start=True, stop=True),
                            reads=[r_kT[a // 4], r_q[Q]], writes=[bres[2 * hh + s]])
                    for hh in range(2):
                        P = pb[s][hh][:, 0:n]
                        fw.op("act", lambda e, hh=hh, sc=sc, P=P, s=s: e.activation(out=P, in_=banks[2 * hh + s][:, sc], func=AF.Exp, scale=0.125),
                              reads=[bres[2 * hh + s]], writes=[r_p[s][hh]])
                        fw.op("dve", lambda e, P=P, msk=msk, n=n: e.tensor_tensor(out=P, in0=P, in1=msk[:, 0:n], op=ALU.mult),
                              reads=[r_p[s][hh], r_cst], writes=[r_p[s][hh]])

                def s_pv(idx):
                    Q, a, isfirst, islast = steps[idx]
                    s = idx % 2
                    qt, msk = geom(Q, a)
                    ob = [4, 5]
                    for hh in range(2):
                        P = pb[s][hh]
                        fns = []
                        for j, i in enumerate(qt):
                            fns.append(lambda e, hh=hh, j=j, i=i, P=P, a=a, st=(isfirst and j == 0), ob=ob: e.matmul(
                                banks[ob[hh]][:, i * 65:(i + 1) * 65], lhsT=P[:, j * 128:(j + 1) * 128], rhs=vaug[:, a, hh, :],
                                start=st, stop=False, skip_group_check=True))
                        fw.op("pe", fns, reads=[r_p[s][hh], r_v[a // 4]], writes=[bres[ob[hh]]])
                    if islast:
                        finish_o(ob, Q, c, [l * 8 + c, l * 8 + 4 + c])

                s_qk(0)
                for idx in range(n_st):
                    if idx + 1 < n_st:
                        s_qk(idx + 1)
                    s_pv(idx)
                    yield

            for _ in proj_chunk(0):
                pass
            for c in range(4):
                run_interleaved(attn_chunk(c), proj_chunk(c + 1) if c + 1 < 4 else None, 4)
            return

        ealt = ebuf[0][0]
        alt0 = 2048 + 1040 + 1024
        kT_b = A[:, alt0:alt0 + 1024].bitcast(BF16)
        qT_b = A[:, alt0 + 1024:alt0 + 2048].bitcast(BF16)
        vaug_b = A[:, alt0 + 2048:alt0 + 2048 + 1040].bitcast(BF16).rearrange("p (t h d) -> p t h d", t=16, h=2)
        mb_b = [A[:, alt0 + 3088 + i * 256: alt0 + 3088 + (i + 1) * 256].bitcast(BF16) for i in range(2)]
        BUFS = [
            dict(kT=kT, qT=qT, vaug=vaug, kmh=kmh, kml=kml, mbs=[mb, mb3], r_kT=r_kT, r_qT=r_qT, r_v=r_v, r_km=r_km, r_mbs=[Res(), Res()]),
            dict(kT=kT_b, qT=qT_b, vaug=vaug_b, kmh=small[:, 120:124].bitcast(BF16), kml=small[:, 124:128].bitcast(BF16), mbs=mb_b,
                 r_kT=[Res() for _ in range(4)], r_qT=[Res() for _ in range(4)], r_v=[Res() for _ in range(4)], r_km=Res(), r_mbs=[Res(), Res()]),
        ]
        if br == 2:
            fw.op("dve", lambda e: e.memset(vaug_b[:, :, :, 64:65], 1.0), writes=BUFS[1]["r_v"])
        pj_banks = [6, 7] if br == 2 else [0, 1, 2, 3, 7]

        def proj_unit(u, B):
            kT, qT, vaug, kmh, kml, mbs = B["kT"], B["qT"], B["vaug"], B["kmh"], B["kml"], B["mbs"]
            r_kT, r_qT, r_v, r_km, r_mbs = B["r_kT"], B["r_qT"], B["r_v"], B["r_km"], B["r_mbs"]
            qc0 = base + u * 128
            kc0 = base + 512 + u * 128
            vc0 = base + 1024 + u * 128
            (wq, wk, wvv), wres = load_w([("w_in", l, None, (0, 1024), (qc0, qc0 + 128)), ("w_in", l, None, (0, 1024), (kc0, kc0 + 128)), ("w_in", l, None, (0, 1024), (vc0, vc0 + 128))])
            for tb in range(4):
                bk = next_bank(pj_banks)
                cs = slice(tb * 512, (tb + 1) * 512)
                dense(bk, slice(0, 512), wk, wres, 8, lambda kc, cs=cs: hT[:, kc, cs], [r_h[tb]])
                if br == 1:
                    fw.op("act", lambda e, bk=bk, cs=cs: e.activation(out=kT[:, cs], in_=banks[bk][:, :], func=AF.Copy),
                          reads=[bres[bk]], writes=[r_kT[tb]])
                else:
                    rope_evac(bk, kT[:, cs], r_kT[tb], tb, t1, t2, r_t)
                yield
            for g4 in range(4):
                bk = next_bank(pj_banks)
                fns = []
                for j in range(4):
                    tt = g4 * 4 + j
                    for kc in range(8):
                        fns.append(lambda e, j=j, tt=tt, kc=kc, bk=bk, wvv=wvv: e.matmul(
                            banks[bk][:, j * 128:(j + 1) * 128], lhsT=hT[:, kc, tt * 128:(tt + 1) * 128],
                            rhs=wvv[:, kc, :], start=(kc == 0), stop=(kc == 7)))
                fw.op("pe", fns, reads=[wres, r_h[g4]], writes=[bres[bk]])
                fw.op("act", lambda e, bk=bk, g4=g4: e.activation(
                    out=vaug[:, g4 * 4:(g4 + 1) * 4, :, 0:64],
                    in_=banks[bk][:, :].rearrange("p (t h d) -> p t h d", t=4, h=2), func=AF.Copy),
                    reads=[bres[bk]], writes=[r_v[g4]])
            if br == 2:
                fw.op("dve", lambda e: e.tensor_reduce(out=ksum, in_=kT.rearrange("p (n k) -> p n k", n=8), axis=AX.X, op=ALU.add),
                      reads=r_kT, writes=[r_small])
                fw.op("dve", lambda e: e.tensor_scalar(out=kmf, in0=ksum, scalar1=1.0 / 256, scalar2=None, op0=ALU.mult),
                      reads=[r_small], writes=[r_small])
                fw.op("dve", lambda e: e.tensor_copy(out=kmh, in_=kmf), reads=[r_small], writes=[r_km])
                fw.op("dve", lambda e: e.tensor_tensor(out=kml, in0=kmf, in1=kmh, op=ALU.subtract), reads=[r_small, r_km], writes=[r_km])


            for Q in range(4):
                bk = next_bank([7] if br == 1 else [6, 7])
                cs = slice(Q * 512, (Q + 1) * 512)
                dense(bk, slice(0, 512), wq, wres, 8, lambda kc, cs=cs: hT[:, kc, cs], [r_h[Q]])
                if br == 1:
                    fw.op("act", lambda e, bk=bk, cs=cs: e.activation(out=qT[:, cs], in_=banks[bk][:, :], func=AF.Copy),
                          reads=[bres[bk]], writes=[r_qT[Q]])
                else:
                    rope_evac(bk, qT[:, cs], r_qT[Q], Q, t1, t2, r_t)
                yield
            for Q in range(4):
                yield
                need_sel = (br == 2 and Q >= 2)
                mb = mbs[Q % 2]
                r_mb = r_mbs[Q % 2]
                if need_sel:
                    gb = [5, 6]
                    gb = [6, 7]
                    for hh in range(2):
                        rows = slice(hh * 64, (hh + 1) * 64)
                        fns = []
                        for i in range(4):
                            qi = slice(Q * 512 + i * 128, Q * 512 + (i + 1) * 128)
                            fns.append(lambda e, hh=hh, i=i, qi=qi, rows=rows: e.matmul(banks[gb[hh]][:, i * 8:(i + 1) * 8], lhsT=qT[rows, qi], rhs=kmh[rows, :], start=True, stop=False))
                            fns.append(lambda e, hh=hh, i=i, qi=qi, rows=rows: e.matmul(banks[gb[hh]][:, i * 8:(i + 1) * 8], lhsT=qT[rows, qi], rhs=kml[rows, :], start=False, stop=True))
                        fw.op("pe", fns, reads=[r_qT[Q], r_km], writes=[bres[gb[hh]]])
                    fw.op("dve", lambda e: e.memset(gm, -1e30), writes=[r_small])
                    for hh in range(2):
                        g3 = banks[gb[hh]][:, 0:32].rearrange("p (i n) -> p i n", i=4)
                        fw.op("dve", [lambda e, hh=hh, g3=g3, Q=Q: e.tensor_copy(out=gm[:, hh, 0:2, 0:2 * Q], in_=g3[:, 0:2, 0:2 * Q]),
                                      lambda e, hh=hh, g3=g3, Q=Q: e.tensor_copy(out=gm[:, hh, 2:4, 0:2 * Q + 1], in_=g3[:, 2:4, 0:2 * Q + 1])],
                              reads=[bres[gb[hh]]], writes=[r_small])
                    mbt4 = mbt.rearrange("p (i f) -> p i f", i=4)
                    mx_all = small[:, 128:192].rearrange("p (g n) -> p g n", g=8)
                    sel_all = small[:, 192:256].rearrange("p (h i n) -> p h i n", h=2, i=4)
                    gm8 = small[:, 0:64].rearrange("p (g n) -> p g n", g=8)
                    fns = [lambda e, g=g: e.max(out=mx_all[:, g, :], in_=gm8[:, g, :]) for g in range(8)]
                    fw.op("dve", fns, reads=[r_small], writes=[r_small])
                    fw.op("dve", lambda e: e.tensor_tensor(out=small[:, 192:256].rearrange("p (g n) -> p g n", g=8), in0=gm8,
                                                           in1=mx_all[:, :, 2:3].to_broadcast([128, 8, 8]), op=ALU.is_ge),
                          reads=[r_small], writes=[r_small])
                    fns = []
                    for hh in range(2):
                        fns.append(lambda e, hh=hh: e.tensor_scalar(out=mbt4[:, :, hh * 64:hh * 64 + 8], in0=sel_all[:, hh, :, :], scalar1=-1.0, scalar2=-NEG,
                                                                    op0=ALU.add, op1=ALU.mult))
                    for hh in range(2):
                        for ip in range(2):
                            nb = 2 * Q + ip
                            fns.append(lambda e, hh=hh, ip=ip, nb=nb: e.memset(mbt4[:, 2 * ip:2 * ip + 2, hh * 64 + nb:hh * 64 + nb + 1], 0.0))
                    fw.op("dve", fns, reads=[r_small], writes=[r_mbt])
                    tb7 = banks[7][:, 0:256].bitcast(BF16)
                    fns = [lambda e, i=i: e.transpose(tb7[:, i * 128:(i + 1) * 128], mbt4[:, i, :], ident) for i in range(4)]
                    fw.op("pe", fns, reads=[r_mbt, r_cst], writes=[bres[7]])
                    fw.op("dve", lambda e, mb=mb, tb7=tb7: e.tensor_copy(out=mb, in_=tb7), reads=[bres[7]], writes=[r_mb])


        def attn_unit(u, B):
            kT, qT, vaug, kmh, kml, mbs = B["kT"], B["qT"], B["vaug"], B["kmh"], B["kml"], B["mbs"]
            r_kT, r_qT, r_v, r_km, r_mbs = B["r_kT"], B["r_qT"], B["r_v"], B["r_km"], B["r_mbs"]
            for Q in range(4):
                qcs = slice(Q * 512, (Q + 1) * 512)
                need_sel = (br == 2 and Q >= 2)
                mb = mbs[Q % 2]
                r_mb = r_mbs[Q % 2]
                steps = list(range(4 * Q + 3, -1, -1))
                ns = len(steps)
                if br == 1:
                    ob = [6, 6]
                    xb = [4, 5]
                else:
                    ob = [4, 5]

                def cols_of(a):
                    i0 = max(0, a - 4 * Q)
                    return i0, slice(i0 * 128, 512)

                def stage_qk(t):
                    a = steps[t]
                    s = t % 2
                    i0, cl = cols_of(a)
                    qcl = slice(Q * 512 + i0 * 128, (Q + 1) * 512)
                    for hh in range(2):
                        rows = slice(hh * 64, (hh + 1) * 64)
                        bk = 2 * hh + s
                        msk = need_sel and a < 4 * Q + 2
                        fns = [lambda e, a=a, rows=rows, bk=bk, cl=cl, qcl=qcl, msk=msk: e.matmul(
                            banks[bk][:, cl], lhsT=kT[rows, a * 128:(a + 1) * 128], rhs=qT[rows, qcl], start=True, stop=not msk)]
                        rd = [r_kT[a // 4], r_qT[Q]]
                        if msk:
                            n = a // 2
                            fns.append(lambda e, rows=rows, bk=bk, cl=cl, n=n, mb=mb: e.matmul(
                                banks[bk][:, cl], lhsT=cst[rows, 1152 + n * 128:1152 + (n + 1) * 128], rhs=mb[rows, cl], start=False, stop=True))
                            rd += [r_mb, r_cst]
                        fw.op("pe", fns, reads=rd, writes=[bres[bk]])
                        if br == 1:
                            s3 = t % 3
                            E = ebuf[s3][hh]
                            SP = spb[s3][hh]
                            fw.op("act", lambda e, bk=bk, cl=cl, E=E: e.activation(out=E[:, cl], in_=banks[bk][:, cl], func=AF.Exp, scale=0.125),
                                  reads=[bres[bk]], writes=[r_e[s3][hh]])
                            fw.op("act", lambda e, cl=cl, E=E, SP=SP: e.activation(out=SP[:, cl], in_=E[:, cl], func=AF.Ln, bias=1.0),
                                  reads=[r_e[s3][hh]], writes=[r_sp[s3][hh]])
                            if a >= 4 * Q:
                                dc = slice(i0 * 128, (i0 + 1) * 128)
                                fw.op("dve", lambda e, SP=SP, dc=dc: e.tensor_tensor(out=SP[:, dc], in0=SP[:, dc], in1=m_lt, op=ALU.mult),
                                      reads=[r_sp[s3][hh], r_cst], writes=[r_sp[s3][hh]])
                        else:
                            P = pb[s][hh]
                            fw.op("act", lambda e, bk=bk, cl=cl, P=P: e.activation(out=P[:, cl], in_=banks[bk][:, cl], func=AF.Exp, scale=0.125),
                                  reads=[bres[bk]], writes=[r_p[s][hh]])
                            if a >= 4 * Q:
                                dc = slice(i0 * 128, (i0 + 1) * 128)
                                fw.op("dve", lambda e, P=P, dc=dc: e.tensor_tensor(out=P[:, dc], in0=P[:, dc], in1=m_le, op=ALU.mult),
                                      reads=[r_p[s][hh], r_cst], writes=[r_p[s][hh]])

                def stage_xa(t):
                    a = steps[t]
                    s = t % 2
                    s3 = t % 3
                    i0, cl = cols_of(a)
                    last = (t == ns - 1)
                    for hh in range(2):
                        SP = spb[s3][hh]
                        fw.op("pe", lambda e, hh=hh, cl=cl, SP=SP, t=t, last=last, xb=xb: e.matmul(
                            banks[xb[hh]][:, cl], lhsT=tincl, rhs=SP[:, cl], start=(t == 0), stop=last, skip_group_check=True),
                            reads=[r_sp[s3][hh], r_cst], writes=[bres[xb[hh]]])
                    for hh in range(2):
                        EX = exbuf[s][hh]
                        fw.op("act", lambda e, hh=hh, cl=cl, EX=EX, xb=xb: e.activation(out=EX[:, cl], in_=banks[xb[hh]][:, cl], func=AF.Exp),
                              reads=[bres[xb[hh]]], writes=[r_ex[s][hh]])

                def stage_xb(t):
                    a = steps[t]
                    s = t % 2
                    s3 = t % 3
                    i0, cl = cols_of(a)
                    last = (t == ns - 1)
                    if not last:
                        for hh in range(2):
                            SP = spb[s3][hh]
                            fw.op("pe", lambda e, hh=hh, cl=cl, SP=SP, xb=xb: e.matmul(
                                banks[xb[hh]][:, cl], lhsT=tcomp, rhs=SP[:, cl], start=False, stop=False, skip_group_check=True),
                                reads=[r_sp[s3][hh], r_cst], writes=[bres[xb[hh]]])
                    for hh in range(2):
                        E, EX, W = ebuf[s3][hh], exbuf[s][hh], pb[s][hh]
                        fw.op("dve", lambda e, cl=cl, E=E, EX=EX, W=W: e.tensor_tensor(out=W[:, cl], in0=E[:, cl], in1=EX[:, cl], op=ALU.mult),
                              reads=[r_e[s3][hh], r_ex[s][hh]], writes=[r_p[s][hh]])
                        if a >= 4 * Q:
                            dc = slice(i0 * 128, (i0 + 1) * 128)
                            fw.op("dve", lambda e, W=W, dc=dc: e.tensor_tensor(out=W[:, dc], in0=W[:, dc], in1=m_lt, op=ALU.mult),
                                  reads=[r_p[s][hh], r_cst], writes=[r_p[s][hh]])

                def stage_pv(t):
                    a = steps[t]
                    s = t % 2
                    i0, cl = cols_of(a)
                    for hh in range(2):
                        P = pb[s][hh]
                        fns = []
                        for i in range(i0, 4):
                            if br == 1:
                                oc = slice(i * 128 + hh * 64, i * 128 + hh * 64 + 64)
                                st = (t == 0 and hh == 0 and i == i0)
                                rhs = vaug[:, a, hh, 0:64]
                            else:
                                oc = slice(i * 65, (i + 1) * 65)
                                st = (t == 0 and i == i0)
                                rhs = vaug[:, a, hh, :]
                            fns.append(lambda e, hh=hh, i=i, oc=oc, st=st, rhs=rhs, P=P, ob=ob: e.matmul(
                                banks[ob[hh]][:, oc], lhsT=P[:, i * 128:(i + 1) * 128], rhs=rhs, start=st, stop=False, skip_group_check=True))
                        fw.op("pe", fns, reads=[r_p[s][hh], r_v[a // 4]], writes=[bres[ob[hh]]])

                if br == 1:
                    stage_qk(0)
                    if ns > 1:
                        stage_qk(1)
                    for t in range(ns):
                        stage_xa(t)
                        if t >= 1:
                            stage_pv(t - 1)
                        if t + 2 < ns:
                            stage_qk(t + 2)
                        stage_xb(t)
                        yield
                    stage_pv(ns - 1)
                else:
                    stage_qk(0)
                    for t in range(ns):
                        if t + 1 < ns:
                            stage_qk(t + 1)
                        stage_pv(t)
                        yield
                finish_o(ob, Q, u, None)

        if br == 2:
            for _ in proj_unit(0, BUFS[0]):
                pass
            for u in units:
                run_interleaved(attn_unit(u, BUFS[u % 2]), proj_unit(u + 1, BUFS[(u + 1) % 2]) if u + 1 < 4 else None, 2)
        else:
            for u in units:
                for _ in proj_unit(u, BUFS[0]):
                    pass
                for _ in attn_unit(u, BUFS[0]):
                    pass

    def post_branch(l, br):
        mT = arena[:, 0:8192].bitcast(BF16).rearrange("p (c n) -> p c n", c=8)
        sg = [arena[:, 8192 + i * 512: 8192 + (i + 1) * 512] for i in range(2)]
        r_sg = [Res(), Res()]
        r_m = [[Res() for _ in range(4)] for _ in range(8)]
        k = 0
        for j in range(8):
            gcol = 3840 + br * 1024 + j * 128
            (wg, wb), wres = load_w([("w_in", l, None, (0, 1024), (gcol, gcol + 128)), ("w_br", l, br, (0, 512), (j * 128, (j + 1) * 128))])
            wres2 = wres
            for tb in range(4):
                cs = slice(tb * 512, (tb + 1) * 512)
                bg = next_bank(list(range(8)))
                dense(bg, slice(0, 512), wg, wres, 8, lambda kc, cs=cs: hT[:, kc, cs], [r_h[tb]])
                by = next_bank(list(range(8)))
                dense(by, slice(0, 512), wb, wres2, 4, lambda kc, cs=cs: oT[:, kc, cs], [r_oT[c][tb] for c in range(4)])
                s = k % 2
                k += 1
                fw.op("act", lambda e, bg=bg, s=s, j=j: e.activation(out=sg[s], in_=banks[bg][:, :], func=AF.Sigmoid, bias=vcol(l, 16 + br * 8 + j)),
                      reads=[bres[bg], r_vecs], writes=[r_sg[s]])
                fw.op("dve", lambda e, by=by, s=s, j=j, cs=cs: e.tensor_tensor(out=mT[:, j, cs], in0=banks[by][:, :], in1=sg[s], op=ALU.mult),
                      reads=[bres[by], r_sg[s]], writes=[r_m[j][tb]])
        for j in range(8):
            (wo,), wres = load_w([("w_out", l, None, (0, 1024), (j * 128, (j + 1) * 128))])
            for tb in range(4):
                cs = slice(tb * 512, (tb + 1) * 512)
                bk = next_bank(list(range(8)))
                dense(bk, slice(0, 512), wo, wres, 8, lambda kc, cs=cs: mT[:, kc, cs], [r_m[c][tb] for c in range(8)])
                fw.op("dve", lambda e, bk=bk, j=j, cs=cs: e.tensor_tensor(out=xT[:, j, cs], in0=banks[bk][:, :], in1=xT[:, j, cs], op=ALU.add),
                      reads=[bres[bk]], writes=[r_x[j][tb]])

    def ffn(l):
        actT = arena[:, 0:11 * 1024].bitcast(BF16).rearrange("p (c n) -> p c n", c=11)
        rb_ = [[oT[:, (2 * s + z) // 2, ((2 * s + z) % 2) * 1024:((2 * s + z) % 2 + 1) * 1024].bitcast(F32) for z in range(2)] for s in range(4)]
        r_rb = [[Res(), Res()] for _ in range(4)]
        tiles = [(0, 512), (512, 510), (1022, 510), (1532, 510), (2042, 6)]
        allh = list(r_h)
        k = [0]
        stages = []
        for half in range(2):
            for f in range(11):
                fc = half * 11 + f
                stages.append(("up", half, f, [("w_up", l, None, (0, 1024), (fc * 128, (fc + 1) * 128)),
                                                ("w_up", l, None, (0, 1024), (2816 + fc * 128, 2816 + (fc + 1) * 128))]))
            for j in range(8):
                stages.append(("dn", half, j, [("w_dn", l, None, (half * 1408, (half + 1) * 1408), (j * 128, (j + 1) * 128))]))
        loaded = [None] * len(stages)
        loaded[0] = load_w(stages[0][3])
        r_act = [Res() for _ in range(11)]
        pend = []

        def flush(keep):
            while len(pend) > keep:
                (s, f, t0, n, ra) = pend.pop(0)
                Ra, Rv = rb_[s][0], rb_[s][1]
                fw.op("act", lambda e, Ra=Ra, n=n: e.activation(out=Ra[:, 0:n], in_=Ra[:, 0:n], func=AF.Silu), reads=[r_rb[s][0]], writes=[r_rb[s][0]])
                fw.op("pool", lambda e, Ra=Ra, Rv=Rv, f=f, t0=t0, n=n: e.tensor_tensor(out=actT[:, f, t0:t0 + n], in0=Ra[:, 0:n], in1=Rv[:, 0:n], op=ALU.mult),
                      reads=[r_rb[s][0], r_rb[s][1]], writes=[ra[f]])

        for si, (kind, half, idx, _) in enumerate(stages):
            if si + 1 < len(stages):
                loaded[si + 1] = load_w(stages[si + 1][3])
            views, wres = loaded[si]
            if kind == "up":
                f = idx
                fc = half * 11 + f
                wa, wvv = views
                for (t0, n) in tiles:
                    s = k[0] % 4
                    k[0] += 1
                    info = []
                    for z, wz in enumerate((wa, wvv)):
                        ch = fc + z * 22
                        bk = next_bank(list(range(8)))
                        if t0 == 0:
                            cs = slice(0, 512)
                            N = 512
                        else:
                            cs = slice(t0 - 2, t0 + n)
                            N = n + 2
                        dense(bk, slice(0, N), wz, wres, 8, lambda kc, cs=cs: hT[:, kc, cs], allh)
                        info.append((z, ch, bk))
                    for (z, ch, bk) in info:
                        R = rb_[s][z]
                        src = banks[bk][:, 0:512] if t0 == 0 else banks[bk][:, 2:n + 2]
                        fw.op("act", lambda e, R=R, src=src, ch=ch, n=n: e.activation(out=R[:, 0:n], in_=src, func=AF.Identity,
                                                                                   scale=vcol(l, 40 + 2 * 44 + ch), bias=vcol(l, 172 + ch)),
                              reads=[bres[bk], r_vecs], writes=[r_rb[s][z]])
                    for tap, off in ((1, 1), (0, 2)):
                        for (z, ch, bk) in info:
                            R = rb_[s][z]
                            if t0 == 0:
                                dst = R[:, off:512]
                                src = banks[bk][:, 0:512 - off]
                            else:
                                dst = R[:, 0:n]
                                src = banks[bk][:, 2 - off:2 - off + n]
                            fw.op("dve", lambda e, dst=dst, src=src, ch=ch, tap=tap: e.scalar_tensor_tensor(
                                out=dst, in0=src, scalar=vcol(l, 40 + tap * 44 + ch), in1=dst, op0=ALU.mult, op1=ALU.add),
                                reads=[bres[bk], r_rb[s][z], r_vecs], writes=[r_rb[s][z]])
                    pend.append((s, f, t0, n, r_act))
                    flush(2)
            else:
                j = idx
                if j == 0:
                    flush(0)
                (wd,) = views
                for tb in range(4):
                    cs = slice(tb * 512, (tb + 1) * 512)
                    bk = next_bank(list(range(8)))
                    dense(bk, slice(0, 512), wd, wres, 11, lambda kc, cs=cs: actT[:, kc, cs], r_act)
                    fw.op("dve", lambda e, bk=bk, j=j, cs=cs: e.tensor_tensor(out=xT[:, j, cs], in0=banks[bk][:, :], in1=xT[:, j, cs], op=ALU.add),
                          reads=[bres[bk]], writes=[r_x[j][tb]])
        fw.barrier()

    for l in range(nlayers):
        rmsnorm_to_hT(l, 0)
        fw.barrier()
        for br in branches:
            mixer_branch(l, br)
            fw.barrier()
            if dbg == "oT":
                break
            post_branch(l, br)
            fw.barrier()
        if dbg == "oT":
            break
        if do_ffn:
            rmsnorm_to_hT(l, 8)
            fw.barrier()
            ffn(l)
            fw.barrier()

    r_out = Res()
    if dbg == "oT":
        for c in range(4):
            fw.dma("pool", "d_out", out_d[c * 128:(c + 1) * 128, :], oT[:, c, :], reads=[r_oT[c][tb] for tb in range(4)], writes=[r_out])
        fw.wait("sp", r_out.w)
    elif dbg == "x":
        for c in range(8):
            fw.dma("sp", "d_out", out_d[c * 128:(c + 1) * 128, :], xT[:, c, :], reads=[r_x[c][tb] for tb in range(4)], writes=[r_out])
    else:
        sq = arena[:, 0:2048].bitcast(BF16).rearrange("p (c n) -> p c n", c=8)
        rsd = arena[:, 2048:2048 + 512]
        ob = [arena[:, 4096 + i * 4096: 4096 + (i + 1) * 4096].rearrange("p (c n) -> p c n", c=8) for i in range(2)]
        r_sq, r_rs, r_ob = Res(), Res(), [Res(), Res()]
        goff = NL * LV
        for tb in range(4):
            cs = slice(tb * 512, (tb + 1) * 512)
            fw.op("act", lambda e, cs=cs: e.activation(out=sq, in_=xT[:, :, cs], func=AF.Square),
                  reads=[r_x[c][tb] for c in range(8)], writes=[r_sq])
            bk = next_bank(list(range(8)))
            fns = [lambda e, c=c, bk=bk: e.matmul(banks[bk][:, :], lhsT=onesm, rhs=sq[:, c, :], start=(c == 0), stop=(c == 7)) for c in range(8)]
            fw.op("pe", fns, reads=[r_sq, r_cst], writes=[bres[bk]])
            fw.op("act", lambda e, bk=bk: e.activation(out=rsd, in_=banks[bk][:, :], func=AF.Ln, bias=1e-6), reads=[bres[bk]], writes=[r_rs])
            fw.op("act", lambda e: e.activation(out=rsd, in_=rsd, func=AF.Exp, scale=-0.5), reads=[r_rs], writes=[r_rs])
            O = ob[tb % 2]
            fns = [lambda e, c=c, cs=cs, O=O: e.scalar_tensor_tensor(out=O[:, c, :], in0=xT[:, c, cs], scalar=vecs[:, goff + c:goff + c + 1],
                                                                    in1=rsd, op0=ALU.mult, op1=ALU.mult) for c in range(8)]
            fw.op("dve", fns, reads=[r_x[c][tb] for c in range(8)] + [r_rs, r_vecs], writes=[r_ob[tb % 2]])
            fw.dma("sp", "d_out", out_d.rearrange("(c p) n -> p c n", p=128)[:, :, cs], O, reads=[r_ob[tb % 2]], writes=[r_out])
    fw.wait("sp", r_out.w)
    fw.barrier()
    fw.replay()
    es.close()
    build.last_plan = plan
    return nc


def host_consts():
    cst = np.zeros((128, 2048), np.float32)
    j = np.arange(128)[:, None]
    s = np.arange(128)[None, :]
    cst[:, 0:128] = np.eye(128)
    cst[:, 128:256] = -1.0 * (j >= s)
    cst[:, 256:384] = -1.0 * (j < s)
    cst[:, 384:512] = (j <= s)
    cst[:, 512:640] = (j < s)
    cst[:, 640:768] = (j > s)
    cst[:, 768:896] = (j <= s)
    cst[:, 896:1024] = (j > s)
    cst[:, 1024:1152] = 1.0 / 1024
    for n in range(7):
        cst[n, 1152 + n * 128:1152 + (n + 1) * 128] = 1.0
        cst[64 + n, 1152 + n * 128:1152 + (n + 1) * 128] = 1.0
    inv = (10000.0 ** (-(np.arange(0, 64, 2, dtype=np.float32)) / np.float32(64))).astype(np.float32)
    ang = (np.arange(S, dtype=np.float32)[None, :] * inv[:, None]).astype(np.float32)
    cos, sin = np.cos(ang).astype(np.float32), np.sin(ang).astype(np.float32)
    cosT = np.tile(cos, (4, 1))
    sinT = np.concatenate([-sin, sin, -sin, sin], 0)
    return cst, np.ascontiguousarray(cosT), np.ascontiguousarray(sinT)


def host_prep(inputs, plan):
    f = lambda a: np.ascontiguousarray(np.asarray(a, dtype=np.float32))
    w_in = f(inputs["w_in"]).copy()
    perm = []
    for c in range(4):
        perm += list(range(c * 64, c * 64 + 64)) + list(range((4 + c) * 64, (4 + c) * 64 + 64))
    w_in[:, :, 0:512] = w_in[:, :, perm]
    w_br = f(inputs["w_branch"]).copy()
    w_br[:, 0] = w_br[:, 0][:, perm, :]
    vecs = np.zeros((128, NV), np.float32)
    pc = lambda v: np.asarray(v, np.float32).reshape(-1, 128).T
    for l in range(NL):
        b = l * LV
        vecs[:, b:b + 8] = pc(inputs["norm_mix"][l])
        vecs[:, b + 8:b + 16] = pc(inputs["norm_ffn"][l])
        vecs[:, b + 16:b + 40] = pc(inputs["b_gate"][l])
        cw = np.asarray(inputs["conv_w"][l], np.float32)
        for tap in range(3):
            vecs[:, b + 40 + tap * 44:b + 40 + (tap + 1) * 44] = pc(cw[tap])
        vecs[:, b + 172:b + 216] = pc(inputs["conv_b"][l])
        vecs[:, b + 216:b + 224] = np.asarray(inputs["sinks"][l], np.float32)[None, :]
    vecs[:, NL * LV:NL * LV + 8] = pc(inputs["norm_final"])
    cst, cosT, sinT = host_consts()
    srcs = {"w_in": w_in, "w_br": w_br, "w_out": f(inputs["w_out"]), "w_up": f(inputs["w_up"]), "w_dn": f(inputs["w_down"])}
    wpack = np.zeros((128, NL * 159744), np.float32)
    for (name, l, sub, r0, r1, c0, c1, off) in plan:
        A = srcs[name][l] if sub is None else srcs[name][l][sub]
        A = A[r0:r1, c0:c1]
        kc, n = (r1 - r0) // 128, c1 - c0
        wpack[:, off:off + kc * n] = A.reshape(kc, 128, n).transpose(1, 0, 2).reshape(128, kc * n)
    shared = {"wpack": wpack, "vecs": vecs, "cosT": cosT, "sinT": sinT, "cst": cst}
    x = np.asarray(inputs["x"], np.float32)
    in_maps = []
    for b in range(8):
        m = dict(shared)
        m["xT"] = np.ascontiguousarray(x[b].T)
        in_maps.append(m)
    return in_maps


_NC = {}


def kernel(**inputs):
    if "nc" not in _NC:
        _NC["nc"] = build()
        _NC["plan"] = build.last_plan
    nc = _NC["nc"]
    in_maps = host_prep(inputs, _NC["plan"])
    res = run_bass_kernel_spmd(nc, in_maps, core_ids=list(range(8)))
    out = np.stack([np.ascontiguousarray(res.results[b]["outT"].T) for b in range(8)], 0)
    return out.astype(np.float32)
```

```python
from contextlib import ExitStack
import numpy as np
import concourse.bass as bass
import concourse.mybir as mybir
from concourse.bass_utils import run_bass_kernel_spmd

F32 = mybir.dt.float32
BF16 = mybir.dt.bfloat16
AF = mybir.ActivationFunctionType
ALU = mybir.AluOpType
AX = mybir.AxisListType

S = 2048
D = 1024
NL = 2
LV = 224
NV = NL * LV + 8
NEG = -30000.0


class Res:
    __slots__ = ("w", "r")

    def __init__(self):
        self.w = None
        self.r = []


class FW:
    ENG = ("pe", "act", "dve", "pool", "sp")

    def __init__(self, nc, es):
        self.nc = nc
        self.streams = {n: [] for n in self.ENG}
        self.sems = {}
        self.cnt = {}
        self.waited = {n: {} for n in self.ENG}
        self.es = es
        for n in self.ENG:
            self.newsem(n)

    def newsem(self, key):
        self.sems[key] = self.es.enter_context(self.nc.semaphore("s_" + key))
        self.cnt[key] = 0

    def wait(self, eng, tok):
        key, val = tok
        if self.waited[eng].get(key, 0) >= val:
            return
        self.waited[eng][key] = val
        sem = self.sems[key]
        self.streams[eng].append(lambda e, sem=sem, val=val: e.wait_ge(sem, val))

    def _deps(self, eng, reads, writes, extra):
        deps = set()
        for r in reads:
            if r.w is not None:
                deps.add(r.w)
        for w in writes:
            if w.w is not None and w.w[0] != eng:
                deps.add(w.w)
            for t in w.r:
                if t[0] != eng:
                    deps.add(t)
        deps.update(extra)
        for t in sorted(deps):
            if eng == "pe" and t[0] == "pe":
                continue
            self.wait(eng, t)

    def _commit(self, tok, reads, writes):
        for r in reads:
            r.r.append(tok)
        for w in writes:
            w.w = tok
            w.r = []

    def op(self, eng, fns, reads=(), writes=(), extra=()):
        if not isinstance(fns, (list, tuple)):
            fns = [fns]
        self._deps(eng, reads, writes, extra)
        self.cnt[eng] += 1
        tok = (eng, self.cnt[eng])
        sem = self.sems[eng]
        st = self.streams[eng]
        for f in fns[:-1]:
            st.append(f)
        last = fns[-1]
        st.append(lambda e, last=last, sem=sem: last(e).then_inc(sem, 1))
        self._commit(tok, reads, writes)
        return tok

    def dma(self, eng, key, out, in_, reads=(), writes=(), extra=()):
        if key not in self.sems:
            self.newsem(key)
        self._deps(eng, reads, writes, extra)
        self.cnt[key] += 16
        tok = (key, self.cnt[key])
        sem = self.sems[key]
        self.streams[eng].append(lambda e, out=out, in_=in_, sem=sem: e.dma_start(out=out, in_=in_).then_inc(sem, 16))
        self._commit(tok, reads, writes)
        return tok

    def barrier(self):
        for e in self.ENG:
            for k, v in self.cnt.items():
                if k != e and v > 0:
                    self.wait(e, (k, v))

    def replay(self):
        with self.nc.Block() as block:
            @block.tensor
            def _(e):
                for f in self.streams["pe"]:
                    f(e)

            @block.scalar
            def _(e):
                for f in self.streams["act"]:
                    f(e)

            @block.vector
            def _(e):
                for f in self.streams["dve"]:
                    f(e)

            @block.gpsimd
            def _(e):
                for f in self.streams["pool"]:
                    f(e)

            @block.sync
            def _(e):
                for f in self.streams["sp"]:
                    f(e)


def run_interleaved(main, bg, ratio):
    n = 0
    bg_alive = bg is not None
    for _ in main:
        n += 1
        if bg_alive and n % ratio == 0:
            try:
                next(bg)
            except StopIteration:
                bg_alive = False
    if bg_alive:
        for _ in bg:
            pass


def build(dbg=None, nlayers=NL, branches=(0, 1, 2), do_ffn=True):
    nc = bass.Bass("TRN2", target_bir_lowering=False)
    es = ExitStack()
    fw = FW(nc, es)

    def dram(name, shape, kind="ExternalInput"):
        return nc.dram_tensor(name, shape, F32, kind=kind).ap()

    xT_d = dram("xT", [D, S])
    WTOT = NL * 159744
    wpack_d = dram("wpack", [128, WTOT])
    plan = []
    woff = [0]
    vecs_d = dram("vecs", [128, NV])
    cos_d = dram("cosT", [128, S])
    sin_d = dram("sinT", [128, S])
    cst_d = dram("cst", [128, 2048])
    out_d = dram("outT", [D, S], kind="ExternalOutput")

    def sb(name, shape, dt):
        return es.enter_context(nc.sbuf_tensor(name, shape, dt))

    xT = sb("xT_s", [128, 8, S], F32)
    hT = sb("hT_s", [128, 8, S], BF16)
    oT = sb("oT_s", [128, 4, S], BF16)
    vecs = sb("vecs_s", [128, NV], F32)
    esink = sb("esink", [128, 8 * NL], F32)
    cst = sb("cst_s", [128, 2048], BF16)
    arena = sb("arena", [128, 14 * 1024], F32)
    wring = [sb(f"wring{i}", [128, 3072], BF16) for i in range(3)]
    ropeb = [sb(f"rope{i}", [128, 2, 512], F32) for i in range(2)]
    ps_all = es.enter_context(nc.psum_tensor("ps_all", [128, 4096], F32))
    banks = [ps_all[:, i * 512:(i + 1) * 512] for i in range(8)]
    bres = [Res() for _ in range(8)]

    ident = cst[:, 0:128]
    tincl = cst[:, 128:256]
    tcomp = cst[:, 256:384]
    m_le = cst[:, 384:512]
    m_lt = cst[:, 512:640]
    m_gt = cst[:, 640:768]
    m_legt = cst[:, 768:1024]
    onesm = cst[:, 1024:1152]

    r_x = [[Res() for _ in range(4)] for _ in range(8)]
    r_h = [Res() for _ in range(4)]
    r_oT = [[Res() for _ in range(4)] for _ in range(4)]
    r_vecs = Res()
    r_cst = Res()
    r_esink = Res()
    r_wring = [Res() for _ in range(3)]
    r_rope = [Res() for _ in range(2)]
    wr_i = [0]
    last_w = [None]
    rp_i = [0]

    fw.dma("sp", "d_vecs", vecs[:, :], vecs_d[:, :], writes=[r_vecs])
    fw.dma("pool", "d_cst", cst[:, :], cst_d[:, :], writes=[r_cst])
    for c in range(8):
        for tb in range(4):
            fw.dma("sp", "d_x", xT[:, c, tb * 512:(tb + 1) * 512], xT_d[c * 128:(c + 1) * 128, tb * 512:(tb + 1) * 512],
                   writes=[r_x[c][tb]])
    for c in range(8):
        for tb in range(4):
            r_x[c][tb].w = ("d_x", fw.cnt["d_x"])
    for l in range(NL):
        fw.op("act", lambda e, l=l: e.activation(out=esink[:, l * 8:(l + 1) * 8], in_=vecs[:, l * LV + 216:l * LV + 224], func=AF.Exp),
              reads=[r_vecs], writes=[r_esink])

    def vcol(l, off, n=1):
        return vecs[:, l * LV + off:l * LV + off + n]

    def load_w(pieces):
        i = wr_i[0] % 3
        wr_i[0] += 1
        slot, res = wring[i], r_wring[i]
        views = []
        off = 0
        for (name, l, sub, (r0, r1), (c0, c1)) in pieces:
            kc = (r1 - r0) // 128
            n = c1 - c0
            plan.append((name, l, sub, r0, r1, c0, c1, woff[0] + off))
            views.append(slot[:, off:off + kc * n].rearrange("p (c n) -> p c n", c=kc))
            off += kc * n
        assert off <= 3072
        last_w[0] = fw.dma("pool", f"d_w{i}", slot[:, 0:off], wpack_d[:, woff[0]:woff[0] + off], writes=[res],
                           extra=([last_w[0]] if last_w[0] is not None else []))
        woff[0] += off
        return views, res

    bank_rr = [0]

    def next_bank(pool):
        b = pool[bank_rr[0] % len(pool)]
        bank_rr[0] += 1
        return b

    def dense(bk, cols, wv, wres, nkc, rhs_fn, rhs_res, wcol0=0, m=128):
        fns = []
        for kc in range(nkc):
            fns.append(lambda e, kc=kc: e.matmul(banks[bk][0:m, cols], lhsT=wv[:, kc, wcol0:wcol0 + m], rhs=rhs_fn(kc),
                                                 start=(kc == 0), stop=(kc == nkc - 1)))
        return fw.op("pe", fns, reads=[wres] + list(rhs_res), writes=[bres[bk]])

    def rmsnorm_to_hT(l, goff):
        sq = arena[:, 0:2048].bitcast(BF16).rearrange("p (c n) -> p c n", c=8)
        rsd = arena[:, 2048:2048 + 1024].rearrange("p (i n) -> p i n", i=2)
        r_sq = Res()
        r_rs = [Res(), Res()]
        for tb in range(4):
            cs = slice(tb * 512, (tb + 1) * 512)
            fw.op("act", lambda e, cs=cs: e.activation(out=sq, in_=xT[:, :, cs], func=AF.Square),
                  reads=[r_x[c][tb] for c in range(8)], writes=[r_sq])
            bk = next_bank(list(range(8)))
            fns = [lambda e, c=c, bk=bk: e.matmul(banks[bk][:, :], lhsT=onesm, rhs=sq[:, c, :], start=(c == 0), stop=(c == 7))
                   for c in range(8)]
            fw.op("pe", fns, reads=[r_sq, r_cst], writes=[bres[bk]])
            rs = rsd[:, tb % 2, :]
            fw.op("act", lambda e, bk=bk, rs=rs: e.activation(out=rs, in_=banks[bk][:, :], func=AF.Ln, bias=1e-6),
                  reads=[bres[bk]], writes=[r_rs[tb % 2]])
            fw.op("act", lambda e, rs=rs: e.activation(out=rs, in_=rs, func=AF.Exp, scale=-0.5),
                  reads=[r_rs[tb % 2]], writes=[r_rs[tb % 2]])
            fns = [lambda e, c=c, cs=cs, rs=rs: e.scalar_tensor_tensor(out=hT[:, c, cs], in0=xT[:, c, cs], scalar=vcol(l, goff + c),
                                                                       in1=rs, op0=ALU.mult, op1=ALU.mult) for c in range(8)]
            fw.op("dve", fns, reads=[r_x[c][tb] for c in range(8)] + [r_rs[tb % 2], r_vecs], writes=[r_h[tb]])

    def load_rope(tb):
        i = rp_i[0] % 2
        rp_i[0] += 1
        cs = slice(tb * 512, (tb + 1) * 512)
        fw.dma("sp", f"d_rp{i}", ropeb[i][:, 0, :], cos_d[:, cs], writes=[r_rope[i]])
        fw.dma("sp", f"d_rp{i}", ropeb[i][:, 1, :], sin_d[:, cs], writes=[r_rope[i]])
        return ropeb[i], r_rope[i]

    def rope_evac(bk, dst, dst_res, tb, t1, t2, r_t):
        rb, rres = load_rope(tb)
        fw.op("dve", lambda e: e.tensor_tensor(out=t1, in0=banks[bk][:, :], in1=rb[:, 0, :], op=ALU.mult),
              reads=[bres[bk], rres], writes=[r_t[0]])
        fns = []
        for g in range(4):
            pg = g ^ 1
            fns.append(lambda e, g=g, pg=pg: e.tensor_tensor(out=t2[g * 32:(g + 1) * 32, :], in0=banks[bk][pg * 32:(pg + 1) * 32, :],
                                                             in1=rb[g * 32:(g + 1) * 32, 1, :], op=ALU.mult))
        fw.op("dve", fns, reads=[bres[bk], rres], writes=[r_t[1]])
        fw.op("dve", lambda e: e.tensor_tensor(out=dst, in0=t1, in1=t2, op=ALU.add), reads=[r_t[0], r_t[1]], writes=[dst_res])

    def mixer_branch(l, br):
        A = arena
        kT = A[:, 0:1024].bitcast(BF16)
        qT = A[:, 1024:2048].bitcast(BF16)
        vaug = A[:, 2048:2048 + 1040].bitcast(BF16).rearrange("p (t h d) -> p t h d", t=16, h=2)
        off = 2048 + 1040
        t1 = A[:, off:off + 512]
        t2 = A[:, off + 512:off + 1024]
        off += 1024
        e_off = off
        ebuf = [[A[:, off + (2 * s + h) * 512: off + (2 * s + h + 1) * 512] for h in range(2)] for s in range(3)]
        off += 3072
        ex_off = off
        exbuf = [[A[:, off + (2 * s + h) * 512: off + (2 * s + h + 1) * 512] for h in range(2)] for s in range(2)]
        off += 2048
        sp_off = off
        spb = [[A[:, off + (2 * s + h) * 256: off + (2 * s + h + 1) * 256].bitcast(BF16) for h in range(2)] for s in range(3)]
        off += 1536
        p_off = off
        pb = [[A[:, off + (2 * s + h) * 256: off + (2 * s + h + 1) * 256].bitcast(BF16) for h in range(2)] for s in range(2)]
        off += 1024
        otok = A[:, off:off + 256].bitcast(BF16)
        off += 256
        mbt = A[:, off:off + 256].bitcast(BF16)
        off += 256
        mb = A[:, off:off + 256].bitcast(BF16)
        off += 256
        small = A[:, off:off + 256]
        off += 256
        qT2 = A[:, off:off + 1024].bitcast(BF16)
        off += 1024
        mb3 = A[:, off:off + 256].bitcast(BF16)
        off += 256
        assert off <= 14 * 1024, off
        gm = small[:, 0:64].rearrange("p (h i n) -> p h i n", h=2, i=4)
        mx = small[:, 64:72]
        selt = small[:, 72:80]
        den = small[:, 80:88].rearrange("p (h i) -> p h i", h=2)
        ksum = small[:, 96:104]
        kmf = small[:, 104:112]
        kmh = small[:, 112:116].bitcast(BF16)
        kml = small[:, 116:120].bitcast(BF16)

        r_kT = [Res() for _ in range(4)]
        r_qT = [Res() for _ in range(4)]
        r_v = [Res() for _ in range(4)]
        r_t = [Res(), Res()]
        r_e = [[Res(), Res()], [Res(), Res()], [Res(), Res()]]
        r_ex = [[Res(), Res()], [Res(), Res()]]
        r_sp = [[Res(), Res()], [Res(), Res()], [Res(), Res()]]
        r_p = [[Res(), Res()], [Res(), Res()]]
        r_otok, r_mbt, r_mb, r_small, r_km = Res(), Res(), Res(), Res(), Res()
        r_osb = [Res(), Res()]
        r_den = [Res(), Res()]

        if br == 0:
            units = [0]
        else:
            units = [0, 1, 2, 3]
        base = {0: 0, 1: 768, 2: 2304}[br]
        has_den = br != 1
        vw = 65 if has_den else 64

        if has_den:
            fw.op("dve", lambda e: e.memset(vaug[:, :, :, 64:65], 1.0), writes=r_v)
        if br == 2:
            fw.op("dve", lambda e: e.memset(mbt, 0.0), writes=[r_mbt])

        def proj_kv(wv, wres, kcol, vcol0):
            for tb in range(4):
                bk = next_bank([0, 1, 2, 3, 7])
                cs = slice(tb * 512, (tb + 1) * 512)
                dense(bk, slice(0, 512), wv, wres, 8, lambda kc, cs=cs: hT[:, kc, cs], [r_h[tb]], wcol0=kcol)
                if br == 1:
                    fw.op("act", lambda e, bk=bk, cs=cs: e.activation(out=kT[:, cs], in_=banks[bk][:, :], func=AF.Copy),
                          reads=[bres[bk]], writes=[r_kT[tb]])
                else:
                    rope_evac(bk, kT[:, cs], r_kT[tb], tb, t1, t2, r_t)
            for g4 in range(4):
                bk = next_bank([0, 1, 2, 3, 7])
                fns = []
                for j in range(4):
                    tt = g4 * 4 + j
                    for kc in range(8):
                        fns.append(lambda e, j=j, tt=tt, kc=kc, bk=bk: e.matmul(
                            banks[bk][:, j * 128:(j + 1) * 128], lhsT=hT[:, kc, tt * 128:(tt + 1) * 128],
                            rhs=wv[:, kc, vcol0:vcol0 + 128], start=(kc == 0), stop=(kc == 7)))
                fw.op("pe", fns, reads=[wres, r_h[g4]], writes=[bres[bk]])
                fw.op("act", lambda e, bk=bk, g4=g4: e.activation(
                    out=vaug[:, g4 * 4:(g4 + 1) * 4, :, 0:64],
                    in_=banks[bk][:, :].rearrange("p (t h d) -> p t h d", t=4, h=2), func=AF.Copy),
                    reads=[bres[bk]], writes=[r_v[g4]])

        def proj_q(wv, wres, qcol, Q):
            bk = next_bank([7])
            cs = slice(Q * 512, (Q + 1) * 512)
            dense(bk, slice(0, 512), wv, wres, 8, lambda kc: hT[:, kc, cs], [r_h[Q]], wcol0=qcol)
            if br == 1:
                fw.op("act", lambda e: e.activation(out=qT[:, cs], in_=banks[bk][:, :], func=AF.Copy),
                      reads=[bres[bk]], writes=[r_qT[Q]])
            else:
                rope_evac(bk, qT[:, cs], r_qT[Q], Q, t1, t2, r_t)

        def finish_o(obanks, Q, chunk, sink_cols):
            ot4 = otok.rearrange("p (i f) -> p i f", i=4)
            if has_den:
                for hh in range(2):
                    osb = A[:, 8192 + hh * 260: 8192 + (hh + 1) * 260]
                    fw.op("act", lambda e, hh=hh, osb=osb: e.activation(out=osb, in_=banks[obanks[hh]][:, 0:260], func=AF.Copy),
                          reads=[bres[obanks[hh]]], writes=[r_osb[hh]])
                for hh in range(2):
                    osb = A[:, 8192 + hh * 260: 8192 + (hh + 1) * 260]
                    o3 = osb.rearrange("p (i d) -> p i d", i=4)
                    if sink_cols is not None:
                        fw.op("dve", lambda e, hh=hh, o3=o3: e.tensor_scalar(out=den[:, hh, :], in0=o3[:, :, 64], scalar1=esink[:, sink_cols[hh]:sink_cols[hh] + 1],
                                                                            scalar2=None, op0=ALU.add),
                              reads=[r_osb[hh], r_esink], writes=[r_den[hh]])
                    else:
                        fw.op("dve", lambda e, hh=hh, o3=o3: e.tensor_copy(out=den[:, hh, :], in_=o3[:, :, 64]),
                              reads=[r_osb[hh]], writes=[r_den[hh]])
                    fw.op("dve", lambda e, hh=hh: e.reciprocal(out=den[:, hh, :], in_=den[:, hh, :]), reads=[r_den[hh]], writes=[r_den[hh]])
                    fw.op("dve", lambda e, hh=hh, o3=o3: e.tensor_tensor(out=ot4[:, :, hh * 64:(hh + 1) * 64], in0=o3[:, :, 0:64],
                                                                        in1=den[:, hh, :].unsqueeze(2).to_broadcast([128, 4, 64]), op=ALU.mult),
                          reads=[r_osb[hh], r_den[hh]], writes=[r_otok])
            else:
                fw.op("act", lambda e: e.activation(out=otok, in_=banks[obanks[0]][:, :], func=AF.Copy),
                      reads=[bres[obanks[0]]], writes=[r_otok])
            tb7 = banks[7][:, 0:256].bitcast(BF16)
            fns = [lambda e, i=i: e.transpose(tb7[:, i * 128:(i + 1) * 128], ot4[:, i, :], ident) for i in range(4)]
            fw.op("pe", fns, reads=[r_otok, r_cst], writes=[bres[7]])
            fw.op("dve", lambda e: e.tensor_copy(out=oT[:, chunk, Q * 512:(Q + 1) * 512], in_=tb7), reads=[bres[7]], writes=[r_oT[chunk][Q]])

        if br == 0:
            (wkv,), wres = load_w([("w_in", l, None, (0, 1024), (512, 768))])
            proj_kv(wkv, wres, 0, 128)
            qTs = [qT, qT2]
            r_qTs = [[Res() for _ in range(4)] for _ in range(2)]
            r_S = [[Res(), Res()], [Res(), Res()]]

            def proj_chunk(c):
                (wq,), wqres = load_w([("w_in", l, None, (0, 1024), (c * 128, (c + 1) * 128))])
                for Q in range(4):
                    bk = next_bank([6, 7])
                    cs = slice(Q * 512, (Q + 1) * 512)
                    dense(bk, slice(0, 512), wq, wqres, 8, lambda kc, cs=cs: hT[:, kc, cs], [r_h[Q]])
                    rope_evac(bk, qTs[c % 2][:, cs], r_qTs[c % 2][Q], Q, t1, t2, r_t)
                    yield

            def attn_chunk(c):
                qTc, r_q = qTs[c % 2], r_qTs[c % 2]
                steps = []
                for Q in range(4):
                    lst = list(range(max(0, 4 * Q - 1), 4 * Q + 4))
                    for si, a in enumerate(lst):
                        steps.append((Q, a, si == 0, si == len(lst) - 1))
                n_st = len(steps)

                def geom(Q, a):
                    i0 = a - 4 * Q
                    if i0 < 0:
                        return [0], m_gt
                    if i0 == 3:
                        return [3], m_le
                    return [i0, i0 + 1], m_legt

                def s_qk(idx):
                    Q, a, _, _ = steps[idx]
                    s = idx % 2
                    qt, msk = geom(Q, a)
                    n = 128 * len(qt)
                    qc = slice(Q * 512 + qt[0] * 128, Q * 512 + qt[0] * 128 + n)
                    sc = slice(0, n)
                    for hh in range(2):
                        rows = slice(hh * 64, (hh + 1) * 64)
                        fw.op("pe", lambda e, hh=hh, rows=rows, sc=sc, qc=qc, a=a, s=s: e.matmul(
                            banks[2 * hh + s][:, sc], lhsT=kT[rows, a * 128:(a + 1) * 128], rhs=qTc[rows, qc], start=True, stop=True),
                            reads=[r_kT[a // 4], r_q[Q]], writes=[bres[2 * hh + s]])
                    for hh in range(2):
                        P = pb[s][hh][:, 0:n]
                        fw.op("act", lambda e, hh=hh, sc=sc, P=P, s=s: e.activation(out=P, in_=banks[2 * hh + s][:, sc], func=AF.Exp, scale=0.125),
                              reads=[bres[2 * hh + s]], writes=[r_p[s][hh]])
                        fw.op("dve", lambda e, P=P, msk=msk, n=n: e.tensor_tensor(out=P, in0=P, in1=msk[:, 0:n], op=ALU.mult),
                              reads=[r_p[s][hh], r_cst], writes=[r_p[s][hh]])

                def s_pv(idx):
                    Q, a, isfirst, islast = steps[idx]
                    s = idx % 2
                    qt, msk = geom(Q, a)
                    ob = [4, 5]
                    for hh in range(2):
                        P = pb[s][hh]
                        fns = []
                        for j, i in enumerate(qt):
                            fns.append(lambda e, hh=hh, j=j, i=i, P=P, a=a, st=(isfirst and j == 0), ob=ob: e.matmul(
                                banks[ob[hh]][:, i * 65:(i + 1) * 65], lhsT=P[:, j * 128:(j + 1) * 128], rhs=vaug[:, a, hh, :],
                                start=st, stop=False, skip_group_check=True))
                        fw.op("pe", fns, reads=[r_p[s][hh], r_v[a // 4]], writes=[bres[ob[hh]]])
                    if islast:
                        finish_o(ob, Q, c, [l * 8 + c, l * 8 + 4 + c])

                s_qk(0)
                for idx in range(n_st):
                    if idx + 1 < n_st:
                        s_qk(idx + 1)
                    s_pv(idx)
                    yield

            for _ in proj_chunk(0):
                pass
            for c in range(4):
                run_interleaved(attn_chunk(c), proj_chunk(c + 1) if c + 1 < 4 else None, 4)
            return

        ealt = ebuf[0][0]
        alt0 = 2048 + 1040 + 1024
        kT_b = A[:, alt0:alt0 + 1024].bitcast(BF16)
        qT_b = A[:, alt0 + 1024:alt0 + 2048].bitcast(BF16)
        vaug_b = A[:, alt0 + 2048:alt0 + 2048 + 1040].bitcast(BF16).rearrange("p (t h d) -> p t h d", t=16, h=2)
        mb_b = [A[:, alt0 + 3088 + i * 256: alt0 + 3088 + (i + 1) * 256].bitcast(BF16) for i in range(2)]
        BUFS = [
            dict(kT=kT, qT=qT, vaug=vaug, kmh=kmh, kml=kml, mbs=[mb, mb3], r_kT=r_kT, r_qT=r_qT, r_v=r_v, r_km=r_km, r_mbs=[Res(), Res()]),
            dict(kT=kT_b, qT=qT_b, vaug=vaug_b, kmh=small[:, 120:124].bitcast(BF16), kml=small[:, 124:128].bitcast(BF16), mbs=mb_b,
                 r_kT=[Res() for _ in range(4)], r_qT=[Res() for _ in range(4)], r_v=[Res() for _ in range(4)], r_km=Res(), r_mbs=[Res(), Res()]),
        ]
        if br == 2:
            fw.op("dve", lambda e: e.memset(vaug_b[:, :, :, 64:65], 1.0), writes=BUFS[1]["r_v"])
        pj_banks = [6, 7] if br == 2 else [0, 1, 2, 3, 7]

        def proj_unit(u, B):
            kT, qT, vaug, kmh, kml, mbs = B["kT"], B["qT"], B["vaug"], B["kmh"], B["kml"], B["mbs"]
            r_kT, r_qT, r_v, r_km, r_mbs = B["r_kT"], B["r_qT"], B["r_v"], B["r_km"], B["r_mbs"]
            qc0 = base + u * 128
            kc0 = base + 512 + u * 128
            vc0 = base + 1024 + u * 128
            (wq, wk, wvv), wres = load_w([("w_in", l, None, (0, 1024), (qc0, qc0 + 128)), ("w_in", l, None, (0, 1024), (kc0, kc0 + 128)), ("w_in", l, None, (0, 1024), (vc0, vc0 + 128))])
            for tb in range(4):
                bk = next_bank(pj_banks)
                cs = slice(tb * 512, (tb + 1) * 512)
                dense(bk, slice(0, 512), wk, wres, 8, lambda kc, cs=cs: hT[:, kc, cs], [r_h[tb]])
                if br == 1:
                    fw.op("act", lambda e, bk=bk, cs=cs: e.activation(out=kT[:, cs], in_=banks[bk][:, :], func=AF.Copy),
                          reads=[bres[bk]], writes=[r_kT[tb]])
                else:
                    rope_evac(bk, kT[:, cs], r_kT[tb], tb, t1, t2, r_t)
                yield
            for g4 in range(4):
                bk = next_bank(pj_banks)
                fns = []
                for j in range(4):
                    tt = g4 * 4 + j
                    for kc in range(8):
                        fns.append(lambda e, j=j, tt=tt, kc=kc, bk=bk, wvv=wvv: e.matmul(
                            banks[bk][:, j * 128:(j + 1) * 128], lhsT=hT[:, kc, tt * 128:(tt + 1) * 128],
                            rhs=wvv[:, kc, :], start=(kc == 0), stop=(kc == 7)))
                fw.op("pe", fns, reads=[wres, r_h[g4]], writes=[bres[bk]])
                fw.op("act", lambda e, bk=bk, g4=g4: e.activation(
                    out=vaug[:, g4 * 4:(g4 + 1) * 4, :, 0:64],
                    in_=banks[bk][:, :].rearrange("p (t h d) -> p t h d", t=4, h=2), func=AF.Copy),
                    reads=[bres[bk]], writes=[r_v[g4]])
            if br == 2:
                fw.op("dve", lambda e: e.tensor_reduce(out=ksum, in_=kT.rearrange("p (n k) -> p n k", n=8), axis=AX.X, op=ALU.add),
                      reads=r_kT, writes=[r_small])
                fw.op("dve", lambda e: e.tensor_scalar(out=kmf, in0=ksum, scalar1=1.0 / 256, scalar2=None, op0=ALU.mult),
                      reads=[r_small], writes=[r_small])
                fw.op("dve", lambda e: e.tensor_copy(out=kmh, in_=kmf), reads=[r_small], writes=[r_km])
                fw.op("dve", lambda e: e.tensor_tensor(out=kml, in0=kmf, in1=kmh, op=ALU.subtract), reads=[r_small, r_km], writes=[r_km])


            for Q in range(4):
                bk = next_bank([7] if br == 1 else [6, 7])
                cs = slice(Q * 512, (Q + 1) * 512)
                dense(bk, slice(0, 512), wq, wres, 8, lambda kc, cs=cs: hT[:, kc, cs], [r_h[Q]])
                if br == 1:
                    fw.op("act", lambda e, bk=bk, cs=cs: e.activation(out=qT[:, cs], in_=banks[bk][:, :], func=AF.Copy),
                          reads=[bres[bk]], writes=[r_qT[Q]])
                else:
                    rope_evac(bk, qT[:, cs], r_qT[Q], Q, t1, t2, r_t)
                yield
            for Q in range(4):
                yield
                need_sel = (br == 2 and Q >= 2)
                mb = mbs[Q % 2]
                r_mb = r_mbs[Q % 2]
                if need_sel:
                    gb = [5, 6]
                    gb = [6, 7]
                    for hh in range(2):
                        rows = slice(hh * 64, (hh + 1) * 64)
                        fns = []
                        for i in range(4):
                            qi = slice(Q * 512 + i * 128, Q * 512 + (i + 1) * 128)
                            fns.append(lambda e, hh=hh, i=i, qi=qi, rows=rows: e.matmul(banks[gb[hh]][:, i * 8:(i + 1) * 8], lhsT=qT[rows, qi], rhs=kmh[rows, :], start=True, stop=False))
                            fns.append(lambda e, hh=hh, i=i, qi=qi, rows=rows: e.matmul(banks[gb[hh]][:, i * 8:(i + 1) * 8], lhsT=qT[rows, qi], rhs=kml[rows, :], start=False, stop=True))
                        fw.op("pe", fns, reads=[r_qT[Q], r_km], writes=[bres[gb[hh]]])
                    fw.op("dve", lambda e: e.memset(gm, -1e30), writes=[r_small])
                    for hh in range(2):
                        g3 = banks[gb[hh]][:, 0:32].rearrange("p (i n) -> p i n", i=4)
                        fw.op("dve", [lambda e, hh=hh, g3=g3, Q=Q: e.tensor_copy(out=gm[:, hh, 0:2, 0:2 * Q], in_=g3[:, 0:2, 0:2 * Q]),
                                      lambda e, hh=hh, g3=g3, Q=Q: e.tensor_copy(out=gm[:, hh, 2:4, 0:2 * Q + 1], in_=g3[:, 2:4, 0:2 * Q + 1])],
                              reads=[bres[gb[hh]]], writes=[r_small])
                    mbt4 = mbt.rearrange("p (i f) -> p i f", i=4)
                    mx_all = small[:, 128:192].rearrange("p (g n) -> p g n", g=8)
                    sel_all = small[:, 192:256].rearrange("p (h i n) -> p h i n", h=2, i=4)
                    gm8 = small[:, 0:64].rearrange("p (g n) -> p g n", g=8)
                    fns = [lambda e, g=g: e.max(out=mx_all[:, g, :], in_=gm8[:, g, :]) for g in range(8)]
                    fw.op("dve", fns, reads=[r_small], writes=[r_small])
                    fw.op("dve", lambda e: e.tensor_tensor(out=small[:, 192:256].rearrange("p (g n) -> p g n", g=8), in0=gm8,
                                                           in1=mx_all[:, :, 2:3].to_broadcast([128, 8, 8]), op=ALU.is_ge),
                          reads=[r_small], writes=[r_small])
                    fns = []
                    for hh in range(2):
                        fns.append(lambda e, hh=hh: e.tensor_scalar(out=mbt4[:, :, hh * 64:hh * 64 + 8], in0=sel_all[:, hh, :, :], scalar1=-1.0, scalar2=-NEG,
                                                                    op0=ALU.add, op1=ALU.mult))
                    for hh in range(2):
                        for ip in range(2):
                            nb = 2 * Q + ip
                            fns.append(lambda e, hh=hh, ip=ip, nb=nb: e.memset(mbt4[:, 2 * ip:2 * ip + 2, hh * 64 + nb:hh * 64 + nb + 1], 0.0))
                    fw.op("dve", fns, reads=[r_small], writes=[r_mbt])
                    tb7 = banks[7][:, 0:256].bitcast(BF16)
                    fns = [lambda e, i=i: e.transpose(tb7[:, i * 128:(i + 1) * 128], mbt4[:, i, :], ident) for i in range(4)]
                    fw.op("pe", fns, reads=[r_mbt, r_cst], writes=[bres[7]])
                    fw.op("dve", lambda e, mb=mb, tb7=tb7: e.tensor_copy(out=mb, in_=tb7), reads=[bres[7]], writes=[r_mb])


        def attn_unit(u, B):
            kT, qT, vaug, kmh, kml, mbs = B["kT"], B["qT"], B["vaug"], B["kmh"], B["kml"], B["mbs"]
            r_kT, r_qT, r_v, r_km, r_mbs = B["r_kT"], B["r_qT"], B["r_v"], B["r_km"], B["r_mbs"]
            for Q in range(4):
                qcs = slice(Q * 512, (Q + 1) * 512)
                need_sel = (br == 2 and Q >= 2)
                mb = mbs[Q % 2]
                r_mb = r_mbs[Q % 2]
                steps = list(range(4 * Q + 3, -1, -1))
                ns = len(steps)
                if br == 1:
                    ob = [6, 6]
                    xb = [4, 5]
                else:
                    ob = [4, 5]

                def cols_of(a):
                    i0 = max(0, a - 4 * Q)
                    return i0, slice(i0 * 128, 512)

                def two(ap0_start, width, cl):
                    return None

                def stage_qk(t):
                    a = steps[t]
                    s = t % 2
                    i0, cl = cols_of(a)
                    qcl = slice(Q * 512 + i0 * 128, (Q + 1) * 512)
                    msk = need_sel and a < 4 * Q + 2
                    for hh in range(2):
                        rows = slice(hh * 64, (hh + 1) * 64)
                        bk = 2 * s + hh
                        fns = [lambda e, a=a, rows=rows, bk=bk, cl=cl, qcl=qcl, msk=msk: e.matmul(
                            banks[bk][:, cl], lhsT=kT[rows, a * 128:(a + 1) * 128], rhs=qT[rows, qcl], start=True, stop=not msk)]
                        rd = [r_kT[a // 4], r_qT[Q]]
                        if msk:
                            n = a // 2
                            fns.append(lambda e, rows=rows, bk=bk, cl=cl, n=n, mb=mb: e.matmul(
                                banks[bk][:, cl], lhsT=cst[rows, 1152 + n * 128:1152 + (n + 1) * 128], rhs=mb[rows, cl], start=False, stop=True))
                            rd += [r_mb, r_cst]
                        fw.op("pe", fns, reads=rd, writes=[bres[bk]])
                    S2 = ps_all[:, 2 * s * 512:(2 * s + 2) * 512].rearrange("p (h n) -> p h n", h=2)[:, :, cl]
                    sb2 = [bres[2 * s], bres[2 * s + 1]]
                    dc = slice(i0 * 128, (i0 + 1) * 128)
                    if br == 1:
                        s3 = t % 3
                        E2 = A[:, e_off + s3 * 1024:e_off + (s3 + 1) * 1024].rearrange("p (h n) -> p h n", h=2)
                        SP2 = A[:, sp_off + s3 * 512:sp_off + (s3 + 1) * 512].bitcast(BF16).rearrange("p (h n) -> p h n", h=2)
                        fw.op("act", lambda e, S2=S2, E2=E2, cl=cl: e.activation(out=E2[:, :, cl], in_=S2, func=AF.Exp, scale=0.125),
                              reads=sb2, writes=r_e[s3])
                        fw.op("act", lambda e, E2=E2, SP2=SP2, cl=cl: e.activation(out=SP2[:, :, cl], in_=E2[:, :, cl], func=AF.Ln, bias=1.0),
                              reads=r_e[s3], writes=r_sp[s3])
                        if a >= 4 * Q:
                            fw.op("dve", lambda e, SP2=SP2, dc=dc: e.tensor_tensor(out=SP2[:, :, dc], in0=SP2[:, :, dc],
                                                                                   in1=m_lt.unsqueeze(1).to_broadcast([128, 2, 128]), op=ALU.mult),
                                  reads=r_sp[s3] + [r_cst], writes=r_sp[s3])
                    else:
                        P2 = A[:, p_off + s * 512:p_off + (s + 1) * 512].bitcast(BF16).rearrange("p (h n) -> p h n", h=2)
                        fw.op("act", lambda e, S2=S2, P2=P2, cl=cl: e.activation(out=P2[:, :, cl], in_=S2, func=AF.Exp, scale=0.125),
                              reads=sb2, writes=r_p[s])
                        if a >= 4 * Q:
                            fw.op("dve", lambda e, P2=P2, dc=dc: e.tensor_tensor(out=P2[:, :, dc], in0=P2[:, :, dc],
                                                                                 in1=m_le.unsqueeze(1).to_broadcast([128, 2, 128]), op=ALU.mult),
                                  reads=r_p[s] + [r_cst], writes=r_p[s])

                def stage_xa(t):
                    a = steps[t]
                    s = t % 2
                    s3 = t % 3
                    i0, cl = cols_of(a)
                    last = (t == ns - 1)
                    for hh in range(2):
                        SP = spb[s3][hh]
                        fw.op("pe", lambda e, hh=hh, cl=cl, SP=SP, t=t, last=last, xb=xb: e.matmul(
                            banks[xb[hh]][:, cl], lhsT=tincl, rhs=SP[:, cl], start=(t == 0), stop=last, skip_group_check=True),
                            reads=[r_sp[s3][hh], r_cst], writes=[bres[xb[hh]]])
                    X2 = ps_all[:, 4 * 512:6 * 512].rearrange("p (h n) -> p h n", h=2)[:, :, cl]
                    EX2 = A[:, ex_off + s * 1024:ex_off + (s + 1) * 1024].rearrange("p (h n) -> p h n", h=2)
                    fw.op("act", lambda e, X2=X2, EX2=EX2, cl=cl: e.activation(out=EX2[:, :, cl], in_=X2, func=AF.Exp),
                          reads=[bres[4], bres[5]], writes=r_ex[s])

                def stage_xb(t):
                    a = steps[t]
                    s = t % 2
                    s3 = t % 3
                    i0, cl = cols_of(a)
                    last = (t == ns - 1)
                    if not last:
                        for hh in range(2):
                            SP = spb[s3][hh]
                            fw.op("pe", lambda e, hh=hh, cl=cl, SP=SP, xb=xb: e.matmul(
                                banks[xb[hh]][:, cl], lhsT=tcomp, rhs=SP[:, cl], start=False, stop=False, skip_group_check=True),
                                reads=[r_sp[s3][hh], r_cst], writes=[bres[xb[hh]]])
                    E2 = A[:, e_off + s3 * 1024:e_off + (s3 + 1) * 1024].rearrange("p (h n) -> p h n", h=2)
                    EX2 = A[:, ex_off + s * 1024:ex_off + (s + 1) * 1024].rearrange("p (h n) -> p h n", h=2)
                    W2 = A[:, p_off + s * 512:p_off + (s + 1) * 512].bitcast(BF16).rearrange("p (h n) -> p h n", h=2)
                    fw.op("dve", lambda e, cl=cl, E2=E2, EX2=EX2, W2=W2: e.tensor_tensor(out=W2[:, :, cl], in0=E2[:, :, cl], in1=EX2[:, :, cl], op=ALU.mult),
                          reads=r_e[s3] + r_ex[s], writes=r_p[s])
                    if a >= 4 * Q:
                        dc = slice(i0 * 128, (i0 + 1) * 128)
                        fw.op("dve", lambda e, W2=W2, dc=dc: e.tensor_tensor(out=W2[:, :, dc], in0=W2[:, :, dc],
                                                                             in1=m_lt.unsqueeze(1).to_broadcast([128, 2, 128]), op=ALU.mult),
                              reads=r_p[s] + [r_cst], writes=r_p[s])

                def stage_pv(t):
                    a = steps[t]
                    s = t % 2
                    i0, cl = cols_of(a)
                    for hh in range(2):
                        P = pb[s][hh]
                        fns = []
                        for i in range(i0, 4):
                            if br == 1:
                                oc = slice(i * 128 + hh * 64, i * 128 + hh * 64 + 64)
                                st = (t == 0 and hh == 0 and i == i0)
                                rhs = vaug[:, a, hh, 0:64]
                            else:
                                oc = slice(i * 65, (i + 1) * 65)
                                st = (t == 0 and i == i0)
                                rhs = vaug[:, a, hh, :]
                            fns.append(lambda e, hh=hh, i=i, oc=oc, st=st, rhs=rhs, P=P, ob=ob: e.matmul(
                                banks[ob[hh]][:, oc], lhsT=P[:, i * 128:(i + 1) * 128], rhs=rhs, start=st, stop=False, skip_group_check=True))
                        fw.op("pe", fns, reads=[r_p[s][hh], r_v[a // 4]], writes=[bres[ob[hh]]])

                if br == 1:
                    stage_qk(0)
                    if ns > 1:
                        stage_qk(1)
                    for t in range(ns):
                        stage_xa(t)
                        if t >= 1:
                            stage_pv(t - 1)
                        if t + 2 < ns:
                            stage_qk(t + 2)
                        stage_xb(t)
                        yield
                    stage_pv(ns - 1)
                else:
                    stage_qk(0)
                    for t in range(ns):
                        if t + 1 < ns:
                            stage_qk(t + 1)
                        stage_pv(t)
                        yield
                finish_o(ob, Q, u, None)

        if br == 2:
            for _ in proj_unit(0, BUFS[0]):
                pass
            for u in units:
                run_interleaved(attn_unit(u, BUFS[u % 2]), proj_unit(u + 1, BUFS[(u + 1) % 2]) if u + 1 < 4 else None, 2)
        else:
            for u in units:
                for _ in proj_unit(u, BUFS[0]):
                    pass
                for _ in attn_unit(u, BUFS[0]):
                    pass

    def post_branch(l, br):
        mT = arena[:, 0:8192].bitcast(BF16).rearrange("p (c n) -> p c n", c=8)
        sg = [arena[:, 8192 + i * 512: 8192 + (i + 1) * 512] for i in range(2)]
        r_sg = [Res(), Res()]
        r_m = [[Res() for _ in range(4)] for _ in range(8)]
        k = 0
        for j in range(8):
            gcol = 3840 + br * 1024 + j * 128
            (wg, wb), wres = load_w([("w_in", l, None, (0, 1024), (gcol, gcol + 128)), ("w_br", l, br, (0, 512), (j * 128, (j + 1) * 128))])
            wres2 = wres
            for tb in range(4):
                cs = slice(tb * 512, (tb + 1) * 512)
                bg = next_bank(list(range(8)))
                dense(bg, slice(0, 512), wg, wres, 8, lambda kc, cs=cs: hT[:, kc, cs], [r_h[tb]])
                by = next_bank(list(range(8)))
                dense(by, slice(0, 512), wb, wres2, 4, lambda kc, cs=cs: oT[:, kc, cs], [r_oT[c][tb] for c in range(4)])
                s = k % 2
                k += 1
                fw.op("act", lambda e, bg=bg, s=s, j=j: e.activation(out=sg[s], in_=banks[bg][:, :], func=AF.Sigmoid, bias=vcol(l, 16 + br * 8 + j)),
                      reads=[bres[bg], r_vecs], writes=[r_sg[s]])
                fw.op("dve", lambda e, by=by, s=s, j=j, cs=cs: e.tensor_tensor(out=mT[:, j, cs], in0=banks[by][:, :], in1=sg[s], op=ALU.mult),
                      reads=[bres[by], r_sg[s]], writes=[r_m[j][tb]])
        for j in range(8):
            (wo,), wres = load_w([("w_out", l, None, (0, 1024), (j * 128, (j + 1) * 128))])
            for tb in range(4):
                cs = slice(tb * 512, (tb + 1) * 512)
                bk = next_bank(list(range(8)))
                dense(bk, slice(0, 512), wo, wres, 8, lambda kc, cs=cs: mT[:, kc, cs], [r_m[c][tb] for c in range(8)])
                fw.op("dve", lambda e, bk=bk, j=j, cs=cs: e.tensor_tensor(out=xT[:, j, cs], in0=banks[bk][:, :], in1=xT[:, j, cs], op=ALU.add),
                      reads=[bres[bk]], writes=[r_x[j][tb]])

    def ffn(l):
        actT = arena[:, 0:11 * 1024].bitcast(BF16).rearrange("p (c n) -> p c n", c=11)
        rb_ = [[oT[:, (2 * s + z) // 2, ((2 * s + z) % 2) * 1024:((2 * s + z) % 2 + 1) * 1024].bitcast(F32) for z in range(2)] for s in range(4)]
        r_rb = [[Res(), Res()] for _ in range(4)]
        tiles = [(0, 512), (512, 510), (1022, 510), (1532, 510), (2042, 6)]
        allh = list(r_h)
        k = [0]
        stages = []
        for half in range(2):
            for f in range(11):
                fc = half * 11 + f
                stages.append(("up", half, f, [("w_up", l, None, (0, 1024), (fc * 128, (fc + 1) * 128)),
                                                ("w_up", l, None, (0, 1024), (2816 + fc * 128, 2816 + (fc + 1) * 128))]))
            for j in range(8):
                stages.append(("dn", half, j, [("w_dn", l, None, (half * 1408, (half + 1) * 1408), (j * 128, (j + 1) * 128))]))
        loaded = [None] * len(stages)
        loaded[0] = load_w(stages[0][3])
        r_act = [Res() for _ in range(11)]
        pend = []

        def flush(keep):
            while len(pend) > keep:
                (s, f, t0, n, ra) = pend.pop(0)
                Ra, Rv = rb_[s][0], rb_[s][1]
                fw.op("act", lambda e, Ra=Ra, n=n: e.activation(out=Ra[:, 0:n], in_=Ra[:, 0:n], func=AF.Silu), reads=[r_rb[s][0]], writes=[r_rb[s][0]])
                fw.op("pool", lambda e, Ra=Ra, Rv=Rv, f=f, t0=t0, n=n: e.tensor_tensor(out=actT[:, f, t0:t0 + n], in0=Ra[:, 0:n], in1=Rv[:, 0:n], op=ALU.mult),
                      reads=[r_rb[s][0], r_rb[s][1]], writes=[ra[f]])

        for si, (kind, half, idx, _) in enumerate(stages):
            if si + 1 < len(stages):
                loaded[si + 1] = load_w(stages[si + 1][3])
            views, wres = loaded[si]
            if kind == "up":
                f = idx
                fc = half * 11 + f
                wa, wvv = views
                for (t0, n) in tiles:
                    s = k[0] % 4
                    k[0] += 1
                    info = []
                    for z, wz in enumerate((wa, wvv)):
                        ch = fc + z * 22
                        bk = next_bank(list(range(8)))
                        if t0 == 0:
                            cs = slice(0, 512)
                            N = 512
                        else:
                            cs = slice(t0 - 2, t0 + n)
                            N = n + 2
                        dense(bk, slice(0, N), wz, wres, 8, lambda kc, cs=cs: hT[:, kc, cs], allh)
                        info.append((z, ch, bk))
                    for (z, ch, bk) in info:
                        R = rb_[s][z]
                        src = banks[bk][:, 0:512] if t0 == 0 else banks[bk][:, 2:n + 2]
                        fw.op("act", lambda e, R=R, src=src, ch=ch, n=n: e.activation(out=R[:, 0:n], in_=src, func=AF.Identity,
                                                                                   scale=vcol(l, 40 + 2 * 44 + ch), bias=vcol(l, 172 + ch)),
                              reads=[bres[bk], r_vecs], writes=[r_rb[s][z]])
                    for tap, off in ((1, 1), (0, 2)):
                        for (z, ch, bk) in info:
                            R = rb_[s][z]
                            if t0 == 0:
                                dst = R[:, off:512]
                                src = banks[bk][:, 0:512 - off]
                            else:
                                dst = R[:, 0:n]
                                src = banks[bk][:, 2 - off:2 - off + n]
                            fw.op("dve", lambda e, dst=dst, src=src, ch=ch, tap=tap: e.scalar_tensor_tensor(
                                out=dst, in0=src, scalar=vcol(l, 40 + tap * 44 + ch), in1=dst, op0=ALU.mult, op1=ALU.add),
                                reads=[bres[bk], r_rb[s][z], r_vecs], writes=[r_rb[s][z]])
                    pend.append((s, f, t0, n, r_act))
                    flush(2)
            else:
                j = idx
                if j == 0:
                    flush(0)
                (wd,) = views
                for tb in range(4):
                    cs = slice(tb * 512, (tb + 1) * 512)
                    bk = next_bank(list(range(8)))
                    dense(bk, slice(0, 512), wd, wres, 11, lambda kc, cs=cs: actT[:, kc, cs], r_act)
                    fw.op("dve", lambda e, bk=bk, j=j, cs=cs: e.tensor_tensor(out=xT[:, j, cs], in0=banks[bk][:, :], in1=xT[:, j, cs], op=ALU.add),
                          reads=[bres[bk]], writes=[r_x[j][tb]])
        fw.barrier()

    for l in range(nlayers):
        rmsnorm_to_hT(l, 0)
        fw.barrier()
        for br in branches:
            mixer_branch(l, br)
            fw.barrier()
            if dbg == "oT":
                break
            post_branch(l, br)
            fw.barrier()
        if dbg == "oT":
            break
        if do_ffn:
            rmsnorm_to_hT(l, 8)
            fw.barrier()
            ffn(l)
            fw.barrier()

    r_out = Res()
    if dbg == "oT":
        for c in range(4):
            fw.dma("pool", "d_out", out_d[c * 128:(c + 1) * 128, :], oT[:, c, :], reads=[r_oT[c][tb] for tb in range(4)], writes=[r_out])
        fw.wait("sp", r_out.w)
    elif dbg == "x":
        for c in range(8):
            fw.dma("sp", "d_out", out_d[c * 128:(c + 1) * 128, :], xT[:, c, :], reads=[r_x[c][tb] for tb in range(4)], writes=[r_out])
    else:
        sq = arena[:, 0:2048].bitcast(BF16).rearrange("p (c n) -> p c n", c=8)
        rsd = arena[:, 2048:2048 + 512]
        ob = [arena[:, 4096 + i * 4096: 4096 + (i + 1) * 4096].rearrange("p (c n) -> p c n", c=8) for i in range(2)]
        r_sq, r_rs, r_ob = Res(), Res(), [Res(), Res()]
        goff = NL * LV
        for tb in range(4):
            cs = slice(tb * 512, (tb + 1) * 512)
            fw.op("act", lambda e, cs=cs: e.activation(out=sq, in_=xT[:, :, cs], func=AF.Square),
                  reads=[r_x[c][tb] for c in range(8)], writes=[r_sq])
            bk = next_bank(list(range(8)))
            fns = [lambda e, c=c, bk=bk: e.matmul(banks[bk][:, :], lhsT=onesm, rhs=sq[:, c, :], start=(c == 0), stop=(c == 7)) for c in range(8)]
            fw.op("pe", fns, reads=[r_sq, r_cst], writes=[bres[bk]])
            fw.op("act", lambda e, bk=bk: e.activation(out=rsd, in_=banks[bk][:, :], func=AF.Ln, bias=1e-6), reads=[bres[bk]], writes=[r_rs])
            fw.op("act", lambda e: e.activation(out=rsd, in_=rsd, func=AF.Exp, scale=-0.5), reads=[r_rs], writes=[r_rs])
            O = ob[tb % 2]
            fns = [lambda e, c=c, cs=cs, O=O: e.scalar_tensor_tensor(out=O[:, c, :], in0=xT[:, c, cs], scalar=vecs[:, goff + c:goff + c + 1],
                                                                    in1=rsd, op0=ALU.mult, op1=ALU.mult) for c in range(8)]
            fw.op("dve", fns, reads=[r_x[c][tb] for c in range(8)] + [r_rs, r_vecs], writes=[r_ob[tb % 2]])
            fw.dma("sp", "d_out", out_d.rearrange("(c p) n -> p c n", p=128)[:, :, cs], O, reads=[r_ob[tb % 2]], writes=[r_out])
    fw.wait("sp", r_out.w)
    fw.barrier()
    fw.replay()
    es.close()
    build.last_plan = plan
    return nc


def host_consts():
    cst = np.zeros((128, 2048), np.float32)
    j = np.arange(128)[:, None]
    s = np.arange(128)[None, :]
    cst[:, 0:128] = np.eye(128)
    cst[:, 128:256] = -1.0 * (j >= s)
    cst[:, 256:384] = -1.0 * (j < s)
    cst[:, 384:512] = (j <= s)
    cst[:, 512:640] = (j < s)
    cst[:, 640:768] = (j > s)
    cst[:, 768:896] = (j <= s)
    cst[:, 896:1024] = (j > s)
    cst[:, 1024:1152] = 1.0 / 1024
    for n in range(7):
        cst[n, 1152 + n * 128:1152 + (n + 1) * 128] = 1.0
        cst[64 + n, 1152 + n * 128:1152 + (n + 1) * 128] = 1.0
    inv = (10000.0 ** (-(np.arange(0, 64, 2, dtype=np.float32)) / np.float32(64))).astype(np.float32)
    ang = (np.arange(S, dtype=np.float32)[None, :] * inv[:, None]).astype(np.float32)
    cos, sin = np.cos(ang).astype(np.float32), np.sin(ang).astype(np.float32)
    cosT = np.tile(cos, (4, 1))
    sinT = np.concatenate([-sin, sin, -sin, sin], 0)
    return cst, np.ascontiguousarray(cosT), np.ascontiguousarray(sinT)


def host_prep(inputs, plan):
    f = lambda a: np.ascontiguousarray(np.asarray(a, dtype=np.float32))
    w_in = f(inputs["w_in"]).copy()
    perm = []
    for c in range(4):
        perm += list(range(c * 64, c * 64 + 64)) + list(range((4 + c) * 64, (4 + c) * 64 + 64))
    w_in[:, :, 0:512] = w_in[:, :, perm]
    w_br = f(inputs["w_branch"]).copy()
    w_br[:, 0] = w_br[:, 0][:, perm, :]
    vecs = np.zeros((128, NV), np.float32)
    pc = lambda v: np.asarray(v, np.float32).reshape(-1, 128).T
    for l in range(NL):
        b = l * LV
        vecs[:, b:b + 8] = pc(inputs["norm_mix"][l])
        vecs[:, b + 8:b + 16] = pc(inputs["norm_ffn"][l])
        vecs[:, b + 16:b + 40] = pc(inputs["b_gate"][l])
        cw = np.asarray(inputs["conv_w"][l], np.float32)
        for tap in range(3):
            vecs[:, b + 40 + tap * 44:b + 40 + (tap + 1) * 44] = pc(cw[tap])
        vecs[:, b + 172:b + 216] = pc(inputs["conv_b"][l])
        vecs[:, b + 216:b + 224] = np.asarray(inputs["sinks"][l], np.float32)[None, :]
    vecs[:, NL * LV:NL * LV + 8] = pc(inputs["norm_final"])
    cst, cosT, sinT = host_consts()
    srcs = {"w_in": w_in, "w_br": w_br, "w_out": f(inputs["w_out"]), "w_up": f(inputs["w_up"]), "w_dn": f(inputs["w_down"])}
    wpack = np.zeros((128, NL * 159744), np.float32)
    for (name, l, sub, r0, r1, c0, c1, off) in plan:
        A = srcs[name][l] if sub is None else srcs[name][l][sub]
        A = A[r0:r1, c0:c1]
        kc, n = (r1 - r0) // 128, c1 - c0
        wpack[:, off:off + kc * n] = A.reshape(kc, 128, n).transpose(1, 0, 2).reshape(128, kc * n)
    shared = {"wpack": wpack, "vecs": vecs, "cosT": cosT, "sinT": sinT, "cst": cst}
    x = np.asarray(inputs["x"], np.float32)
    in_maps = []
    for b in range(8):
        m = dict(shared)
        m["xT"] = np.ascontiguousarray(x[b].T)
        in_maps.append(m)
    return in_maps


_NC = {}


def kernel(**inputs):
    if "nc" not in _NC:
        _NC["nc"] = build()
        _NC["plan"] = build.last_plan
    nc = _NC["nc"]
    in_maps = host_prep(inputs, _NC["plan"])
    res = run_bass_kernel_spmd(nc, in_maps, core_ids=list(range(8)))
    out = np.stack([np.ascontiguousarray(res.results[b]["outT"].T) for b in range(8)], 0)
    return out.astype(np.float32)
```
